# Optimizing a Trainium2 kernel written in Bass

```python
import math
import jax
import jax.numpy as jnp
from jax import lax
import numpy as np

D_MODEL = 2048
BATCH = 4
SEQ = 4096
DEPTH = 4

CHUNK = 64
N_A = DEPTH // 2
N_B = DEPTH - N_A
ALPHA = (2.0 * DEPTH) ** 0.25
BETA = (8.0 * DEPTH) ** -0.25
LN_EPS = 1e-5
RET_HEADS = 8
RET_DK = D_MODEL // RET_HEADS
RET_DV = 2 * D_MODEL // RET_HEADS
RET_QK = RET_HEADS * RET_DK
RET_V = RET_HEADS * RET_DV
RET_IN = 2 * RET_QK + 2 * RET_V
ROPE_BASE = 10000.0
SB_HEADS = 16
SB_DH = D_MODEL // SB_HEADS
Q_BLOCK = 128
N_KEYS = 128
N_EXPERTS = N_KEYS * N_KEYS
PEER_HEADS = 8
PEER_DQ = 256
PEER_TOPK = 16
PEER_BLOCK = 128

kernel_name = "yoco_retention_stickbreaking_peer_deepnorm"


def layer_norm(x, g, b):
    xf = x.astype(jnp.float32)
    mu = jnp.mean(xf, axis=-1, keepdims=True)
    var = jnp.mean(jnp.square(xf - mu), axis=-1, keepdims=True)
    y = (xf - mu) * lax.rsqrt(var + LN_EPS)
    return (y * g.astype(jnp.float32) + b.astype(jnp.float32)).astype(x.dtype)


def rope(x, cos, sin):
    x1, x2 = jnp.split(x, 2, axis=-1)
    c = cos[None, :, None, :]
    s = sin[None, :, None, :]
    return jnp.concatenate([x1 * c - x2 * s, x1 * s + x2 * c], axis=-1)


def retention(x, w_in, gn_g, w_out):
    B, S, _ = x.shape
    nc = S // CHUNK
    proj = x @ w_in
    q, k, v, g = jnp.split(proj, [RET_QK, 2 * RET_QK, 2 * RET_QK + RET_V], axis=-1)
    pos = jnp.arange(S, dtype=jnp.float32)
    inv_freq = ROPE_BASE ** (-jnp.arange(0, RET_DK, 2, dtype=jnp.float32) / RET_DK)
    ang = pos[:, None] * inv_freq[None, :]
    cos, sin = jnp.cos(ang), jnp.sin(ang)
    q = rope(q.reshape(B, S, RET_HEADS, RET_DK).astype(jnp.float32), cos, sin) * (RET_DK ** -0.5)
    k = rope(k.reshape(B, S, RET_HEADS, RET_DK).astype(jnp.float32), cos, sin)
    v = v.reshape(B, S, RET_HEADS, RET_DV).astype(jnp.float32)
    qc = q.transpose(0, 2, 1, 3).reshape(B, RET_HEADS, nc, CHUNK, RET_DK)
    kc = k.transpose(0, 2, 1, 3).reshape(B, RET_HEADS, nc, CHUNK, RET_DK)
    vc = v.transpose(0, 2, 1, 3).reshape(B, RET_HEADS, nc, CHUNK, RET_DV)
    log_gamma = jnp.log(1.0 - jnp.exp2(-5.0 - jnp.arange(RET_HEADS, dtype=jnp.float32)))
    n = jnp.arange(CHUNK, dtype=jnp.float32)
    dist = jnp.abs(n[:, None] - n[None, :])
    decay_intra = jnp.exp(log_gamma[:, None, None] * dist)
    scores = jnp.einsum('bhcnd,bhcmd->bhcnm', qc, kc) * decay_intra[None, :, None]
    o_intra = jnp.einsum('bhcnm,bhcme->bhcne', scores, vc)
    q_dec = jnp.exp(log_gamma[:, None] * (n + 1.0))
    k_dec = jnp.exp(log_gamma[:, None] * (CHUNK - 1.0 - n))
    chunk_dec = jnp.exp(log_gamma * CHUNK)

    def step(state, inp):
        qi, ki, vi = inp
        o = jnp.einsum('bhnd,bhde->bhne', qi * q_dec[None, :, :, None], state)
        state = state * chunk_dec[None, :, None, None] + jnp.einsum(
            'bhmd,bhme->bhde', ki * k_dec[None, :, :, None], vi)
        return state, o

    state0 = jnp.zeros((B, RET_HEADS, RET_DK, RET_DV), jnp.float32)
    xs = (qc.transpose(2, 0, 1, 3, 4), kc.transpose(2, 0, 1, 3, 4), vc.transpose(2, 0, 1, 3, 4))
    _, o_inter = lax.scan(step, state0, xs)
    o = o_intra + o_inter.transpose(1, 2, 0, 3, 4)
    o = o.reshape(B, RET_HEADS, S, RET_DV).transpose(0, 2, 1, 3)
    mu = jnp.mean(o, axis=-1, keepdims=True)
    var = jnp.mean(jnp.square(o - mu), axis=-1, keepdims=True)
    o = ((o - mu) * lax.rsqrt(var + LN_EPS)).reshape(B, S, RET_V) * gn_g.astype(jnp.float32)
    out = jax.nn.silu(g.astype(jnp.float32)) * o
    return out.astype(x.dtype) @ w_out


def shared_kv(x, kv_w):
    B, S, _ = x.shape
    kv = x @ kv_w
    k, v = jnp.split(kv, 2, axis=-1)
    k = k.reshape(B, S, SB_HEADS, SB_DH).transpose(0, 2, 1, 3)
    v = v.reshape(B, S, SB_HEADS, SB_DH).transpose(0, 2, 1, 3)
    return k, v


def stick_breaking(x, wq, w_out, k, v):
    B, S, _ = x.shape
    q = (x @ wq).reshape(B, S, SB_HEADS, SB_DH).transpose(0, 2, 1, 3)
    scale = SB_DH ** -0.5
    outs = []
    for blk in range(S // Q_BLOCK):
        t0 = blk * Q_BLOCK
        L = t0 + Q_BLOCK
        qb = q[:, :, t0:L]
        kb = k[:, :, :L]
        vb = v[:, :, :L]
        z = jnp.einsum('bhtd,bhsd->bhts', qb, kb).astype(jnp.float32) * scale
        t_idx = t0 + jnp.arange(Q_BLOCK)[:, None]
        s_idx = jnp.arange(L)[None, :]
        causal = s_idx < t_idx
        log_beta = jax.nn.log_sigmoid(z)
        log_1m = jnp.where(causal, jax.nn.log_sigmoid(-z), 0.0)
        tail = lax.cumsum(log_1m, axis=3, reverse=True) - log_1m
        w = jnp.where(causal, jnp.exp(log_beta + tail), 0.0)
        outs.append(jnp.einsum('bhts,bhsd->bhtd', w.astype(vb.dtype), vb))
    o = jnp.concatenate(outs, axis=2)
    o = o.transpose(0, 2, 1, 3).reshape(B, S, D_MODEL)
    return o @ w_out


def peer(x, wq, sub_keys, u_tab, v_tab):
    B, S, D = x.shape
    T = B * S
    xt = x.reshape(T, D)
    q = (xt @ wq).reshape(T, PEER_HEADS, 2, PEER_DQ // 2)
    s = jnp.einsum('thpd,hpkd->thpk', q, sub_keys).astype(jnp.float32)
    v_top, i_top = lax.top_k(s, PEER_TOPK)
    cand = (v_top[:, :, 0, :, None] + v_top[:, :, 1, None, :]).reshape(T, PEER_HEADS, PEER_TOPK * PEER_TOPK)
    cid = (i_top[:, :, 0, :, None] * N_KEYS + i_top[:, :, 1, None, :]).reshape(T, PEER_HEADS, PEER_TOPK * PEER_TOPK)
    sc, pos = lax.top_k(cand, PEER_TOPK)
    ids = jnp.take_along_axis(cid, pos, axis=-1)
    g = jax.nn.softmax(sc, axis=-1)
    nblk = T // PEER_BLOCK

    def block(args):
        xb, idb, gb = args
        ub = u_tab[idb]
        h = jax.nn.gelu(jnp.einsum('td,thkd->thk', xb, ub).astype(jnp.float32))
        return jnp.einsum('thk,thkd->td', (gb * h).astype(xb.dtype), v_tab[idb])

    y = lax.map(block, (xt.reshape(nblk, PEER_BLOCK, D),
                        ids.reshape(nblk, PEER_BLOCK, PEER_HEADS, PEER_TOPK),
                        g.reshape(nblk, PEER_BLOCK, PEER_HEADS, PEER_TOPK)))
    return y.reshape(B, S, D)


def setup_inputs(seed: int = 0) -> dict:
    key = jax.random.key(seed)
    ks = jax.random.split(key, 16)
    D = D_MODEL
    nrm = jax.random.normal
    x = nrm(ks[0], (BATCH, SEQ, D), jnp.float32)
    col_scale = jnp.concatenate([jnp.ones((2 * RET_QK,), jnp.float32),
                                 jnp.full((RET_V,), BETA, jnp.float32),
                                 jnp.ones((RET_V,), jnp.float32)])
    ret_w_in = nrm(ks[1], (N_A, D, RET_IN), jnp.float32) * (D ** -0.5) * col_scale
    ret_gn_g = 1.0 + 0.02 * nrm(ks[2], (N_A, RET_V), jnp.float32)
    ret_w_out = nrm(ks[3], (N_A, RET_V, D), jnp.float32) * (RET_V ** -0.5) * BETA
    kv_scale = jnp.concatenate([jnp.ones((D,), jnp.float32), jnp.full((D,), BETA, jnp.float32)])
    kv_w = nrm(ks[4], (D, 2 * D), jnp.float32) * (D ** -0.5) * kv_scale
    sb_wq = nrm(ks[5], (N_B, D, D), jnp.float32) * (D ** -0.5)
    sb_w_out = nrm(ks[6], (N_B, D, D), jnp.float32) * (D ** -0.5) * BETA
    peer_wq = nrm(ks[7], (DEPTH, D, PEER_HEADS * PEER_DQ), jnp.float32) * (D ** -0.5)
    peer_sub_keys = nrm(ks[8], (DEPTH, PEER_HEADS, 2, N_KEYS, PEER_DQ // 2), jnp.float32) * ((PEER_DQ // 2) ** -0.5)
    peer_u = nrm(ks[9], (DEPTH, N_EXPERTS, D), jnp.float32) * (D ** -0.5) * BETA
    peer_v = nrm(ks[10], (DEPTH, N_EXPERTS, D), jnp.float32) * BETA
    ln_g = 1.0 + 0.02 * nrm(ks[11], (DEPTH, 2, D), jnp.float32)
    ln_b = 0.02 * nrm(ks[12], (DEPTH, 2, D), jnp.float32)
    return {"x": x, "ret_w_in": ret_w_in, "ret_gn_g": ret_gn_g, "ret_w_out": ret_w_out,
            "kv_w": kv_w, "sb_wq": sb_wq, "sb_w_out": sb_w_out,
            "peer_wq": peer_wq, "peer_sub_keys": peer_sub_keys, "peer_u": peer_u, "peer_v": peer_v,
            "ln_g": ln_g, "ln_b": ln_b}


def reference(x, ret_w_in, ret_gn_g, ret_w_out, kv_w, sb_wq, sb_w_out,
              peer_wq, peer_sub_keys, peer_u, peer_v, ln_g, ln_b):
    k_sh = None
    v_sh = None
    for l in range(DEPTH):
        if l < N_A:
            mix = retention(x, ret_w_in[l], ret_gn_g[l], ret_w_out[l])
        else:
            mix = stick_breaking(x, sb_wq[l - N_A], sb_w_out[l - N_A], k_sh, v_sh)
        x = layer_norm(ALPHA * x + mix, ln_g[l, 0], ln_b[l, 0])
        ffn = peer(x, peer_wq[l], peer_sub_keys[l], peer_u[l], peer_v[l])
        x = layer_norm(ALPHA * x + ffn, ln_g[l, 1], ln_b[l, 1])
        if l == N_A - 1:
            k_sh, v_sh = shared_kv(x, kv_w)
    return x
```

```python
import numpy as np
import concourse.bass as bass
import concourse.mybir as mybir

F32 = mybir.dt.float32
BF16 = mybir.dt.bfloat16
ALU = mybir.AluOpType
AF = mybir.ActivationFunctionType
AX = mybir.AxisListType

ENGS = ("pe", "act", "dve", "pool", "sp")


class Op:
    __slots__ = ("eng", "fn", "reads", "writes", "dma", "deps", "idx", "eidx",
                 "need_inc", "inc_val", "slot")

    def __init__(self, eng, fn, reads, writes, dma):
        self.eng = eng
        self.fn = fn
        self.reads = reads
        self.writes = writes
        self.dma = dma
        self.deps = []
        self.need_inc = False
        self.inc_val = None
        self.slot = None


class Prog:
    def __init__(self, nc):
        self.nc = nc
        self.ops = []
        self.last_w = {}
        self.readers = {}
        self.eng_sems = None
        self.eng_cnt = {e: 0 for e in ENGS}
        self.dma_sems = {}
        self.waited = {e: {} for e in ENGS}
        self.eng_nops = {e: 0 for e in ENGS}
        self._ctx = []

    def sem(self, name):
        cm = self.nc.semaphore(name)
        h = cm.__enter__()
        self._ctx.append((cm, "sem"))
        return h

    def sb(self, name, shape, dt):
        self._uid = getattr(self, "_uid", 0) + 1
        cm = self.nc.sbuf_tensor("%s_%d" % (name, self._uid), list(shape), dt)
        h = cm.__enter__()
        self._ctx.append((cm, "sb"))
        return h

    def ps(self, name, shape, dt):
        self._uid = getattr(self, "_uid", 0) + 1
        cm = self.nc.psum_tensor("%s_%d" % (name, self._uid), list(shape), dt)
        h = cm.__enter__()
        self._ctx.append((cm, "ps"))
        return h

    def close(self):
        for cm, kind in reversed(self._ctx):
            cm.__exit__(None, None, None)
        self._ctx = []

    def mark(self):
        return len(self._ctx)

    def release(self, mark):
        self.flush()
        keep = []
        tail = self._ctx[mark:]
        self._ctx = self._ctx[:mark]
        for cm, kind in reversed(tail):
            if kind == "sem":
                keep.append((cm, kind))
            else:
                cm.__exit__(None, None, None)
        self._ctx.extend(reversed(keep))

    def init_sems(self):
        self.eng_sems = {e: self.sem("s_" + e) for e in ENGS}

    def op(self, eng, fn, reads=(), writes=(), dma=None):
        o = Op(eng, fn, tuple(reads), tuple(writes), dma)
        o.idx = len(self.ops)
        o.eidx = self.eng_nops[eng]
        self.eng_nops[eng] += 1
        deps = set()
        for k in o.reads:
            w = self.last_w.get(k)
            if w is not None:
                deps.add(w)
        for k in o.writes:
            w = self.last_w.get(k)
            if w is not None:
                deps.add(w)
            for r in self.readers.get(k, ()):
                deps.add(r)
        deps.discard(o.idx)
        for k in o.reads:
            self.readers.setdefault(k, []).append(o.idx)
        for k in o.writes:
            self.last_w[k] = o.idx
            self.readers[k] = []
        o.deps = sorted(deps)
        self.ops.append(o)
        return o

    def flush(self):
        nc = self.nc
        ops = self.ops
        if not ops:
            return
        if self.eng_sems is None:
            self.init_sems()
        edges = {}
        for o in ops:
            need = []
            for d in o.deps:
                p = ops[d]
                if p.dma is None and p.eng == o.eng and o.dma is None:
                    if o.eng == "pe":
                        continue
                    if o.eidx - p.eidx > 2:
                        continue
                need.append(d)
                if p.dma is None:
                    p.need_inc = True
            edges[o.idx] = need
        last_on = {}
        for o in ops:
            if o.dma is None:
                last_on[o.eng] = o
        for e, o in last_on.items():
            o.need_inc = True
        if not hasattr(self, "dma_pool"):
            self.dma_pool = []
        slotmap = {}
        for o in ops:
            if o.dma is not None:
                if o.dma not in slotmap:
                    k = len(slotmap)
                    if k >= len(self.dma_pool):
                        self.dma_pool.append([self.sem("dpool%d" % k), 0])
                    slotmap[o.dma] = k
                o.slot = slotmap[o.dma]
                ent = self.dma_pool[o.slot]
                ent[1] += 16
                o.inc_val = ent[1]
            elif o.need_inc:
                self.eng_cnt[o.eng] += 1
                o.inc_val = self.eng_cnt[o.eng]
        per_eng = {e: [] for e in ENGS}
        for o in ops:
            per_eng[o.eng].append(o)
        final_eng = dict(self.eng_cnt)
        final_dma = {k: v[1] for k, v in enumerate(self.dma_pool)}

        def emit_stream(ename, eobj):
            waited = self.waited[ename]
            for o in per_eng[ename]:
                for d in edges[o.idx]:
                    p = ops[d]
                    if p.dma is not None:
                        key = ("d", p.slot)
                        sem = self.dma_pool[p.slot][0]
                    else:
                        key = ("e", p.eng)
                        sem = self.eng_sems[p.eng]
                    if waited.get(key, 0) >= p.inc_val:
                        continue
                    waited[key] = p.inc_val
                    eobj.wait_ge(sem, p.inc_val)
                ins = o.fn(eobj)
                if o.dma is not None:
                    ins.then_inc(self.dma_pool[o.slot][0], 16)
                elif o.need_inc:
                    ins.then_inc(self.eng_sems[o.eng], 1)
            for e2 in ENGS:
                if final_eng[e2] > 0 and e2 != ename and waited.get(("e", e2), 0) < final_eng[e2]:
                    eobj.wait_ge(self.eng_sems[e2], final_eng[e2])
                    waited[("e", e2)] = final_eng[e2]
            for k, v in final_dma.items():
                if v > 0 and waited.get(("d", k), 0) < v:
                    eobj.wait_ge(self.dma_pool[k][0], v)
                    waited[("d", k)] = v

        with nc.Block() as block:
            @block.tensor
            def _(e):
                emit_stream("pe", e)

            @block.scalar
            def _(e):
                emit_stream("act", e)

            @block.vector
            def _(e):
                emit_stream("dve", e)

            @block.gpsimd
            def _(e):
                emit_stream("pool", e)

            @block.sync
            def _(e):
                emit_stream("sp", e)

        self.ops = []
        self.last_w = {}
        self.readers = {}
        self.eng_nops = {e: 0 for e in ENGS}


import math
import numpy as np

D = 2048
KC = 16
ALPHA = 8.0 ** 0.25
LN_EPS = 1e-5
U32 = mybir.dt.uint32


class Ctx:
    pass


def dma(P, eng, out, in_, reads, writes, slot):
    P.op(eng, lambda e: e.dma_start(out=out, in_=in_), reads=reads, writes=writes, dma=slot)


def make_ident(P, C):
    identf = P.sb("identf", [128, 128], F32)
    C.ident = P.sb("ident", [128, 128], BF16)
    P.op("pool", lambda e: e.memset(identf[:], 1.0), writes=["identf"])
    P.op("pool", lambda e: e.affine_select(out=identf[:], in_=identf[:], pattern=[[-1, 128]],
                                           compare_op=ALU.is_equal, fill=0.0, base=0, channel_multiplier=1),
         reads=["identf"], writes=["identf"])
    P.op("dve", lambda e: e.tensor_copy(out=C.ident[:], in_=identf[:]), reads=["identf"], writes=["ident"])
    C.iota = P.sb("iota_i", [128, 128], F32)
    P.op("pool", lambda e: e.iota(C.iota[:], pattern=[[1, 128]], base=0, channel_multiplier=0,
                                  allow_small_or_imprecise_dtypes=True), writes=["iota"])


def emit_tile_to_xT(P, C, src_sb, src_key, xT_d, t, pfx, bufs):
    xb, tp, xo = bufs
    P.op("act", lambda e: e.copy(out=xb[:], in_=src_sb), reads=[src_key], writes=[pfx + "xb"])
    for half in range(2):
        for c in range(8):
            cc = half * 8 + c
            P.op("pe", lambda e, cc=cc, c=c: e.transpose(out=tp[:, c * 128:(c + 1) * 128], in_=xb[:, cc * 128:(cc + 1) * 128],
                                                         identity=C.ident[:]),
                 reads=[pfx + "xb", "ident"], writes=[pfx + "tp"])
        P.op("dve", lambda e, half=half: e.tensor_copy(out=xo[:, half * 8:(half + 1) * 8, :],
                                                       in_=tp[:].rearrange("p (c t) -> p c t", c=8)),
             reads=[pfx + "tp"], writes=[pfx + "xo"])
    dma(P, "sp", xT_d[:, t * 128:(t + 1) * 128].rearrange("(c p) t -> p c t", p=128), xo[:],
        [pfx + "xo"], [("xT", t)], pfx + "xo_st")


def phase_x_to_xT(P, C, x_d, xT_d, NT):
    m = P.mark()
    xs = [P.sb("a_xs%d" % i, [128, D], F32) for i in range(2)]
    xb = P.sb("a_xb", [128, D], BF16)
    tp = P.ps("a_tp", [128, 1024], BF16)
    xo = P.sb("a_xo", [128, 16, 128], BF16)
    for t in range(NT):
        b = t % 2
        dma(P, "sp", xs[b][:], x_d[t * 128:(t + 1) * 128, :], [("xres", t)], ["a_xs%d" % b], "a_ld%d" % b)
        emit_tile_to_xT(P, C, xs[b][:], "a_xs%d" % b, xT_d, t, "a_", (xb, tp, xo))
    P.release(m)


def ln_alloc(P, C, pfx):
    L = Ctx()
    L.pfx = pfx
    L.g = P.sb(pfx + "g", [128, D], F32)
    L.b = P.sb(pfx + "b", [128, D], F32)
    L.xin = P.sb(pfx + "xin", [128, D], F32)
    L.y = P.sb(pfx + "y", [128, D], F32)
    L.st = P.sb(pfx + "st", [128, 4, 6], F32)
    L.mv = P.sb(pfx + "mv", [128, 4], F32)
    L.xb = P.sb(pfx + "xb", [128, D], BF16)
    L.tp = P.ps(pfx + "tp", [128, 1024], BF16)
    L.xo = P.sb(pfx + "xo", [128, 16, 128], BF16)
    return L


def ln_load_params(P, L, g_d, b_d):
    dma(P, "sp", L.g[:], g_d.partition_broadcast(128), [], [L.pfx + "g"], L.pfx + "gld")
    dma(P, "sp", L.b[:], b_d.partition_broadcast(128), [], [L.pfx + "b"], L.pfx + "bld")


def ln_tile(P, C, L, mix_ap, mix_keys, xres_in_d, xres_out_d, xT_d, t, mix_in_psum_parts=None):
    pfx = L.pfx
    dma(P, "sp", L.xin[:], xres_in_d[t * 128:(t + 1) * 128, :], [("xres", t)], [pfx + "xin"], pfx + "xin_ld")
    P.op("dve", lambda e: e.scalar_tensor_tensor(out=L.y[:], in0=L.xin[:], scalar=ALPHA, in1=mix_ap,
                                                 op0=ALU.mult, op1=ALU.add),
         reads=[pfx + "xin"] + list(mix_keys), writes=[pfx + "y"])
    for q in range(4):
        P.op("dve", lambda e, q=q: e.bn_stats(out=L.st[:, q, :], in_=L.y[:, q * 512:(q + 1) * 512]),
             reads=[pfx + "y"], writes=[pfx + "st"])
    P.op("dve", lambda e: e.bn_aggr(out=L.mv[:, 0:2], in_=L.st[:].rearrange("p a b -> p (a b)")), reads=[pfx + "st"], writes=[pfx + "mv"])
    P.op("dve", lambda e: e.tensor_scalar(out=L.mv[:, 2:3], in0=L.mv[:, 1:2], scalar1=LN_EPS, scalar2=None, op0=ALU.add),
         reads=[pfx + "mv"], writes=[pfx + "mv"])
    P.op("act", lambda e: e.activation(out=L.mv[:, 3:4], in_=L.mv[:, 2:3], func=AF.Sqrt), reads=[pfx + "mv"], writes=[pfx + "mv"])
    P.op("dve", lambda e: e.reciprocal(out=L.mv[:, 2:3], in_=L.mv[:, 3:4]), reads=[pfx + "mv"], writes=[pfx + "mv"])
    P.op("dve", lambda e: e.tensor_scalar(out=L.y[:], in0=L.y[:], scalar1=L.mv[:, 0:1], scalar2=L.mv[:, 2:3],
                                          op0=ALU.subtract, op1=ALU.mult),
         reads=[pfx + "y", pfx + "mv"], writes=[pfx + "y"])
    P.op("pool", lambda e: e.tensor_tensor(out=L.y[:], in0=L.y[:], in1=L.g[:], op=ALU.mult),
         reads=[pfx + "y", pfx + "g"], writes=[pfx + "y"])
    P.op("pool", lambda e: e.tensor_tensor(out=L.y[:], in0=L.y[:], in1=L.b[:], op=ALU.add),
         reads=[pfx + "y", pfx + "b"], writes=[pfx + "y"])
    dma(P, "sp", xres_out_d[t * 128:(t + 1) * 128, :], L.y[:], [pfx + "y"], [("xres", t)], pfx + "y_st")
    if xT_d is not None:
        emit_tile_to_xT(P, C, L.y[:], pfx + "y", xT_d, t, pfx, (L.xb, L.tp, L.xo))


def linear_phase(P, C, actT_d, K, S, W_d, n0, N, mode, epilogue, pfx, ncol=512, TB=512, post_block=None):
    kc = K // 128
    if kc * ncol > 8192:
        ncol = 8192 // kc
    m = P.mark()
    wf = [P.sb(pfx + "wf%d" % i, [128, kc, ncol], F32) for i in range(2)]
    wb = [P.sb(pfx + "wb%d" % i, [128, kc, ncol], BF16) for i in range(2)]
    ab = [P.sb(pfx + "ab%d" % i, [128, kc, TB], BF16) for i in range(2)]
    pb = [P.ps(pfx + "pb%d" % i, [128, 512], F32) for i in range(4)]
    ncb = N // ncol
    ntb = S // TB
    cnt = 0
    pcount = 0
    for cb in range(ncb):
        wi = cb % 2
        c0 = n0 + cb * ncol
        dma(P, "sp", wf[wi][:], W_d[:, c0:c0 + ncol].rearrange("(kc p) n -> p kc n", p=128), [], [pfx + "wf%d" % wi], pfx + "wld%d" % wi)
        P.op("pool", lambda e, wi=wi: e.tensor_copy(out=wb[wi][:], in_=wf[wi][:]), reads=[pfx + "wf%d" % wi], writes=[pfx + "wb%d" % wi])
        for tb in range(ntb):
            ai = cnt % 2
            cnt += 1
            dma(P, "act", ab[ai][:], actT_d[:, tb * TB:(tb + 1) * TB].rearrange("(kc p) t -> p kc t", p=128),
                [("xT", tt) for tt in range(tb * TB // 128, (tb + 1) * TB // 128)] if pfx != "wo_" else [pfx + "actsrc"],
                [pfx + "ab%d" % ai], pfx + "ald%d" % ai)
            if mode == "tok":
                for ti in range(TB // 128):
                    t = tb * (TB // 128) + ti
                    pi = pcount % 4
                    pcount += 1
                    for k in range(kc):
                        P.op("pe", lambda e, k=k, ai=ai, wi=wi, ti=ti, pi=pi: e.matmul(
                            pb[pi][:, 0:ncol], lhsT=ab[ai][:, k, ti * 128:(ti + 1) * 128], rhs=wb[wi][:, k, :],
                            start=(k == 0), stop=(k == kc - 1)),
                            reads=[pfx + "ab%d" % ai, pfx + "wb%d" % wi], writes=[pfx + "pb%d" % pi])
                    epilogue(t, cb, pb[pi][:, 0:ncol], pfx + "pb%d" % pi)
            else:
                for j in range(ncol // 128):
                    pi = pcount % 4
                    pcount += 1
                    for k in range(kc):
                        P.op("pe", lambda e, k=k, ai=ai, wi=wi, j=j, pi=pi: e.matmul(
                            pb[pi][:, 0:TB], lhsT=wb[wi][:, k, j * 128:(j + 1) * 128], rhs=ab[ai][:, k, :],
                            start=(k == 0), stop=(k == kc - 1)),
                            reads=[pfx + "ab%d" % ai, pfx + "wb%d" % wi], writes=[pfx + "pb%d" % pi])
                    epilogue(tb, cb, j, pb[pi][:, 0:TB], pfx + "pb%d" % pi)
                if post_block is not None:
                    post_block(tb, cb)
    P.release(m)


def peer_precast(P, C, src_d, dst_d, nelem_per_part, pfx):
    m = P.mark()
    CH = 8192
    f = [P.sb(pfx + "f%d" % i, [128, CH], F32) for i in range(2)]
    b = [P.sb(pfx + "b%d" % i, [128, CH], BF16) for i in range(2)]
    n = nelem_per_part // CH
    engs = ["dve", "pool", "act"]
    for i in range(n):
        bi = i % 2
        dma(P, "sp", f[bi][:], src_d[:, i * CH:(i + 1) * CH], [], [pfx + "f%d" % bi], pfx + "ld%d" % bi)
        eg = engs[i % 3]
        if eg == "act":
            P.op("act", lambda e, bi=bi: e.copy(out=b[bi][:], in_=f[bi][:]), reads=[pfx + "f%d" % bi], writes=[pfx + "b%d" % bi])
        else:
            P.op(eg, lambda e, bi=bi: e.tensor_copy(out=b[bi][:], in_=f[bi][:]), reads=[pfx + "f%d" % bi], writes=[pfx + "b%d" % bi])
        dma(P, "sp", dst_d[:, i * CH:(i + 1) * CH], b[bi][:], [pfx + "b%d" % bi], [pfx + "dst"], pfx + "st%d" % bi)
    P.release(m)


def peer_gbuild(P, C, qT_d, keysT_d, GT_d, NT):
    m = P.mark()
    kf = P.sb("g_kf", [128, 16, 128], F32)
    kb = P.sb("g_kb", [128, 16, 128], BF16)
    dma(P, "sp", kf[:], keysT_d, [], ["g_kf"], "g_kld")
    P.op("dve", lambda e: e.tensor_copy(out=kb[:], in_=kf[:]), reads=["g_kf"], writes=["g_kb"])
    qt = [P.sb("g_qt%d" % i, [128, 16, 128], BF16) for i in range(2)]
    S_sb = P.sb("g_S", [128, 16, 128], F32)
    wk = P.sb("g_wk", [128, 256], F32)
    V16 = P.sb("g_V16", [128, 16, 16], F32)
    I1u = P.sb("g_I1u", [128, 8, 16], U32)
    I1f = P.sb("g_I1f", [128, 8, 16], F32)
    I1b = P.sb("g_I1b", [128, 128], BF16)
    I1T = P.sb("g_I1T", [128, 128], F32)
    cand = P.sb("g_cand", [128, 8, 256], F32)
    T16 = P.sb("g_T16", [128, 8, 16], F32)
    neg = P.sb("g_neg", [128, 16], F32)
    negmx = P.sb("g_negmx", [128, 8], F32)
    e1 = P.sb("g_e1", [128, 8, 16], F32)
    e2 = P.sb("g_e2", [128, 8, 128], F32)
    junk = P.sb("g_junk", [128, 16], F32)
    Z = P.sb("g_Z", [128, 8], F32)
    rZ = P.sb("g_rZ", [128, 8], F32)
    cc = P.sb("g_cc", [128, 8, 16], F32)
    tmp = P.sb("g_tmp", [128, 16, 128], F32)
    tmp2 = P.sb("g_tmp2", [128, 16, 128], F32)
    R = P.sb("g_R", [128, 128, 128], BF16)
    RT2 = P.sb("g_RT2", [128, 128, 128], BF16)
    A = P.sb("g_A", [128, 128, 128], BF16)
    GT = P.sb("g_GT", [128, 128, 128], BF16)
    sps = P.ps("g_sps", [128, 512], F32)
    gps = [P.ps("g_gps%d" % i, [128, 512], F32) for i in range(2)]
    tps = [P.ps("g_tps%d" % i, [128, 1024], BF16) for i in range(2)]
    ips = P.ps("g_ips", [128, 128], BF16)
    for t in range(NT):
        qi = t % 2
        qk = "g_qt%d" % qi
        dma(P, "sp", qt[qi][:], qT_d[:, t * 128:(t + 1) * 128].rearrange("(c p) t -> p c t", p=128), [("pq", t)], [qk], "g_qld%d" % qi)
        for g4 in range(4):
            for c4 in range(4):
                c = g4 * 4 + c4
                P.op("pe", lambda e, c=c, c4=c4, qi=qi: e.matmul(sps[:, c4 * 128:(c4 + 1) * 128], lhsT=qt[qi][:, c, :], rhs=kb[:, c, :],
                                                                 start=True, stop=True), reads=[qk, "g_kb"], writes=["g_sps"])
            P.op("act", lambda e, g4=g4: e.copy(out=S_sb[:, g4 * 4:(g4 + 1) * 4, :], in_=sps[:].rearrange("p (c k) -> p c k", c=4)),
                 reads=["g_sps"], writes=["g_S"])
        for c in range(16):
            P.op("dve", lambda e, c=c: e.max(out=V16[:, c, 0:8], in_=S_sb[:, c, :]), reads=["g_S"], writes=["g_V16"])
            if c % 2 == 0:
                P.op("dve", lambda e, c=c: e.max_index(out=I1u[:, c // 2, 0:8], in_max=V16[:, c, 0:8], in_values=S_sb[:, c, :]),
                     reads=["g_S", "g_V16"], writes=["g_I1u"])
            P.op("dve", lambda e, c=c: e.match_replace(out=wk[:, 0:128], in_to_replace=V16[:, c, 0:8], in_values=S_sb[:, c, :], imm_value=-1e30),
                 reads=["g_S", "g_V16"], writes=["g_wk"])
            P.op("dve", lambda e, c=c: e.max(out=V16[:, c, 8:16], in_=wk[:, 0:128]), reads=["g_wk"], writes=["g_V16"])
            if c % 2 == 0:
                P.op("dve", lambda e, c=c: e.max_index(out=I1u[:, c // 2, 8:16], in_max=V16[:, c, 8:16], in_values=wk[:, 0:128]),
                     reads=["g_wk", "g_V16"], writes=["g_I1u"])
        P.op("dve", lambda e: e.tensor_copy(out=I1f[:], in_=I1u[:]), reads=["g_I1u"], writes=["g_I1f"])
        P.op("dve", lambda e: e.tensor_copy(out=I1b[:], in_=I1f[:].rearrange("p h a -> p (h a)")), reads=["g_I1f"], writes=["g_I1b"])
        P.op("pe", lambda e: e.transpose(out=ips[:], in_=I1b[:], identity=C.ident[:]), reads=["g_I1b", "ident"], writes=["g_ips"])
        P.op("act", lambda e: e.copy(out=I1T[:], in_=ips[:]), reads=["g_ips"], writes=["g_I1T"])
        for h in range(8):
            P.op("dve", lambda e, h=h: e.tensor_tensor(out=cand[:, h, :].rearrange("p (a b) -> p a b", a=16),
                                                       in0=V16[:, 2 * h, :].unsqueeze(2).to_broadcast([128, 16, 16]),
                                                       in1=V16[:, 2 * h + 1, :].unsqueeze(1).to_broadcast([128, 16, 16]), op=ALU.add),
                 reads=["g_V16"], writes=["g_cand"])
            P.op("dve", lambda e, h=h: e.max(out=T16[:, h, 0:8], in_=cand[:, h, :]), reads=["g_cand"], writes=["g_T16"])
            P.op("dve", lambda e, h=h: e.match_replace(out=wk[:], in_to_replace=T16[:, h, 0:8], in_values=cand[:, h, :], imm_value=-1e30),
                 reads=["g_cand", "g_T16"], writes=["g_wk"])
            P.op("dve", lambda e, h=h: e.max(out=T16[:, h, 8:16], in_=wk[:]), reads=["g_wk"], writes=["g_T16"])
        P.op("dve", lambda e: e.tensor_scalar(out=neg[:], in0=V16[:, :, 0], scalar1=-1.0, scalar2=None, op0=ALU.mult),
             reads=["g_V16"], writes=["g_neg"])
        P.op("dve", lambda e: e.tensor_scalar(out=negmx[:], in0=T16[:, :, 0], scalar1=-1.0, scalar2=None, op0=ALU.mult),
             reads=["g_T16"], writes=["g_negmx"])
        P.op("dve", lambda e: e.memset(Z[:], 0.0), writes=["g_Z"])
        for h in range(8):
            P.op("act", lambda e, h=h: e.activation(out=e1[:, h, :], in_=V16[:, 2 * h, :], func=AF.Exp, bias=neg[:, 2 * h:2 * h + 1], scale=1.0),
                 reads=["g_V16", "g_neg"], writes=["g_e1"])
            P.op("act", lambda e, h=h: e.activation(out=e2[:, h, :], in_=S_sb[:, 2 * h + 1, :], func=AF.Exp, bias=neg[:, 2 * h + 1:2 * h + 2], scale=1.0),
                 reads=["g_S", "g_neg"], writes=["g_e2"])
            P.op("act", lambda e, h=h: e.activation(out=junk[:], in_=T16[:, h, :], func=AF.Exp, bias=negmx[:, h:h + 1], scale=1.0,
                                                    accum_out=Z[:, h:h + 1]),
                 reads=["g_T16", "g_negmx"], writes=["g_junk", "g_Z"])
        P.op("dve", lambda e: e.reciprocal(out=rZ[:], in_=Z[:]), reads=["g_Z"], writes=["g_rZ"])
        P.op("dve", lambda e: e.tensor_tensor(out=cc[:], in0=e1[:], in1=rZ[:].unsqueeze(2).to_broadcast([128, 8, 16]), op=ALU.mult),
             reads=["g_e1", "g_rZ"], writes=["g_cc"])
        P.op("dve", lambda e: e.tensor_tensor(out=A[:], in0=C.iota[:].unsqueeze(1).to_broadcast([128, 128, 128]),
                                               in1=I1T[:].unsqueeze(2).to_broadcast([128, 128, 128]), op=ALU.is_equal),
             reads=["iota", "g_I1T"], writes=["g_A"])
        for h in range(8):
            P.op("dve", lambda e, h=h: e.tensor_tensor(out=tmp[:], in0=S_sb[:, 2 * h + 1, :].unsqueeze(1).to_broadcast([128, 16, 128]),
                                                       in1=V16[:, 2 * h, :].unsqueeze(2).to_broadcast([128, 16, 128]), op=ALU.add),
                 reads=["g_S", "g_V16"], writes=["g_tmp"])
            P.op("dve", lambda e, h=h: e.scalar_tensor_tensor(out=tmp2[:], in0=tmp[:], scalar=T16[:, h, 15:16],
                                                              in1=e2[:, h, :].unsqueeze(1).to_broadcast([128, 16, 128]),
                                                              op0=ALU.is_ge, op1=ALU.mult),
                 reads=["g_tmp", "g_T16", "g_e2"], writes=["g_tmp2"])
            P.op("pool", lambda e, h=h: e.tensor_tensor(out=R[:, h * 16:(h + 1) * 16, :], in0=tmp2[:],
                                                        in1=cc[:, h, :].unsqueeze(2).to_broadcast([128, 16, 128]), op=ALU.mult),
                 reads=["g_tmp2", "g_cc"], writes=["g_R"])
        for jg in range(16):
            ti = jg % 2
            for jj in range(8):
                j = jg * 8 + jj
                P.op("pe", lambda e, j=j, jj=jj, ti=ti: e.transpose(out=tps[ti][:, jj * 128:(jj + 1) * 128], in_=R[:, :, j], identity=C.ident[:]),
                     reads=["g_R", "ident"], writes=["g_tps%d" % ti])
            eng = "act" if jg % 2 else "dve"
            if eng == "act":
                P.op("act", lambda e, jg=jg, ti=ti: e.copy(out=RT2[:, :, jg * 8:(jg + 1) * 8], in_=tps[ti][:].rearrange("p (j t) -> p t j", j=8)),
                     reads=["g_tps%d" % ti], writes=["g_RT2"])
            else:
                P.op("dve", lambda e, jg=jg, ti=ti: e.tensor_copy(out=RT2[:, :, jg * 8:(jg + 1) * 8], in_=tps[ti][:].rearrange("p (j t) -> p t j", j=8)),
                     reads=["g_tps%d" % ti], writes=["g_RT2"])
        for tg in range(32):
            gi = tg % 2
            for t4 in range(4):
                tt = tg * 4 + t4
                P.op("pe", lambda e, tt=tt, t4=t4, gi=gi: e.matmul(gps[gi][:, t4 * 128:(t4 + 1) * 128], lhsT=A[:, tt, :], rhs=RT2[:, tt, :],
                                                                   start=True, stop=True),
                     reads=["g_A", "g_RT2"], writes=["g_gps%d" % gi])
            if tg % 2:
                P.op("act", lambda e, tg=tg, gi=gi: e.copy(out=GT[:, :, tg * 4:(tg + 1) * 4], in_=gps[gi][:].rearrange("p (t j) -> p j t", t=4)),
                     reads=["g_gps%d" % gi], writes=["g_GT"])
            else:
                P.op("dve", lambda e, tg=tg, gi=gi: e.tensor_copy(out=GT[:, :, tg * 4:(tg + 1) * 4], in_=gps[gi][:].rearrange("p (t j) -> p j t", t=4)),
                     reads=["g_gps%d" % gi], writes=["g_GT"])
        dma(P, "sp", GT_d[t], GT[:], ["g_GT"], [("GT", t)], "g_gst")
    P.release(m)


def peer_main(P, C, xT_d, UTb_d, Vb_d, GT_d, NT, out_cb, TBT=4, CG=4):
    m = P.mark()
    TB = TBT * 128
    NG = 128 // CG
    xt = P.sb("m_xt", [128, 16, TB], BF16)
    Y = [P.sb("m_Y%d" % i, [128, D], F32) for i in range(TBT)]
    ut = [P.sb("m_ut%d" % i, [128, 16, CG * 128], BF16) for i in range(2)]
    vt = [P.sb("m_vt%d" % i, [128, CG, D], BF16) for i in range(2)]
    gt = [P.sb("m_gt%d" % i, [128, CG, TB], BF16) for i in range(2)]
    hh = [P.sb("m_hh%d" % i, [128, TB], F32) for i in range(2)]
    gh = [P.sb("m_gh%d" % i, [128, CG, TB], BF16) for i in range(2)]
    hps = [P.ps("m_hps%d" % i, [128, 512], F32) for i in range(2)]
    yps = [P.ps("m_yps%d" % i, [128, 1024], F32) for i in range(2)]
    L = C.ln
    ntb = NT // TBT
    gcnt = 0
    ycnt = 0
    hcnt = 0
    for tb in range(ntb):
        dma(P, "act", xt[:], xT_d[:, tb * TB:(tb + 1) * TB].rearrange("(c p) t -> p c t", p=128),
            [("xT", tt) for tt in range(tb * TBT, (tb + 1) * TBT)], ["m_xt"], "m_xld")
        for g in range(NG):
            bi = gcnt % 2
            gcnt += 1
            j0 = g * CG
            dma(P, "sp", ut[bi][:].rearrange("p c (j i) -> p c j i", j=CG),
                UTb_d[:, j0:j0 + CG, :].rearrange("(c p) j i -> p c j i", p=128), ["UTb"], ["m_ut%d" % bi], "m_uld%d" % bi)
            dma(P, "pool", vt[bi][:], Vb_d[j0:j0 + CG, :, :].rearrange("j i d -> i j d"), ["Vb"], ["m_vt%d" % bi], "m_vld%d" % bi)
            for ti in range(TBT):
                dma(P, "sp", gt[bi][:, :, ti * 128:(ti + 1) * 128], GT_d[tb * TBT + ti][:, j0:j0 + CG, :],
                    [("GT", tb * TBT + ti)], ["m_gt%d" % bi], "m_gld%d" % bi)
            for cj in range(CG):
                hi = hcnt % 2
                hcnt += 1
                for k in range(16):
                    P.op("pe", lambda e, k=k, bi=bi, cj=cj, hi=hi: e.matmul(hps[hi][:, 0:TB], lhsT=ut[bi][:, k, cj * 128:(cj + 1) * 128], rhs=xt[:, k, :],
                                                                            start=(k == 0), stop=(k == 15)),
                         reads=["m_ut%d" % bi, "m_xt"], writes=["m_hps%d" % hi])
                P.op("act", lambda e, hi=hi: e.activation(out=hh[hi][:], in_=hps[hi][:, 0:TB], func=AF.Gelu_apprx_tanh),
                     reads=["m_hps%d" % hi], writes=["m_hh%d" % hi])
                P.op("dve", lambda e, hi=hi, bi=bi, cj=cj: e.tensor_tensor(out=gh[bi][:, cj, :], in0=hh[hi][:], in1=gt[bi][:, cj, :], op=ALU.mult),
                     reads=["m_hh%d" % hi, "m_gt%d" % bi], writes=["m_gh%d" % bi])
            for ti in range(TBT):
                for dh in range(2):
                    yi = ycnt % 2
                    ycnt += 1
                    for cj in range(CG):
                        for db in range(2):
                            P.op("pe", lambda e, cj=cj, db=db, dh=dh, ti=ti, bi=bi, yi=yi: e.matmul(
                                yps[yi][:, db * 512:(db + 1) * 512], lhsT=gh[bi][:, cj, ti * 128:(ti + 1) * 128],
                                rhs=vt[bi][:, cj, dh * 1024 + db * 512: dh * 1024 + (db + 1) * 512],
                                start=(cj == 0), stop=(cj == CG - 1)),
                                reads=["m_gh%d" % bi, "m_vt%d" % bi], writes=["m_yps%d" % yi])
                    if g == 0:
                        P.op("act", lambda e, ti=ti, dh=dh, yi=yi: e.copy(out=Y[ti][:, dh * 1024:(dh + 1) * 1024], in_=yps[yi][:]),
                             reads=["m_yps%d" % yi], writes=["m_Y%d" % ti])
                    else:
                        eng = "pool_no"
                        P.op("dve", lambda e, ti=ti, dh=dh, yi=yi: e.tensor_tensor(out=Y[ti][:, dh * 1024:(dh + 1) * 1024],
                                                                                  in0=Y[ti][:, dh * 1024:(dh + 1) * 1024], in1=yps[yi][:], op=ALU.add),
                             reads=["m_yps%d" % yi, "m_Y%d" % ti], writes=["m_Y%d" % ti])
        for ti in range(TBT):
            out_cb(tb * TBT + ti, Y[ti][:], "m_Y%d" % ti)
    P.release(m)


def ln_pass(P, C, mix_d, xres_in_d, xres_out_d, xT_d, g_d, b_d, NT, pfx):
    m = P.mark()
    L = ln_alloc(P, C, pfx)
    ln_load_params(P, L, g_d, b_d)
    mx = [P.sb(pfx + "mx%d" % i, [128, D], F32) for i in range(2)]
    for t in range(NT):
        i = t % 2
        dma(P, "act", mx[i][:], mix_d[t * 128:(t + 1) * 128, :], [("mix", t)], [pfx + "mx%d" % i], pfx + "mld%d" % i)
        ln_tile(P, C, L, mx[i][:], [pfx + "mx%d" % i], xres_in_d, xres_out_d, xT_d, t)
    P.release(m)


def make_store_epi_tok(P, dst_d, col0, ncol, dt, pfx, wkey, func=None):
    bufs = [P.sb(pfx + "eo%d" % i, [128, ncol], dt) for i in range(2)]
    cnt = [0]

    def epi(t, cb, ps, pkey):
        i = cnt[0] % 2
        cnt[0] += 1
        if func is None:
            P.op("act", lambda e: e.copy(out=bufs[i][:], in_=ps), reads=[pkey], writes=[pfx + "eo%d" % i])
        else:
            P.op("act", lambda e: e.activation(out=bufs[i][:], in_=ps, func=func), reads=[pkey], writes=[pfx + "eo%d" % i])
        dma(P, "sp", dst_d[t * 128:(t + 1) * 128, col0 + cb * ncol: col0 + (cb + 1) * ncol], bufs[i][:],
            [pfx + "eo%d" % i], [(wkey, t)], pfx + "est%d" % i)
    return epi


def make_store_epi_feat(P, dst_d, row0, pfx, wkey, TB=512):
    bufs = [P.sb(pfx + "fo%d" % i, [128, TB], BF16) for i in range(2)]
    cnt = [0]

    def epi(tb, cb, j, ps, pkey):
        i = cnt[0] % 2
        cnt[0] += 1
        P.op("act", lambda e: e.copy(out=bufs[i][:], in_=ps), reads=[pkey], writes=[pfx + "fo%d" % i])
        c = cb * 4 + j
        dma(P, "sp", dst_d[row0 + c * 128: row0 + (c + 1) * 128, tb * TB:(tb + 1) * TB], bufs[i][:],
            [pfx + "fo%d" % i], [(wkey, tt) for tt in range(tb * TB // 128, (tb + 1) * TB // 128)], pfx + "fst%d" % i)
    return epi


def retention_layer(P, C, T, l, xres_in_d, xres_out_d, NT):
    S = NT * 128
    w_in = T.ret_w_in[l]
    m = P.mark()
    tab = [P.sb("ra_tab%d" % i, [128, 4, 512], F32) for i in range(2)]
    t1 = P.sb("ra_t1", [128, 512], F32)
    t2 = P.sb("ra_t2", [128, 512], F32)
    t3 = P.sb("ra_t3", [128, 512], F32)
    t4 = P.sb("ra_t4", [128, 512], F32)
    o1 = [P.sb("ra_o1%d" % i, [128, 512], BF16) for i in range(2)]
    o2 = [P.sb("ra_o2%d" % i, [128, 512], BF16) for i in range(2)]
    kdec = P.sb("ra_kdec", [128, 8], F32)
    ktp = P.ps("ra_ktp", [128, 1024], BF16)
    kto = [P.sb("ra_kto%d" % i, [128, 4, 256], BF16) for i in range(2)]
    dma(P, "sp", kdec[:], T.kdec, [], ["ra_kdec"], "ra_kdld")
    st = {"prev": None, "cnt": 0, "tb": -1, "tabi": 0, "kcnt": 0}

    def qk_epi(tb, cb, j, ps, pkey):
        if j % 2 == 0:
            st["prev"] = (ps, pkey)
            return
        A, akey = st["prev"]
        B, bkey = ps, pkey
        isq = cb < 4
        ti = (tb % 2)
        if st["tb"] != (cb, tb):
            st["tb"] = (cb, tb)
            dma(P, "sp", tab[ti][:], T.rope[:, :, tb * 512:(tb + 1) * 512].rearrange("f p t -> p f t"), [], ["ra_tab%d" % ti], "ra_tld%d" % ti)
        cs = tab[ti][:, 2 if isq else 0, :]
        sn = tab[ti][:, 3 if isq else 1, :]
        i = st["cnt"] % 2
        st["cnt"] += 1
        P.op("dve", lambda e: e.tensor_tensor(out=t1[:], in0=A, in1=cs, op=ALU.mult), reads=[akey, "ra_tab%d" % ti], writes=["ra_t1"])
        P.op("dve", lambda e: e.tensor_tensor(out=t2[:], in0=B, in1=sn, op=ALU.mult), reads=[bkey, "ra_tab%d" % ti], writes=["ra_t2"])
        P.op("dve", lambda e: e.tensor_tensor(out=t3[:], in0=A, in1=sn, op=ALU.mult), reads=[akey, "ra_tab%d" % ti], writes=["ra_t3"])
        P.op("dve", lambda e: e.tensor_tensor(out=t4[:], in0=B, in1=cs, op=ALU.mult), reads=[bkey, "ra_tab%d" % ti], writes=["ra_t4"])
        P.op("pool", lambda e: e.tensor_tensor(out=o1[i][:], in0=t1[:], in1=t2[:], op=ALU.subtract), reads=["ra_t1", "ra_t2"], writes=["ra_o1%d" % i])
        P.op("pool", lambda e: e.tensor_tensor(out=o2[i][:], in0=t3[:], in1=t4[:], op=ALU.add), reads=["ra_t3", "ra_t4"], writes=["ra_o2%d" % i])
        c = (cb * 4 + j - 1) % 16
        dst = T.qT if isq else T.kT
        wk = "qT" if isq else "kT"
        tts = range(tb * 4, tb * 4 + 4)
        dma(P, "sp", dst[c * 128:(c + 1) * 128, tb * 512:(tb + 1) * 512], o1[i][:], ["ra_o1%d" % i], [(wk, tt) for tt in tts], "ra_s1%d" % i)
        dma(P, "sp", dst[(c + 1) * 128:(c + 2) * 128, tb * 512:(tb + 1) * 512], o2[i][:], ["ra_o2%d" % i], [(wk, tt) for tt in tts], "ra_s2%d" % i)
        if not isq:
            h = c // 2
            ki = st["kcnt"] % 2
            st["kcnt"] += 1
            for half, ob in enumerate((o1[i], o2[i])):
                for q in range(4):
                    P.op("pe", lambda e, ob=ob, q=q, half=half: e.transpose(out=ktp[:, (half * 4 + q) * 128:(half * 4 + q + 1) * 128],
                                                                            in_=ob[:, q * 128:(q + 1) * 128], identity=C.ident[:]),
                         reads=["ra_o1%d" % i, "ra_o2%d" % i, "ident"], writes=["ra_ktp"])
            P.op("dve", lambda e, ki=ki, h=h: e.tensor_scalar(out=kto[ki][:].rearrange("p q (a f) -> p a q f", a=2),
                                                              in0=ktp[:].rearrange("p (a q f) -> p a q f", a=2, q=4),
                                                              scalar1=kdec[:, h:h + 1], scalar2=None, op0=ALU.mult),
                 reads=["ra_ktp", "ra_kdec"], writes=["ra_kto%d" % ki])
            dma(P, "sp", T.kd[tb * 512:(tb + 1) * 512, h * 256:(h + 1) * 256].rearrange("(q p) f -> p q f", p=128), kto[ki][:],
                ["ra_kto%d" % ki], [("kd", tt) for tt in tts], "ra_ks%d" % ki)

    linear_phase(P, C, T.xT, 2048, S, w_in, 0, 4096, "feat", qk_epi, "lqk_")
    P.release(m)
    m = P.mark()
    epi_v = make_store_epi_tok(P, T.v, 0, 512, BF16, "rbv_", "v")
    linear_phase(P, C, T.xT, 2048, S, w_in, 4096, 4096, "tok", epi_v, "lv_")
    P.release(m)
    m = P.mark()
    epi_g = make_store_epi_tok(P, T.sg, 0, 512, F32, "rbg_", "sg", func=AF.Silu)
    linear_phase(P, C, T.xT, 2048, S, w_in, 8192, 4096, "tok", epi_g, "lg_")
    P.release(m)
    m = P.mark()
    qt = [P.sb("rc_qt%d" % i, [128, 16, 128], BF16) for i in range(2)]
    kt = [P.sb("rc_kt%d" % i, [128, 16, 128], BF16) for i in range(2)]
    kdt = [P.sb("rc_kd%d" % i, [128, 2048], BF16) for i in range(2)]
    vt = [P.sb("rc_v%d" % i, [128, 4096], BF16) for i in range(2)]
    sgt = [P.sb("rc_sg%d" % i, [128, 4096], F32) for i in range(2)]
    Sf = P.sb("rc_Sf", [128, 16, 512], F32)
    Sb = P.sb("rc_Sb", [128, 16, 512], BF16)
    MT = P.sb("rc_MT", [128, 8, 128], F32)
    qdec = P.sb("rc_qdec", [128, 8, 128], F32)
    gng = P.sb("rc_gng", [128, 4096], F32)
    PT = [P.sb("rc_PT%d" % i, [128, 128], BF16) for i in range(2)]
    qd = [P.sb("rc_qd%d" % i, [128, 2, 128], BF16) for i in range(2)]
    on = [P.sb("rc_on%d" % i, [128, 512], F32) for i in range(2)]
    gb = [P.sb("rc_gb%d" % i, [128, 512], BF16) for i in range(2)]
    gto = [P.sb("rc_gto%d" % i, [128, 4, 128], BF16) for i in range(2)]
    stt = P.sb("rc_st", [128, 6], F32)
    mv = P.sb("rc_mv", [128, 4], F32)
    sc_ps = P.ps("rc_scps", [128, 128], F32)
    o_ps = [P.ps("rc_ops%d" % i, [128, 512], F32) for i in range(2)]
    s_ps = [P.ps("rc_sps%d" % i, [128, 512], F32) for i in range(2)]
    g_tp = P.ps("rc_gtp", [128, 512], BF16)
    dma(P, "sp", MT[:], T.retMT, [], ["rc_MT"], "rc_mld")
    dma(P, "sp", qdec[:], T.qdec, [], ["rc_qdec"], "rc_qdld")
    dma(P, "sp", gng[:], T.ret_gn_g[l].partition_broadcast(128), [], ["rc_gng"], "rc_gld")
    P.op("pool", lambda e: e.memset(Sf[:], 0.0), writes=[("Sf", hh, dd) for hh in range(8) for dd in range(2)])
    P.op("pool", lambda e: e.memset(Sb[:], 0.0), writes=[("Sb", hh) for hh in range(8)])
    hc = 0
    for t in range(NT):
        i = t % 2
        dma(P, "sp", qt[i][:], T.qT[:, t * 128:(t + 1) * 128].rearrange("(c p) t -> p c t", p=128), [("qT", t)], ["rc_qt%d" % i], "rc_l1%d" % i)
        dma(P, "sp", kt[i][:], T.kT[:, t * 128:(t + 1) * 128].rearrange("(c p) t -> p c t", p=128), [("kT", t)], ["rc_kt%d" % i], "rc_l2%d" % i)
        dma(P, "act", kdt[i][:], T.kd[t * 128:(t + 1) * 128, :], [("kd", t)], ["rc_kd%d" % i], "rc_l3%d" % i)
        dma(P, "act", vt[i][:], T.v[t * 128:(t + 1) * 128, :], [("v", t)], ["rc_v%d" % i], "rc_l4%d" % i)
        dma(P, "act", sgt[i][:], T.sg[t * 128:(t + 1) * 128, :], [("sg", t)], ["rc_sg%d" % i], "rc_l5%d" % i)
        for h in range(8):
            b = hc % 2
            hc += 1
            for dc in range(2):
                P.op("pe", lambda e, h=h, dc=dc, i=i: e.matmul(sc_ps[:], lhsT=kt[i][:, 2 * h + dc, :], rhs=qt[i][:, 2 * h + dc, :],
                                                               start=(dc == 0), stop=(dc == 1)),
                     reads=["rc_kt%d" % i, "rc_qt%d" % i], writes=["rc_scps"])
            P.op("dve", lambda e, h=h, b=b: e.tensor_tensor(out=PT[b][:], in0=sc_ps[:], in1=MT[:, h, :], op=ALU.mult),
                 reads=["rc_scps", "rc_MT"], writes=["rc_PT%d" % b])
            P.op("pool", lambda e, h=h, b=b, i=i: e.tensor_tensor(out=qd[b][:], in0=qt[i][:, 2 * h:2 * h + 2, :],
                                                                  in1=qdec[:, h, :].unsqueeze(1).to_broadcast([128, 2, 128]), op=ALU.mult),
                 reads=["rc_qt%d" % i, "rc_qdec"], writes=["rc_qd%d" % b])
            P.op("pe", lambda e, h=h, b=b, i=i: e.matmul(o_ps[b][:], lhsT=PT[b][:], rhs=vt[i][:, h * 512:(h + 1) * 512], start=True, stop=False),
                 reads=["rc_PT%d" % b, "rc_v%d" % i], writes=["rc_ops%d" % b])
            for dc in range(2):
                P.op("pe", lambda e, h=h, b=b, dc=dc: e.matmul(o_ps[b][:], lhsT=qd[b][:, dc, :], rhs=Sb[:, 2 * h + dc, :], start=False, stop=(dc == 1)),
                     reads=["rc_qd%d" % b, ("Sb", h)], writes=["rc_ops%d" % b])
            for dc in range(2):
                P.op("pe", lambda e, h=h, dc=dc, i=i: e.matmul(s_ps[dc][:], lhsT=kdt[i][:, h * 256 + dc * 128: h * 256 + (dc + 1) * 128],
                                                               rhs=vt[i][:, h * 512:(h + 1) * 512], start=True, stop=True),
                     reads=["rc_kd%d" % i, "rc_v%d" % i], writes=["rc_sps%d" % dc])
                P.op("dve", lambda e, h=h, dc=dc: e.scalar_tensor_tensor(out=Sf[:, 2 * h + dc, :], in0=Sf[:, 2 * h + dc, :], scalar=float(T.gamma128[h]),
                                                                         in1=s_ps[dc][:], op0=ALU.mult, op1=ALU.add),
                     reads=["rc_sps%d" % dc, ("Sf", h, dc)], writes=[("Sf", h, dc)])
                P.op("act", lambda e, h=h, dc=dc: e.copy(out=Sb[:, 2 * h + dc, :], in_=Sf[:, 2 * h + dc, :]),
                     reads=[("Sf", h, dc)], writes=[("Sb", h)])
            P.op("dve", lambda e, b=b: e.bn_stats(out=stt[:], in_=o_ps[b][:]), reads=["rc_ops%d" % b], writes=["rc_st"])
            P.op("dve", lambda e: e.bn_aggr(out=mv[:, 0:2], in_=stt[:]), reads=["rc_st"], writes=["rc_mv"])
            P.op("dve", lambda e: e.tensor_scalar(out=mv[:, 2:3], in0=mv[:, 1:2], scalar1=LN_EPS, scalar2=None, op0=ALU.add),
                 reads=["rc_mv"], writes=["rc_mv"])
            P.op("act", lambda e: e.activation(out=mv[:, 3:4], in_=mv[:, 2:3], func=AF.Sqrt), reads=["rc_mv"], writes=["rc_mv"])
            P.op("dve", lambda e: e.reciprocal(out=mv[:, 2:3], in_=mv[:, 3:4]), reads=["rc_mv"], writes=["rc_mv"])
            P.op("dve", lambda e, b=b: e.tensor_scalar(out=on[b][:], in0=o_ps[b][:], scalar1=mv[:, 0:1], scalar2=mv[:, 2:3],
                                                       op0=ALU.subtract, op1=ALU.mult),
                 reads=["rc_ops%d" % b, "rc_mv"], writes=["rc_on%d" % b])
            P.op("pool", lambda e, b=b, h=h: e.tensor_tensor(out=on[b][:], in0=on[b][:], in1=gng[:, h * 512:(h + 1) * 512], op=ALU.mult),
                 reads=["rc_on%d" % b, "rc_gng"], writes=["rc_on%d" % b])
            P.op("pool", lambda e, b=b, h=h, i=i: e.tensor_tensor(out=gb[b][:], in0=on[b][:], in1=sgt[i][:, h * 512:(h + 1) * 512], op=ALU.mult),
                 reads=["rc_on%d" % b, "rc_sg%d" % i], writes=["rc_gb%d" % b])
            for q in range(4):
                P.op("pe", lambda e, b=b, q=q: e.transpose(out=g_tp[:, q * 128:(q + 1) * 128], in_=gb[b][:, q * 128:(q + 1) * 128], identity=C.ident[:]),
                     reads=["rc_gb%d" % b, "ident"], writes=["rc_gtp"])
            P.op("act", lambda e, b=b: e.copy(out=gto[b][:], in_=g_tp[:].rearrange("p (q t) -> p q t", q=4)), reads=["rc_gtp"], writes=["rc_gto%d" % b])
            dma(P, "sp", T.gT[h * 512:(h + 1) * 512, t * 128:(t + 1) * 128].rearrange("(q p) t -> p q t", p=128), gto[b][:],
                ["rc_gto%d" % b], ["wo_actsrc"], "rc_gs%d" % b)
    P.release(m)
    m = P.mark()
    epi_o = make_store_epi_tok(P, T.mix, 0, 256, F32, "rdo_", "mix")
    linear_phase(P, C, T.gT, 4096, S, T.ret_w_out[l], 0, 2048, "tok", epi_o, "wo_", ncol=256)
    P.release(m)
    ln_pass(P, C, T.mix, xres_in_d, xres_out_d, T.xT, T.ln_g[l][0:1, :], T.ln_b[l][0:1, :], NT, "rln_")


def shared_kv_phase(P, C, T, NT):
    S = NT * 128
    m = P.mark()
    epi_k = make_store_epi_feat(P, T.KT, 0, "kvk_", "KT")
    linear_phase(P, C, T.xT, 2048, S, T.kv_w, 0, 2048, "feat", epi_k, "lkk_")
    P.release(m)
    m = P.mark()
    epi_v = make_store_epi_tok(P, T.Vs, 0, 512, BF16, "kvv_", "Vs")
    linear_phase(P, C, T.xT, 2048, S, T.kv_w, 2048, 2048, "tok", epi_v, "lkv_")
    P.release(m)


def sb_layer(P, C, T, l, xres_in_d, xres_out_d, NT):
    S = NT * 128
    lb = l - 2
    scale = 128.0 ** -0.5
    m = P.mark()
    epi_q = make_store_epi_feat(P, T.QT, 0, "sq_", "QT")
    linear_phase(P, C, T.xT, 2048, S, T.sb_wq[lb], 0, 2048, "feat", epi_q, "lsq_")
    P.release(m)
    m = P.mark()
    NQB = S // 512
    Kh = [P.sb("sa_K%d" % i, [128, S], BF16) for i in range(2)]
    Qh = [P.sb("sa_Q%d" % i, [128, S], BF16) for i in range(2)]
    Vh = [P.sb("sa_V%d" % i, [128, NT, 128], BF16) for i in range(2)]
    cm = P.sb("sa_cm", [128, 4, 512], F32)
    tri = P.sb("sa_tri", [128, 128], BF16)
    ones = P.sb("sa_ones", [128, 128], BF16)
    trif = P.sb("sa_trif", [128, 128], F32)
    carry = P.sb("sa_carry", [128, 512], F32)
    ex = P.sb("sa_ex", [128, 512], F32)
    sp = P.sb("sa_sp", [128, 512], F32)
    spm = P.sb("sa_spm", [128, 512], F32)
    shi = P.sb("sa_shi", [128, 512], BF16)
    slo = P.sb("sa_slo", [128, 512], BF16)
    u = P.sb("sa_u", [128, 512], F32)
    lw = P.sb("sa_lw", [128, 512], F32)
    w = [P.sb("sa_w%d" % i, [128, 512], BF16) for i in range(2)]
    oo = [P.sb("sa_oo%d" % i, [128, 512], BF16) for i in range(2)]
    z_ps = [P.ps("sa_zps%d" % i, [128, 512], F32) for i in range(2)]
    t_ps = P.ps("sa_tps", [128, 512], F32)
    c_ps = P.ps("sa_cps", [128, 512], F32)
    o_ps = [P.ps("sa_ops%d" % i, [128, 512], F32) for i in range(2)]
    dma(P, "sp", cm[:], T.cmask, [], ["sa_cm"], "sa_cmld")
    P.op("pool", lambda e: e.memset(trif[:], 1.0), writes=["sa_trif"])
    P.op("pool", lambda e: e.affine_select(out=trif[:], in_=trif[:], pattern=[[-1, 128]], compare_op=ALU.is_gt, fill=0.0, base=0, channel_multiplier=1),
         reads=["sa_trif"], writes=["sa_trif"])
    P.op("dve", lambda e: e.tensor_copy(out=tri[:], in_=trif[:]), reads=["sa_trif"], writes=["sa_tri"])
    P.op("pool", lambda e: e.memset(ones[:], 1.0), writes=["sa_ones"])
    zc = 0
    wc = 0
    oc = 0
    for h in range(16):
        hi = h % 2
        dma(P, "sp", Kh[hi][:], T.KT[h * 128:(h + 1) * 128, :], [("KT", tt) for tt in range(NT)], ["sa_K%d" % hi], "sa_kld%d" % hi)
        dma(P, "sp", Qh[hi][:], T.QT[h * 128:(h + 1) * 128, :], [("QT", tt) for tt in range(NT)], ["sa_Q%d" % hi], "sa_qld%d" % hi)
        dma(P, "act", Vh[hi][:], T.Vs[:, h * 128:(h + 1) * 128].rearrange("(a p) d -> p a d", p=128), [("Vs", tt) for tt in range(NT)], ["sa_V%d" % hi], "sa_vld%d" % hi)
        for qb in range(NQB):
            oi = oc % 2
            oc += 1
            P.op("pool", lambda e: e.memset(carry[:], 0.0), writes=["sa_carry"])
            na = (qb + 1) * 4
            for ai, a in enumerate(range(na - 1, -1, -1)):
                zi = zc % 2
                zc += 1
                diag = a >= qb * 4
                P.op("pe", lambda e, a=a, qb=qb, hi=hi, zi=zi: e.matmul(z_ps[zi][:], lhsT=Kh[hi][:, a * 128:(a + 1) * 128], rhs=Qh[hi][:, qb * 512:(qb + 1) * 512],
                                                                       start=True, stop=True),
                     reads=["sa_K%d" % hi, "sa_Q%d" % hi], writes=["sa_zps%d" % zi])
                P.op("act", lambda e, zi=zi: e.activation(out=ex[:], in_=z_ps[zi][:], func=AF.Exp, scale=scale), reads=["sa_zps%d" % zi], writes=["sa_ex"])
                P.op("act", lambda e: e.activation(out=sp[:], in_=ex[:], func=AF.Ln, bias=1.0, scale=1.0), reads=["sa_ex"], writes=["sa_sp"])
                if diag:
                    P.op("pool", lambda e, a=a, qb=qb: e.tensor_tensor(out=spm[:], in0=sp[:], in1=cm[:, a - qb * 4, :], op=ALU.mult),
                         reads=["sa_sp", "sa_cm"], writes=["sa_spm"])
                    src, skey = spm, "sa_spm"
                else:
                    src, skey = sp, "sa_sp"
                P.op("act", lambda e, src=src: e.copy(out=shi[:], in_=src[:]), reads=[skey], writes=["sa_shi"])
                P.op("dve", lambda e, src=src: e.tensor_tensor(out=slo[:], in0=src[:], in1=shi[:], op=ALU.subtract), reads=[skey, "sa_shi"], writes=["sa_slo"])
                P.op("pe", lambda e: e.matmul(t_ps[:], lhsT=tri[:], rhs=shi[:], start=True, stop=False), reads=["sa_tri", "sa_shi"], writes=["sa_tps"])
                P.op("pe", lambda e: e.matmul(t_ps[:], lhsT=tri[:], rhs=slo[:], start=False, stop=True), reads=["sa_tri", "sa_slo"], writes=["sa_tps"])
                P.op("pe", lambda e: e.matmul(c_ps[:], lhsT=ones[:], rhs=shi[:], start=True, stop=False), reads=["sa_ones", "sa_shi"], writes=["sa_cps"])
                P.op("pe", lambda e: e.matmul(c_ps[:], lhsT=ones[:], rhs=slo[:], start=False, stop=True), reads=["sa_ones", "sa_slo"], writes=["sa_cps"])
                P.op("dve", lambda e: e.tensor_tensor(out=u[:], in0=t_ps[:], in1=carry[:], op=ALU.add), reads=["sa_tps", "sa_carry"], writes=["sa_u"])
                P.op("pool", lambda e: e.tensor_tensor(out=u[:], in0=u[:], in1=sp[:], op=ALU.add), reads=["sa_u", "sa_sp"], writes=["sa_u"])
                P.op("dve", lambda e, zi=zi: e.scalar_tensor_tensor(out=lw[:], in0=z_ps[zi][:], scalar=scale, in1=u[:], op0=ALU.mult, op1=ALU.subtract),
                     reads=["sa_zps%d" % zi, "sa_u"], writes=["sa_lw"])
                P.op("dve", lambda e: e.tensor_tensor(out=carry[:], in0=carry[:], in1=c_ps[:], op=ALU.add), reads=["sa_carry", "sa_cps"], writes=["sa_carry"])
                wi = wc % 2
                wc += 1
                if diag:
                    P.op("act", lambda e: e.activation(out=ex[:], in_=lw[:], func=AF.Exp), reads=["sa_lw"], writes=["sa_ex"])
                    P.op("pool", lambda e, wi=wi, a=a, qb=qb: e.tensor_tensor(out=w[wi][:], in0=ex[:], in1=cm[:, a - qb * 4, :], op=ALU.mult),
                         reads=["sa_ex", "sa_cm"], writes=["sa_w%d" % wi])
                else:
                    P.op("act", lambda e, wi=wi: e.activation(out=w[wi][:], in_=lw[:], func=AF.Exp), reads=["sa_lw"], writes=["sa_w%d" % wi])
                P.op("pe", lambda e, wi=wi, a=a, hi=hi, oi=oi, ai=ai, na=na: e.matmul(o_ps[oi][:], lhsT=Vh[hi][:, a, :], rhs=w[wi][:], start=(ai == 0), stop=(ai == na - 1)),
                     reads=["sa_V%d" % hi, "sa_w%d" % wi], writes=["sa_ops%d" % oi])
            P.op("act", lambda e, oi=oi: e.copy(out=oo[oi][:], in_=o_ps[oi][:]), reads=["sa_ops%d" % oi], writes=["sa_oo%d" % oi])
            dma(P, "sp", T.oT[h * 128:(h + 1) * 128, qb * 512:(qb + 1) * 512], oo[oi][:], ["sa_oo%d" % oi], ["wo_actsrc"], "sa_ost%d" % oi)
    P.release(m)
    m = P.mark()
    epi_o = make_store_epi_tok(P, T.mix, 0, 512, F32, "sdo_", "mix")
    linear_phase(P, C, T.oT, 2048, S, T.sb_w_out[lb], 0, 2048, "tok", epi_o, "wo_")
    P.release(m)
    ln_pass(P, C, T.mix, xres_in_d, xres_out_d, T.xT, T.ln_g[l][0:1, :], T.ln_b[l][0:1, :], NT, "sln_")


from concourse.bass_utils import run_bass_kernel_spmd

N_CORES = 4
SEQ = 4096
RET_HEADS = 8


def _consts(S):
    pos = np.arange(S, dtype=np.float32)
    inv_freq = (10000.0 ** (-np.arange(0, 256, 2, dtype=np.float32) / 256)).astype(np.float32)
    ang = (pos[None, :] * inv_freq[:, None]).astype(np.float32)
    cos = np.cos(ang.astype(np.float64)).astype(np.float32)
    sin = np.sin(ang.astype(np.float64)).astype(np.float32)
    rope = np.stack([cos, sin, cos / 16.0, sin / 16.0]).astype(np.float32)
    lg = np.log(1.0 - np.exp2(-5.0 - np.arange(8, dtype=np.float64)))
    n = np.arange(128)
    ch = n // 64
    MT = np.zeros((128, 8, 128), np.float32)
    for h in range(8):
        dist = np.abs(n[:, None] - n[None, :]).astype(np.float64)
        Mh = np.where(ch[:, None] == ch[None, :], np.exp(lg[h] * dist),
                      np.where(ch[None, :] < ch[:, None], np.exp(lg[h] * (n[:, None] - n[None, :])), 0.0))
        MT[:, h, :] = Mh.T
    qdec = np.zeros((128, 8, 128), np.float32)
    kdec = np.zeros((128, 8), np.float32)
    for h in range(8):
        qdec[:, h, :] = np.exp(lg[h] * (n + 1.0))[None, :]
        kdec[:, h] = np.exp(lg[h] * (127.0 - n))
    gamma128 = [float(np.exp(lg[h] * 128.0)) for h in range(8)]
    s = np.arange(128)[:, None]
    t = np.arange(512)[None, :]
    cmask = np.stack([((k * 128 + s) < t).astype(np.float32) for k in range(4)], axis=1)
    return dict(rope=rope, retMT=MT, qdec=qdec, kdec=kdec, cmask=np.ascontiguousarray(cmask)), gamma128


def build_nc(S=SEQ, depth=4, n_a=2):
    NT = S // 128
    nc = bass.Bass("TRN2", target_bir_lowering=False)
    T = Ctx()

    def din(name, shape):
        return nc.dram_tensor(name, list(shape), F32, kind="ExternalInput").ap()

    def dsc(name, shape, dt):
        return nc.dram_tensor(name, list(shape), dt, kind="Internal").ap()

    x = din("x", [S, 2048])
    T.ret_w_in = din("ret_w_in", [2, 2048, 12288])
    T.ret_gn_g = din("ret_gn_g", [2, 4096])
    T.ret_gn_g = [T.ret_gn_g[i:i + 1, :] for i in range(2)]
    T.ret_w_out = din("ret_w_out", [2, 4096, 2048])
    T.kv_w = din("kv_w", [2048, 4096])
    T.sb_wq = din("sb_wq", [2, 2048, 2048])
    T.sb_w_out = din("sb_w_out", [2, 2048, 2048])
    T.peer_wq = din("peer_wq", [4, 2048, 2048])
    T.keysT = din("keysT", [4, 128, 16, 128])
    T.UTp = din("UTp", [4, 2048, 128, 128])
    T.Vp = din("Vp", [4, 128, 128, 2048])
    T.ln_g = din("ln_g", [4, 2, 2048])
    T.ln_b = din("ln_b", [4, 2, 2048])
    T.rope = din("rope", [4, 128, S])
    T.retMT = din("retMT", [128, 8, 128])
    T.qdec = din("qdec", [128, 8, 128])
    T.kdec = din("kdec", [128, 8])
    T.cmask = din("cmask", [128, 4, 512])
    out = nc.dram_tensor("out", [S, 2048], F32, kind="ExternalOutput").ap()
    _, T.gamma128 = _consts(128)
    xres = dsc("xres", [S, 2048], F32)
    T.xT = dsc("xT", [2048, S], BF16)
    T.qT = dsc("qT", [2048, S], BF16)
    T.kT = dsc("kT", [2048, S], BF16)
    T.kd = dsc("kd", [S, 2048], BF16)
    T.v = dsc("v", [S, 4096], BF16)
    T.sg = dsc("sg", [S, 4096], F32)
    T.gT = dsc("gT", [4096, S], BF16)
    T.mix = dsc("mix", [S, 2048], F32)
    T.KT = dsc("KT", [2048, S], BF16)
    T.Vs = dsc("Vs", [S, 2048], BF16)
    T.QT = dsc("QT", [2048, S], BF16)
    T.oT = dsc("oT", [2048, S], BF16)
    T.pq = dsc("pq", [2048, S], BF16)
    T.UTb = dsc("UTb", [2048, 128, 128], BF16)
    T.Vb = dsc("Vb", [128, 128, 2048], BF16)
    T.GTd = dsc("GTd", [NT, 128, 128, 128], BF16)
    P = Prog(nc)
    C = Ctx()
    make_ident(P, C)
    phase_x_to_xT(P, C, x, T.xT, NT)
    cur = x
    for l in range(depth):
        if l < n_a:
            retention_layer(P, C, T, l, cur, xres, NT)
        else:
            sb_layer(P, C, T, l, cur, xres, NT)
        cur = xres
        m = P.mark()
        epi_pq = make_store_epi_feat(P, T.pq, 0, "pq_", "pq")
        linear_phase(P, C, T.xT, 2048, S, T.peer_wq[l], 0, 2048, "feat", epi_pq, "lpq_")
        P.release(m)
        peer_precast(P, C, T.UTp[l].rearrange("(p a) j i -> p (a j i)", p=128), T.UTb.rearrange("(p a) j i -> p (a j i)", p=128), 16 * 128 * 128, "pcu_")
        peer_precast(P, C, T.Vp[l].rearrange("j i d -> j (i d)"), T.Vb.rearrange("j i d -> j (i d)"), 128 * 2048, "pcv_")
        peer_gbuild(P, C, T.pq, T.keysT[l], T.GTd, NT)
        m = P.mark()
        C.ln = ln_alloc(P, C, "pln_")
        ln_load_params(P, C.ln, T.ln_g[l][1:2, :], T.ln_b[l][1:2, :])
        last = (l == depth - 1)
        dst = out if last else xres

        def out_cb(t, yap, ykey, dst=dst, last=last):
            ln_tile(P, C, C.ln, yap, [ykey], xres, dst, None if last else T.xT, t)
        peer_main(P, C, T.xT, T.UTb, T.Vb, T.GTd, NT, out_cb)
        P.release(m)
        if l == n_a - 1 and depth > n_a:
            shared_kv_phase(P, C, T, NT)
    P.flush()
    P.close()
    return nc


_NC_CACHE = {}


def kernel(x, ret_w_in, ret_gn_g, ret_w_out, kv_w, sb_wq, sb_w_out, peer_wq, peer_sub_keys, peer_u, peer_v, ln_g, ln_b):
    f = lambda a: np.ascontiguousarray(np.asarray(a, dtype=np.float32))
    x = f(x)
    B, S, _ = x.shape
    consts, _ = _consts(S)
    sk = f(peer_sub_keys)
    keysT = np.ascontiguousarray(sk.reshape(4, 16, 128, 128).transpose(0, 3, 1, 2))
    pu = f(peer_u).reshape(4, 128, 128, 2048)
    UTp = np.ascontiguousarray(pu.transpose(0, 3, 2, 1))
    del pu
    pv = f(peer_v).reshape(4, 128, 128, 2048)
    Vp = np.ascontiguousarray(pv.transpose(0, 2, 1, 3))
    del pv
    shared = dict(ret_w_in=f(ret_w_in), ret_gn_g=f(ret_gn_g), ret_w_out=f(ret_w_out), kv_w=f(kv_w), sb_wq=f(sb_wq),
                  sb_w_out=f(sb_w_out), peer_wq=f(peer_wq), keysT=keysT, UTp=UTp, Vp=Vp, ln_g=f(ln_g), ln_b=f(ln_b), **consts)
    if S not in _NC_CACHE:
        _NC_CACHE[S] = build_nc(S)
    nc = _NC_CACHE[S]
    in_maps = [dict(shared, x=np.ascontiguousarray(x[b])) for b in range(B)]
    res = run_bass_kernel_spmd(nc, in_maps, core_ids=list(range(B)))
    return np.stack([np.asarray(r["out"], dtype=np.float32) for r in res.results], axis=0)
```

```python
import numpy as np
import concourse.bass as bass
import concourse.mybir as mybir

F32 = mybir.dt.float32
BF16 = mybir.dt.bfloat16
ALU = mybir.AluOpType
AF = mybir.ActivationFunctionType
AX = mybir.AxisListType

ENGS = ("pe", "act", "dve", "pool", "sp")


class Op:
    __slots__ = ("eng", "fn", "reads", "writes", "dma", "deps", "idx", "eidx",
                 "need_inc", "inc_val", "slot")

    def __init__(self, eng, fn, reads, writes, dma):
        self.eng = eng
        self.fn = fn
        self.reads = reads
        self.writes = writes
        self.dma = dma
        self.deps = []
        self.need_inc = False
        self.inc_val = None
        self.slot = None


class Prog:
    def __init__(self, nc):
        self.nc = nc
        self.ops = []
        self.last_w = {}
        self.readers = {}
        self.eng_sems = None
        self.eng_cnt = {e: 0 for e in ENGS}
        self.dma_sems = {}
        self.waited = {e: {} for e in ENGS}
        self.eng_nops = {e: 0 for e in ENGS}
        self._ctx = []

    def sem(self, name):
        cm = self.nc.semaphore(name)
        h = cm.__enter__()
        self._ctx.append((cm, "sem"))
        return h

    def sb(self, name, shape, dt):
        self._uid = getattr(self, "_uid", 0) + 1
        cm = self.nc.sbuf_tensor("%s_%d" % (name, self._uid), list(shape), dt)
        h = cm.__enter__()
        self._ctx.append((cm, "sb"))
        return h

    def ps(self, name, shape, dt):
        self._uid = getattr(self, "_uid", 0) + 1
        cm = self.nc.psum_tensor("%s_%d" % (name, self._uid), list(shape), dt)
        h = cm.__enter__()
        self._ctx.append((cm, "ps"))
        return h

    def close(self):
        for cm, kind in reversed(self._ctx):
            cm.__exit__(None, None, None)
        self._ctx = []

    def mark(self):
        return len(self._ctx)

    def release(self, mark):
        self.flush()
        keep = []
        tail = self._ctx[mark:]
        self._ctx = self._ctx[:mark]
        for cm, kind in reversed(tail):
            if kind == "sem":
                keep.append((cm, kind))
            else:
                cm.__exit__(None, None, None)
        self._ctx.extend(reversed(keep))

    def init_sems(self):
        self.eng_sems = {e: self.sem("s_" + e) for e in ENGS}

    def op(self, eng, fn, reads=(), writes=(), dma=None):
        o = Op(eng, fn, tuple(reads), tuple(writes), dma)
        o.idx = len(self.ops)
        o.eidx = self.eng_nops[eng]
        self.eng_nops[eng] += 1
        deps = set()
        for k in o.reads:
            w = self.last_w.get(k)
            if w is not None:
                deps.add(w)
        for k in o.writes:
            w = self.last_w.get(k)
            if w is not None:
                deps.add(w)
            for r in self.readers.get(k, ()):
                deps.add(r)
        deps.discard(o.idx)
        for k in o.reads:
            self.readers.setdefault(k, []).append(o.idx)
        for k in o.writes:
            self.last_w[k] = o.idx
            self.readers[k] = []
        o.deps = sorted(deps)
        self.ops.append(o)
        return o

    def flush(self):
        nc = self.nc
        ops = self.ops
        if not ops:
            return
        if self.eng_sems is None:
            self.init_sems()
        edges = {}
        for o in ops:
            need = []
            for d in o.deps:
                p = ops[d]
                if p.dma is None and p.eng == o.eng and o.dma is None:
                    if o.eng == "pe":
                        continue
                    if o.eidx - p.eidx > 2:
                        continue
                need.append(d)
                if p.dma is None:
                    p.need_inc = True
            edges[o.idx] = need
        last_on = {}
        for o in ops:
            if o.dma is None:
                last_on[o.eng] = o
        for e, o in last_on.items():
            o.need_inc = True
        if not hasattr(self, "dma_pool"):
            self.dma_pool = []
        slotmap = {}
        for o in ops:
            if o.dma is not None:
                if o.dma not in slotmap:
                    k = len(slotmap)
                    if k >= len(self.dma_pool):
                        self.dma_pool.append([self.sem("dpool%d" % k), 0])
                    slotmap[o.dma] = k
                o.slot = slotmap[o.dma]
                ent = self.dma_pool[o.slot]
                ent[1] += 16
                o.inc_val = ent[1]
            elif o.need_inc:
                self.eng_cnt[o.eng] += 1
                o.inc_val = self.eng_cnt[o.eng]
        per_eng = {e: [] for e in ENGS}
        for o in ops:
            per_eng[o.eng].append(o)
        final_eng = dict(self.eng_cnt)
        final_dma = {k: v[1] for k, v in enumerate(self.dma_pool)}

        def emit_stream(ename, eobj):
            waited = self.waited[ename]
            for o in per_eng[ename]:
                for d in edges[o.idx]:
                    p = ops[d]
                    if p.dma is not None:
                        key = ("d", p.slot)
                        sem = self.dma_pool[p.slot][0]
                    else:
                        key = ("e", p.eng)
                        sem = self.eng_sems[p.eng]
                    if waited.get(key, 0) >= p.inc_val:
                        continue
                    waited[key] = p.inc_val
                    eobj.wait_ge(sem, p.inc_val)
                ins = o.fn(eobj)
                if o.dma is not None:
                    ins.then_inc(self.dma_pool[o.slot][0], 16)
                elif o.need_inc:
                    ins.then_inc(self.eng_sems[o.eng], 1)
            for e2 in ENGS:
                if final_eng[e2] > 0 and e2 != ename and waited.get(("e", e2), 0) < final_eng[e2]:
                    eobj.wait_ge(self.eng_sems[e2], final_eng[e2])
                    waited[("e", e2)] = final_eng[e2]
            for k, v in final_dma.items():
                if v > 0 and waited.get(("d", k), 0) < v:
                    eobj.wait_ge(self.dma_pool[k][0], v)
                    waited[("d", k)] = v

        with nc.Block() as block:
            @block.tensor
            def _(e):
                emit_stream("pe", e)

            @block.scalar
            def _(e):
                emit_stream("act", e)

            @block.vector
            def _(e):
                emit_stream("dve", e)

            @block.gpsimd
            def _(e):
                emit_stream("pool", e)

            @block.sync
            def _(e):
                emit_stream("sp", e)

        self.ops = []
        self.last_w = {}
        self.readers = {}
        self.eng_nops = {e: 0 for e in ENGS}


import math
import numpy as np

D = 2048
KC = 16
ALPHA = 8.0 ** 0.25
LN_EPS = 1e-5
U32 = mybir.dt.uint32


class Ctx:
    pass


def dma(P, eng, out, in_, reads, writes, slot):
    P.op(eng, lambda e: e.dma_start(out=out, in_=in_), reads=reads, writes=writes, dma=slot)


def make_ident(P, C):
    identf = P.sb("identf", [128, 128], F32)
    C.ident = P.sb("ident", [128, 128], BF16)
    P.op("pool", lambda e: e.memset(identf[:], 1.0), writes=["identf"])
    P.op("pool", lambda e: e.affine_select(out=identf[:], in_=identf[:], pattern=[[-1, 128]],
                                           compare_op=ALU.is_equal, fill=0.0, base=0, channel_multiplier=1),
         reads=["identf"], writes=["identf"])
    P.op("dve", lambda e: e.tensor_copy(out=C.ident[:], in_=identf[:]), reads=["identf"], writes=["ident"])
    C.iota = P.sb("iota_i", [128, 128], F32)
    P.op("pool", lambda e: e.iota(C.iota[:], pattern=[[1, 128]], base=0, channel_multiplier=0,
                                  allow_small_or_imprecise_dtypes=True), writes=["iota"])


def emit_tile_to_xT(P, C, src_sb, src_key, xT_d, t, pfx, bufs):
    xb, tp, xo = bufs
    P.op("act", lambda e: e.copy(out=xb[:], in_=src_sb), reads=[src_key], writes=[pfx + "xb"])
    for half in range(2):
        for c in range(8):
            cc = half * 8 + c
            P.op("pe", lambda e, cc=cc, c=c: e.transpose(out=tp[:, c * 128:(c + 1) * 128], in_=xb[:, cc * 128:(cc + 1) * 128],
                                                         identity=C.ident[:]),
                 reads=[pfx + "xb", "ident"], writes=[pfx + "tp"])
        P.op("dve", lambda e, half=half: e.tensor_copy(out=xo[:, half * 8:(half + 1) * 8, :],
                                                       in_=tp[:].rearrange("p (c t) -> p c t", c=8)),
             reads=[pfx + "tp"], writes=[pfx + "xo"])
    dma(P, "sp", xT_d[:, t * 128:(t + 1) * 128].rearrange("(c p) t -> p c t", p=128), xo[:],
        [pfx + "xo"], [("xT", t)], pfx + "xo_st")


def phase_x_to_xT(P, C, x_d, xT_d, NT):
    m = P.mark()
    xs = [P.sb("a_xs%d" % i, [128, D], F32) for i in range(2)]
    xb = P.sb("a_xb", [128, D], BF16)
    tp = P.ps("a_tp", [128, 1024], BF16)
    xo = P.sb("a_xo", [128, 16, 128], BF16)
    for t in range(NT):
        b = t % 2
        dma(P, "sp", xs[b][:], x_d[t * 128:(t + 1) * 128, :], [("xres", t)], ["a_xs%d" % b], "a_ld%d" % b)
        emit_tile_to_xT(P, C, xs[b][:], "a_xs%d" % b, xT_d, t, "a_", (xb, tp, xo))
    P.release(m)


def ln_alloc(P, C, pfx):
    L = Ctx()
    L.pfx = pfx
    L.g = P.sb(pfx + "g", [128, D], F32)
    L.b = P.sb(pfx + "b", [128, D], F32)
    L.xin = P.sb(pfx + "xin", [128, D], F32)
    L.y = P.sb(pfx + "y", [128, D], F32)
    L.st = P.sb(pfx + "st", [128, 4, 6], F32)
    L.mv = P.sb(pfx + "mv", [128, 4], F32)
    L.xb = P.sb(pfx + "xb", [128, D], BF16)
    L.tp = P.ps(pfx + "tp", [128, 1024], BF16)
    L.xo = P.sb(pfx + "xo", [128, 16, 128], BF16)
    return L


def ln_load_params(P, L, g_d, b_d):
    dma(P, "sp", L.g[:], g_d.partition_broadcast(128), [], [L.pfx + "g"], L.pfx + "gld")
    dma(P, "sp", L.b[:], b_d.partition_broadcast(128), [], [L.pfx + "b"], L.pfx + "bld")


def ln_tile(P, C, L, mix_ap, mix_keys, xres_in_d, xres_out_d, xT_d, t, mix_in_psum_parts=None):
    pfx = L.pfx
    dma(P, "sp", L.xin[:], xres_in_d[t * 128:(t + 1) * 128, :], [("xres", t)], [pfx + "xin"], pfx + "xin_ld")
    P.op("dve", lambda e: e.scalar_tensor_tensor(out=L.y[:], in0=L.xin[:], scalar=ALPHA, in1=mix_ap,
                                                 op0=ALU.mult, op1=ALU.add),
         reads=[pfx + "xin"] + list(mix_keys), writes=[pfx + "y"])
    for q in range(4):
        P.op("dve", lambda e, q=q: e.bn_stats(out=L.st[:, q, :], in_=L.y[:, q * 512:(q + 1) * 512]),
             reads=[pfx + "y"], writes=[pfx + "st"])
    P.op("dve", lambda e: e.bn_aggr(out=L.mv[:, 0:2], in_=L.st[:].rearrange("p a b -> p (a b)")), reads=[pfx + "st"], writes=[pfx + "mv"])
    P.op("dve", lambda e: e.tensor_scalar(out=L.mv[:, 2:3], in0=L.mv[:, 1:2], scalar1=LN_EPS, scalar2=None, op0=ALU.add),
         reads=[pfx + "mv"], writes=[pfx + "mv"])
    P.op("act", lambda e: e.activation(out=L.mv[:, 3:4], in_=L.mv[:, 2:3], func=AF.Sqrt), reads=[pfx + "mv"], writes=[pfx + "mv"])
    P.op("dve", lambda e: e.reciprocal(out=L.mv[:, 2:3], in_=L.mv[:, 3:4]), reads=[pfx + "mv"], writes=[pfx + "mv"])
    P.op("dve", lambda e: e.tensor_scalar(out=L.y[:], in0=L.y[:], scalar1=L.mv[:, 0:1], scalar2=L.mv[:, 2:3],
                                          op0=ALU.subtract, op1=ALU.mult),
         reads=[pfx + "y", pfx + "mv"], writes=[pfx + "y"])
    P.op("pool", lambda e: e.tensor_tensor(out=L.y[:], in0=L.y[:], in1=L.g[:], op=ALU.mult),
         reads=[pfx + "y", pfx + "g"], writes=[pfx + "y"])
    P.op("pool", lambda e: e.tensor_tensor(out=L.y[:], in0=L.y[:], in1=L.b[:], op=ALU.add),
         reads=[pfx + "y", pfx + "b"], writes=[pfx + "y"])
    dma(P, "sp", xres_out_d[t * 128:(t + 1) * 128, :], L.y[:], [pfx + "y"], [("xres", t)], pfx + "y_st")
    if xT_d is not None:
        emit_tile_to_xT(P, C, L.y[:], pfx + "y", xT_d, t, pfx, (L.xb, L.tp, L.xo))


def linear_phase(P, C, actT_d, K, S, W_d, n0, N, mode, epilogue, pfx, ncol=512, TB=512, post_block=None):
    kc = K // 128
    if kc * ncol > 8192:
        ncol = 8192 // kc
    NSUB = 2 if ((N // ncol) % 2 == 0 and kc <= 16) else 1
    m = P.mark()
    kh = kc // 2
    wf = [P.sb(pfx + "wf%d" % i, [128, kh, ncol], F32) for i in range(2)]
    wb = [P.sb(pfx + "wb%d" % i, [128, NSUB, kc, ncol], BF16) for i in range(2)]
    ab = [P.sb(pfx + "ab%d" % i, [128, kc, TB], BF16) for i in range(2)]
    pb = [P.ps(pfx + "pb%d" % i, [128, 512], F32) for i in range(4)]
    nsb = N // (ncol * NSUB)
    ntb = S // TB
    cnt = 0
    pcount = 0
    wcnt = 0
    for sbk in range(nsb):
        wi = sbk % 2
        for sub in range(NSUB):
            for half in range(2):
                fi = wcnt % 2
                wcnt += 1
                c0 = n0 + (sbk * NSUB + sub) * ncol
                dma(P, "sp", wf[fi][:], W_d[half * kh * 128:(half + 1) * kh * 128, c0:c0 + ncol].rearrange("(kc p) n -> p kc n", p=128),
                    [], [pfx + "wf%d" % fi], pfx + "wld%d" % fi)
                P.op("pool", lambda e, wi=wi, fi=fi, sub=sub, half=half: e.tensor_copy(out=wb[wi][:, sub, half * kh:(half + 1) * kh, :], in_=wf[fi][:]),
                     reads=[pfx + "wf%d" % fi], writes=[pfx + "wb%d" % wi])
        for tb in range(ntb):
            ai = cnt % 2
            cnt += 1
            dma(P, "act" if cnt % 2 else "sp", ab[ai][:], actT_d[:, tb * TB:(tb + 1) * TB].rearrange("(kc p) t -> p kc t", p=128),
                [pfx + "actsrc"], [pfx + "ab%d" % ai], pfx + "ald%d" % ai)
            for sub in range(NSUB):
                cb = sbk * NSUB + sub
                if mode == "tok":
                    for ti in range(TB // 128):
                        t = tb * (TB // 128) + ti
                        pi = pcount % 4
                        pcount += 1
                        for k in range(kc):
                            P.op("pe", lambda e, k=k, ai=ai, wi=wi, ti=ti, pi=pi, sub=sub: e.matmul(
                                pb[pi][:, 0:ncol], lhsT=ab[ai][:, k, ti * 128:(ti + 1) * 128], rhs=wb[wi][:, sub, k, :],
                                start=(k == 0), stop=(k == kc - 1)),
                                reads=[pfx + "ab%d" % ai, pfx + "wb%d" % wi], writes=[pfx + "pb%d" % pi])
                        epilogue(t, cb, pb[pi][:, 0:ncol], pfx + "pb%d" % pi)
                else:
                    for j in range(ncol // 128):
                        pi = pcount % 4
                        pcount += 1
                        for k in range(kc):
                            P.op("pe", lambda e, k=k, ai=ai, wi=wi, j=j, pi=pi, sub=sub: e.matmul(
                                pb[pi][:, 0:TB], lhsT=wb[wi][:, sub, k, j * 128:(j + 1) * 128], rhs=ab[ai][:, k, :],
                                start=(k == 0), stop=(k == kc - 1)),
                                reads=[pfx + "ab%d" % ai, pfx + "wb%d" % wi], writes=[pfx + "pb%d" % pi])
                        epilogue(tb, cb, j, pb[pi][:, 0:TB], pfx + "pb%d" % pi)
    P.release(m)


def peer_precast(P, C, src_d, dst_d, nelem_per_part, pfx):
    m = P.mark()
    CH = 8192
    f = [P.sb(pfx + "f%d" % i, [128, CH], F32) for i in range(2)]
    b = [P.sb(pfx + "b%d" % i, [128, CH], BF16) for i in range(2)]
    n = nelem_per_part // CH
    engs = ["dve", "pool", "act"]
    for i in range(n):
        bi = i % 2
        dma(P, "sp" if bi == 0 else "act", f[bi][:], src_d[:, i * CH:(i + 1) * CH], [], [pfx + "f%d" % bi], pfx + "ld%d" % bi)
        eg = ["dve", "act"][i % 2]
        if eg == "act":
            P.op("act", lambda e, bi=bi: e.copy(out=b[bi][:], in_=f[bi][:]), reads=[pfx + "f%d" % bi], writes=[pfx + "b%d" % bi])
        else:
            P.op(eg, lambda e, bi=bi: e.tensor_copy(out=b[bi][:], in_=f[bi][:]), reads=[pfx + "f%d" % bi], writes=[pfx + "b%d" % bi])
        dma(P, "pool" if bi == 0 else "sp", dst_d[:, i * CH:(i + 1) * CH], b[bi][:], [pfx + "b%d" % bi], [pfx + "dst%d" % i], pfx + "st%d" % bi)
    P.release(m)


def peer_gbuild(P, C, qT_d, keysT_d, GT_d, NT):
    m0 = P.mark()
    kb = P.sb("g_kb", [128, 16, 128], BF16)
    m1 = P.mark()
    kf = P.sb("g_kf", [128, 16, 128], F32)
    dma(P, "sp", kf[:], keysT_d, [], ["g_kf"], "g_kld")
    P.op("dve", lambda e: e.tensor_copy(out=kb[:], in_=kf[:]), reads=["g_kf"], writes=["g_kb"])
    P.release(m1)
    qt = [P.sb("g_qt%d" % i, [128, 16, 128], BF16) for i in range(2)]
    S_sb = P.sb("g_S", [128, 16, 128], F32)
    wk = P.sb("g_wk", [128, 256], F32)
    V16 = P.sb("g_V16", [128, 16, 16], F32)
    I1u = P.sb("g_I1u", [128, 8, 16], U32)
    I1f = P.sb("g_I1f", [128, 8, 16], F32)
    I1b = P.sb("g_I1b", [128, 128], BF16)
    I1T = [P.sb("g_I1T%d" % i, [128, 128], F32) for i in range(2)]
    cand = P.sb("g_cand", [128, 256], F32)
    T16 = P.sb("g_T16", [128, 8, 16], F32)
    neg = P.sb("g_neg", [128, 16], F32)
    negmx = P.sb("g_negmx", [128, 8], F32)
    e1 = P.sb("g_e1", [128, 8, 16], F32)
    e2 = P.sb("g_e2", [128, 8, 128], F32)
    junk = P.sb("g_junk", [128, 16], F32)
    Z = P.sb("g_Z", [128, 8], F32)
    rZ = P.sb("g_rZ", [128, 8], F32)
    cc = P.sb("g_cc", [128, 8, 16], F32)
    tmp = [P.sb("g_tmp%d" % i, [128, 8, 128], F32) for i in range(2)]
    tmp2 = [P.sb("g_tmp2%d" % i, [128, 8, 128], F32) for i in range(2)]
    R = [P.sb("g_R%d" % i, [128, 128, 128], BF16) for i in range(2)]
    RT2 = P.sb("g_RT2", [128, 128, 128], BF16)
    A = P.sb("g_A", [128, 64, 128], BF16)
    GT = P.sb("g_GT", [128, 128, 128], BF16)
    sps = P.ps("g_sps", [128, 512], F32)
    gps = [P.ps("g_gps%d" % i, [128, 512], F32) for i in range(2)]
    tps = [P.ps("g_tps%d" % i, [128, 1024], BF16) for i in range(2)]
    ips = P.ps("g_ips", [128, 128], BF16)

    def stage1(t):
        qi = t % 2
        ri = t % 2
        qk = "g_qt%d" % qi
        rk = "g_R%d" % ri
        dma(P, "sp", qt[qi][:], qT_d[:, t * 128:(t + 1) * 128].rearrange("(c p) t -> p c t", p=128), [], [qk], "g_qld%d" % qi)
        for g4 in range(4):
            for c4 in range(4):
                c = g4 * 4 + c4
                P.op("pe", lambda e, c=c, c4=c4: e.matmul(sps[:, c4 * 128:(c4 + 1) * 128], lhsT=qt[qi][:, c, :], rhs=kb[:, c, :],
                                                          start=True, stop=True), reads=[qk, "g_kb"], writes=["g_sps"])
            P.op("act", lambda e, g4=g4: e.copy(out=S_sb[:, g4 * 4:(g4 + 1) * 4, :], in_=sps[:].rearrange("p (c k) -> p c k", c=4)),
                 reads=["g_sps"], writes=["g_S"])
        for c in range(16):
            P.op("dve", lambda e, c=c: e.max(out=V16[:, c, 0:8], in_=S_sb[:, c, :]), reads=["g_S"], writes=["g_V16"])
            if c % 2 == 0:
                P.op("dve", lambda e, c=c: e.max_index(out=I1u[:, c // 2, 0:8], in_max=V16[:, c, 0:8], in_values=S_sb[:, c, :]),
                     reads=["g_S", "g_V16"], writes=["g_I1u"])
            P.op("dve", lambda e, c=c: e.match_replace(out=wk[:, 0:128], in_to_replace=V16[:, c, 0:8], in_values=S_sb[:, c, :], imm_value=-1e30),
                 reads=["g_S", "g_V16"], writes=["g_wk"])
            P.op("dve", lambda e, c=c: e.max(out=V16[:, c, 8:16], in_=wk[:, 0:128]), reads=["g_wk"], writes=["g_V16"])
            if c % 2 == 0:
                P.op("dve", lambda e, c=c: e.max_index(out=I1u[:, c // 2, 8:16], in_max=V16[:, c, 8:16], in_values=wk[:, 0:128]),
                     reads=["g_wk", "g_V16"], writes=["g_I1u"])
        P.op("dve", lambda e: e.tensor_copy(out=I1f[:], in_=I1u[:]), reads=["g_I1u"], writes=["g_I1f"])
        P.op("dve", lambda e: e.tensor_copy(out=I1b[:], in_=I1f[:].rearrange("p h a -> p (h a)")), reads=["g_I1f"], writes=["g_I1b"])
        P.op("pe", lambda e: e.transpose(out=ips[:], in_=I1b[:], identity=C.ident[:]), reads=["g_I1b", "ident"], writes=["g_ips"])
        P.op("act", lambda e: e.copy(out=I1T[t % 2][:], in_=ips[:]), reads=["g_ips"], writes=["g_I1T%d" % (t % 2)])
        for h in range(8):
            P.op("dve", lambda e, h=h: e.tensor_tensor(out=cand[:].rearrange("p (a b) -> p a b", a=16),
                                                       in0=V16[:, 2 * h, :].unsqueeze(2).to_broadcast([128, 16, 16]),
                                                       in1=V16[:, 2 * h + 1, :].unsqueeze(1).to_broadcast([128, 16, 16]), op=ALU.add),
                 reads=["g_V16"], writes=["g_cand"])
            P.op("dve", lambda e, h=h: e.max(out=T16[:, h, 0:8], in_=cand[:]), reads=["g_cand"], writes=["g_T16"])
            P.op("dve", lambda e, h=h: e.match_replace(out=wk[:], in_to_replace=T16[:, h, 0:8], in_values=cand[:], imm_value=-1e30),
                 reads=["g_cand", "g_T16"], writes=["g_wk"])
            P.op("dve", lambda e, h=h: e.max(out=T16[:, h, 8:16], in_=wk[:]), reads=["g_wk"], writes=["g_T16"])
        P.op("dve", lambda e: e.tensor_scalar(out=neg[:], in0=V16[:, :, 0], scalar1=-1.0, scalar2=None, op0=ALU.mult),
             reads=["g_V16"], writes=["g_neg"])
        P.op("dve", lambda e: e.tensor_scalar(out=negmx[:], in0=T16[:, :, 0], scalar1=-1.0, scalar2=None, op0=ALU.mult),
             reads=["g_T16"], writes=["g_negmx"])
        P.op("dve", lambda e: e.memset(Z[:], 0.0), writes=["g_Z"])
        for h in range(8):
            P.op("act", lambda e, h=h: e.activation(out=e1[:, h, :], in_=V16[:, 2 * h, :], func=AF.Exp, bias=neg[:, 2 * h:2 * h + 1], scale=1.0),
                 reads=["g_V16", "g_neg"], writes=["g_e1"])
            P.op("act", lambda e, h=h: e.activation(out=e2[:, h, :], in_=S_sb[:, 2 * h + 1, :], func=AF.Exp, bias=neg[:, 2 * h + 1:2 * h + 2], scale=1.0),
                 reads=["g_S", "g_neg"], writes=["g_e2"])
            P.op("act", lambda e, h=h: e.activation(out=junk[:], in_=T16[:, h, :], func=AF.Exp, bias=negmx[:, h:h + 1], scale=1.0,
                                                    accum_out=Z[:, h:h + 1]),
                 reads=["g_T16", "g_negmx", "g_Z"], writes=["g_junk", "g_Z"])
        P.op("dve", lambda e: e.reciprocal(out=rZ[:], in_=Z[:]), reads=["g_Z"], writes=["g_rZ"])
        P.op("dve", lambda e: e.tensor_tensor(out=cc[:], in0=e1[:], in1=rZ[:].unsqueeze(2).to_broadcast([128, 8, 16]), op=ALU.mult),
             reads=["g_e1", "g_rZ"], writes=["g_cc"])
        k = 0
        for h in range(8):
            for ah in range(2):
                bi = k % 2
                k += 1
                a0 = ah * 8
                P.op("pool", lambda e, h=h, a0=a0, bi=bi: e.tensor_tensor(out=tmp[bi][:], in0=S_sb[:, 2 * h + 1, :].unsqueeze(1).to_broadcast([128, 8, 128]),
                                                                          in1=V16[:, 2 * h, a0:a0 + 8].unsqueeze(2).to_broadcast([128, 8, 128]), op=ALU.add),
                     reads=["g_S", "g_V16"], writes=["g_tmp%d" % bi])
                P.op("dve", lambda e, h=h, bi=bi: e.scalar_tensor_tensor(out=tmp2[bi][:], in0=tmp[bi][:], scalar=T16[:, h, 15:16],
                                                                         in1=e2[:, h, :].unsqueeze(1).to_broadcast([128, 8, 128]),
                                                                         op0=ALU.is_ge, op1=ALU.mult),
                     reads=["g_tmp%d" % bi, "g_T16", "g_e2"], writes=["g_tmp2%d" % bi])
                P.op("pool", lambda e, h=h, a0=a0, bi=bi: e.tensor_tensor(out=R[ri][:, h * 16 + a0:h * 16 + a0 + 8, :], in0=tmp2[bi][:],
                                                                          in1=cc[:, h, a0:a0 + 8].unsqueeze(2).to_broadcast([128, 8, 128]), op=ALU.mult),
                     reads=["g_tmp2%d" % bi, "g_cc"], writes=[rk])

    def stage2a(t):
        ri = t % 2
        rk = "g_R%d" % ri
        for jg in range(16):
            ti = jg % 2
            for jj in range(8):
                j = jg * 8 + jj
                P.op("pe", lambda e, j=j, jj=jj, ti=ti: e.transpose(out=tps[ti][:, jj * 128:(jj + 1) * 128], in_=R[ri][:, :, j], identity=C.ident[:]),
                     reads=[rk, "ident"], writes=["g_tps%d" % ti])
            P.op("act", lambda e, jg=jg, ti=ti: e.copy(out=RT2[:, :, jg * 8:(jg + 1) * 8], in_=tps[ti][:].rearrange("p (j t) -> p t j", j=8)),
                 reads=["g_tps%d" % ti], writes=["g_RT2"])

    def stage2b(t):
        ik = "g_I1T%d" % (t % 2)
        for th in range(2):
            P.op("dve", lambda e, th=th: e.tensor_tensor(out=A[:], in0=C.iota[:].unsqueeze(1).to_broadcast([128, 64, 128]),
                                                         in1=I1T[t % 2][:, th * 64:(th + 1) * 64].unsqueeze(2).to_broadcast([128, 64, 128]), op=ALU.is_equal),
                 reads=["iota", ik], writes=["g_A"])
            for tg in range(16):
                gi = tg % 2
                for t4 in range(4):
                    tl = tg * 4 + t4
                    tt = th * 64 + tl
                    P.op("pe", lambda e, tt=tt, tl=tl, t4=t4, gi=gi: e.matmul(gps[gi][:, t4 * 128:(t4 + 1) * 128], lhsT=A[:, tl, :], rhs=RT2[:, tt, :],
                                                                              start=True, stop=True),
                         reads=["g_A", "g_RT2"], writes=["g_gps%d" % gi])
                t0 = th * 64 + tg * 4
                P.op("act", lambda e, t0=t0, gi=gi: e.copy(out=GT[:, :, t0:t0 + 4], in_=gps[gi][:].rearrange("p (t j) -> p j t", t=4)),
                     reads=["g_gps%d" % gi], writes=["g_GT"])
        dma(P, "sp", GT_d[t], GT[:], ["g_GT"], [("GT", t)], "g_gst")

    stage1(0)
    for t in range(NT):
        stage2a(t)
        if t + 1 < NT:
            stage1(t + 1)
        stage2b(t)
    P.release(m0)


def peer_main(P, C, xT_d, UTb_d, Vb_d, GT_d, NT, out_cb, TBT=4, CG=4):
    m = P.mark()
    TB = TBT * 128
    NG = 128 // CG
    xt = P.sb("m_xt", [128, 16, TB], BF16)
    Y = [P.sb("m_Y%d" % i, [128, D], F32) for i in range(TBT)]
    ut = [P.sb("m_ut%d" % i, [128, 16, CG * 128], BF16) for i in range(2)]
    vt = [P.sb("m_vt%d" % i, [128, CG, D], BF16) for i in range(2)]
    gt = [P.sb("m_gt%d" % i, [128, CG, TB], BF16) for i in range(2)]
    hh = [P.sb("m_hh%d" % i, [128, TB], F32) for i in range(2)]
    gh = [P.sb("m_gh%d" % i, [128, CG, TB], BF16) for i in range(2)]
    hps = [P.ps("m_hps%d" % i, [128, 512], F32) for i in range(2)]
    yps = [P.ps("m_yps%d" % i, [128, 1024], F32) for i in range(2)]
    L = C.ln
    ntb = NT // TBT
    gcnt = 0
    ycnt = 0
    hcnt = 0
    for tb in range(ntb):
        dma(P, "act", xt[:], xT_d[:, tb * TB:(tb + 1) * TB].rearrange("(c p) t -> p c t", p=128),
            [("xT", tt) for tt in range(tb * TBT, (tb + 1) * TBT)], ["m_xt"], "m_xld")
        for g in range(NG):
            bi = gcnt % 2
            gcnt += 1
            j0 = g * CG
            dma(P, "sp", ut[bi][:].rearrange("p c (j i) -> p c j i", j=CG),
                UTb_d[:, j0:j0 + CG, :].rearrange("(c p) j i -> p c j i", p=128), ["UTb"], ["m_ut%d" % bi], "m_uld%d" % bi)
            dma(P, "pool", vt[bi][:], Vb_d[j0:j0 + CG, :, :].rearrange("j i d -> i j d"), ["Vb"], ["m_vt%d" % bi], "m_vld%d" % bi)
            for ti in range(TBT):
                dma(P, "sp", gt[bi][:, :, ti * 128:(ti + 1) * 128], GT_d[tb * TBT + ti][:, j0:j0 + CG, :],
                    [("GT", tb * TBT + ti)], ["m_gt%d" % bi], "m_gld%d" % bi)
            for cj in range(CG):
                hi = hcnt % 2
                hcnt += 1
                for k in range(16):
                    P.op("pe", lambda e, k=k, bi=bi, cj=cj, hi=hi: e.matmul(hps[hi][:, 0:TB], lhsT=ut[bi][:, k, cj * 128:(cj + 1) * 128], rhs=xt[:, k, :],
                                                                            start=(k == 0), stop=(k == 15)),
                         reads=["m_ut%d" % bi, "m_xt"], writes=["m_hps%d" % hi])
                P.op("act", lambda e, hi=hi: e.activation(out=hh[hi][:], in_=hps[hi][:, 0:TB], func=AF.Gelu_apprx_tanh),
                     reads=["m_hps%d" % hi], writes=["m_hh%d" % hi])
                P.op("dve", lambda e, hi=hi, bi=bi, cj=cj: e.tensor_tensor(out=gh[bi][:, cj, :], in0=hh[hi][:], in1=gt[bi][:, cj, :], op=ALU.mult),
                     reads=["m_hh%d" % hi, "m_gt%d" % bi], writes=["m_gh%d" % bi])
            for ti in range(TBT):
                for dh in range(2):
                    yi = ycnt % 2
                    ycnt += 1
                    for cj in range(CG):
                        for db in range(2):
                            P.op("pe", lambda e, cj=cj, db=db, dh=dh, ti=ti, bi=bi, yi=yi: e.matmul(
                                yps[yi][:, db * 512:(db + 1) * 512], lhsT=gh[bi][:, cj, ti * 128:(ti + 1) * 128],
                                rhs=vt[bi][:, cj, dh * 1024 + db * 512: dh * 1024 + (db + 1) * 512],
                                start=(cj == 0), stop=(cj == CG - 1)),
                                reads=["m_gh%d" % bi, "m_vt%d" % bi], writes=["m_yps%d" % yi])
                    if g == 0:
                        P.op("act", lambda e, ti=ti, dh=dh, yi=yi: e.copy(out=Y[ti][:, dh * 1024:(dh + 1) * 1024], in_=yps[yi][:]),
                             reads=["m_yps%d" % yi], writes=["m_Y%d" % ti])
                    else:
                        eng = "pool_no"
                        P.op("dve", lambda e, ti=ti, dh=dh, yi=yi: e.tensor_tensor(out=Y[ti][:, dh * 1024:(dh + 1) * 1024],
                                                                                  in0=Y[ti][:, dh * 1024:(dh + 1) * 1024], in1=yps[yi][:], op=ALU.add),
                             reads=["m_yps%d" % yi, "m_Y%d" % ti], writes=["m_Y%d" % ti])
        for ti in range(TBT):
            out_cb(tb * TBT + ti, Y[ti][:], "m_Y%d" % ti)
    P.release(m)


def ln_pass(P, C, mix_d, xres_in_d, xres_out_d, xT_d, g_d, b_d, NT, pfx):
    m = P.mark()
    L = ln_alloc(P, C, pfx)
    ln_load_params(P, L, g_d, b_d)
    mx = [P.sb(pfx + "mx%d" % i, [128, D], F32) for i in range(2)]
    for t in range(NT):
        i = t % 2
        dma(P, "act", mx[i][:], mix_d[t * 128:(t + 1) * 128, :], [("mix", t)], [pfx + "mx%d" % i], pfx + "mld%d" % i)
        ln_tile(P, C, L, mx[i][:], [pfx + "mx%d" % i], xres_in_d, xres_out_d, xT_d, t)
    P.release(m)


def make_store_epi_tok(P, dst_d, col0, ncol, dt, pfx, wkey, func=None):
    bufs = [P.sb(pfx + "eo%d" % i, [128, ncol], dt) for i in range(2)]
    cnt = [0]

    def epi(t, cb, ps, pkey):
        i = cnt[0] % 2
        cnt[0] += 1
        if func is None:
            P.op("act", lambda e: e.copy(out=bufs[i][:], in_=ps), reads=[pkey], writes=[pfx + "eo%d" % i])
        else:
            P.op("act", lambda e: e.activation(out=bufs[i][:], in_=ps, func=func), reads=[pkey], writes=[pfx + "eo%d" % i])
        dma(P, "sp", dst_d[t * 128:(t + 1) * 128, col0 + cb * ncol: col0 + (cb + 1) * ncol], bufs[i][:],
            [pfx + "eo%d" % i], [(wkey, t)], pfx + "est%d" % i)
    return epi


def make_store_epi_feat(P, dst_d, row0, pfx, wkey, TB=512):
    bufs = [P.sb(pfx + "fo%d" % i, [128, TB], BF16) for i in range(2)]
    cnt = [0]

    def epi(tb, cb, j, ps, pkey):
        i = cnt[0] % 2
        cnt[0] += 1
        P.op("act", lambda e: e.copy(out=bufs[i][:], in_=ps), reads=[pkey], writes=[pfx + "fo%d" % i])
        c = cb * 4 + j
        dma(P, "sp", dst_d[row0 + c * 128: row0 + (c + 1) * 128, tb * TB:(tb + 1) * TB], bufs[i][:],
            [pfx + "fo%d" % i], [(wkey, tt) for tt in range(tb * TB // 128, (tb + 1) * TB // 128)], pfx + "fst%d" % i)
    return epi


def retention_layer(P, C, T, l, xres_in_d, xres_out_d, NT):
    S = NT * 128
    w_in = T.ret_w_in[l]
    m = P.mark()
    tab = [P.sb("ra_tab%d" % i, [128, 4, 512], F32) for i in range(2)]
    t1 = P.sb("ra_t1", [128, 512], F32)
    t2 = P.sb("ra_t2", [128, 512], F32)
    t3 = P.sb("ra_t3", [128, 512], F32)
    t4 = P.sb("ra_t4", [128, 512], F32)
    o1 = [P.sb("ra_o1%d" % i, [128, 512], BF16) for i in range(2)]
    o2 = [P.sb("ra_o2%d" % i, [128, 512], BF16) for i in range(2)]
    kdec = P.sb("ra_kdec", [128, 8], F32)
    ktp = P.ps("ra_ktp", [128, 1024], BF16)
    kto = [P.sb("ra_kto%d" % i, [128, 4, 256], BF16) for i in range(2)]
    dma(P, "sp", kdec[:], T.kdec, [], ["ra_kdec"], "ra_kdld")
    st = {"prev": None, "cnt": 0, "tb": -1, "tabi": 0, "kcnt": 0}

    def qk_epi(tb, cb, j, ps, pkey):
        if j % 2 == 0:
            st["prev"] = (ps, pkey)
            return
        A, akey = st["prev"]
        B, bkey = ps, pkey
        isq = cb < 4
        ti = (tb % 2)
        if st["tb"] != (cb, tb):
            st["tb"] = (cb, tb)
            dma(P, "sp", tab[ti][:], T.rope[:, :, tb * 512:(tb + 1) * 512].rearrange("f p t -> p f t"), [], ["ra_tab%d" % ti], "ra_tld%d" % ti)
        cs = tab[ti][:, 2 if isq else 0, :]
        sn = tab[ti][:, 3 if isq else 1, :]
        i = st["cnt"] % 2
        st["cnt"] += 1
        P.op("dve", lambda e: e.tensor_tensor(out=t1[:], in0=A, in1=cs, op=ALU.mult), reads=[akey, "ra_tab%d" % ti], writes=["ra_t1"])
        P.op("dve", lambda e: e.tensor_tensor(out=t2[:], in0=B, in1=sn, op=ALU.mult), reads=[bkey, "ra_tab%d" % ti], writes=["ra_t2"])
        P.op("dve", lambda e: e.tensor_tensor(out=t3[:], in0=A, in1=sn, op=ALU.mult), reads=[akey, "ra_tab%d" % ti], writes=["ra_t3"])
        P.op("dve", lambda e: e.tensor_tensor(out=t4[:], in0=B, in1=cs, op=ALU.mult), reads=[bkey, "ra_tab%d" % ti], writes=["ra_t4"])
        P.op("pool", lambda e: e.tensor_tensor(out=o1[i][:], in0=t1[:], in1=t2[:], op=ALU.subtract), reads=["ra_t1", "ra_t2"], writes=["ra_o1%d" % i])
        P.op("pool", lambda e: e.tensor_tensor(out=o2[i][:], in0=t3[:], in1=t4[:], op=ALU.add), reads=["ra_t3", "ra_t4"], writes=["ra_o2%d" % i])
        c = (cb * 4 + j - 1) % 16
        dst = T.qT if isq else T.kT
        wk = "qT" if isq else "kT"
        tts = range(tb * 4, tb * 4 + 4)
        dma(P, "sp", dst[c * 128:(c + 1) * 128, tb * 512:(tb + 1) * 512], o1[i][:], ["ra_o1%d" % i], [(wk, tt) for tt in tts], "ra_s1%d" % i)
        dma(P, "sp", dst[(c + 1) * 128:(c + 2) * 128, tb * 512:(tb + 1) * 512], o2[i][:], ["ra_o2%d" % i], [(wk, tt) for tt in tts], "ra_s2%d" % i)
        if not isq:
            h = c // 2
            ki = st["kcnt"] % 2
            st["kcnt"] += 1
            for half, ob in enumerate((o1[i], o2[i])):
                for q in range(4):
                    P.op("pe", lambda e, ob=ob, q=q, half=half: e.transpose(out=ktp[:, (half * 4 + q) * 128:(half * 4 + q + 1) * 128],
                                                                            in_=ob[:, q * 128:(q + 1) * 128], identity=C.ident[:]),
                         reads=["ra_o1%d" % i, "ra_o2%d" % i, "ident"], writes=["ra_ktp"])
            P.op("dve", lambda e, ki=ki, h=h: e.tensor_scalar(out=kto[ki][:].rearrange("p q (a f) -> p a q f", a=2),
                                                              in0=ktp[:].rearrange("p (a q f) -> p a q f", a=2, q=4),
                                                              scalar1=kdec[:, h:h + 1], scalar2=None, op0=ALU.mult),
                 reads=["ra_ktp", "ra_kdec"], writes=["ra_kto%d" % ki])
            dma(P, "sp", T.kd[tb * 512:(tb + 1) * 512, h * 256:(h + 1) * 256].rearrange("(q p) f -> p q f", p=128), kto[ki][:],
                ["ra_kto%d" % ki], [("kd", tt) for tt in tts], "ra_ks%d" % ki)

    linear_phase(P, C, T.xT, 2048, S, w_in, 0, 4096, "feat", qk_epi, "lqk_")
    P.release(m)
    m = P.mark()
    epi_v = make_store_epi_tok(P, T.v, 0, 512, BF16, "rbv_", "v")
    linear_phase(P, C, T.xT, 2048, S, w_in, 4096, 4096, "tok", epi_v, "lv_")
    P.release(m)
    m = P.mark()
    epi_g = make_store_epi_tok(P, T.sg, 0, 512, F32, "rbg_", "sg", func=AF.Silu)
    linear_phase(P, C, T.xT, 2048, S, w_in, 8192, 4096, "tok", epi_g, "lg_")
    P.release(m)
    m = P.mark()
    qt = [P.sb("rc_qt%d" % i, [128, 16, 128], BF16) for i in range(2)]
    kt = [P.sb("rc_kt%d" % i, [128, 16, 128], BF16) for i in range(2)]
    kdt = [P.sb("rc_kd%d" % i, [128, 2048], BF16) for i in range(2)]
    vt = [P.sb("rc_v%d" % i, [128, 4096], BF16) for i in range(2)]
    sgt = [P.sb("rc_sg%d" % i, [128, 4096], F32) for i in range(2)]
    Sf = P.sb("rc_Sf", [128, 16, 512], F32)
    Sb = P.sb("rc_Sb", [128, 16, 512], BF16)
    MT = P.sb("rc_MT", [128, 8, 128], F32)
    qdec = P.sb("rc_qdec", [128, 8, 128], F32)
    gng = P.sb("rc_gng", [128, 4096], F32)
    PT = [P.sb("rc_PT%d" % i, [128, 128], BF16) for i in range(2)]
    qd = [P.sb("rc_qd%d" % i, [128, 2, 128], BF16) for i in range(2)]
    on = [P.sb("rc_on%d" % i, [128, 512], F32) for i in range(2)]
    gb = [P.sb("rc_gb%d" % i, [128, 512], BF16) for i in range(2)]
    gto = [P.sb("rc_gto%d" % i, [128, 4, 128], BF16) for i in range(2)]
    stt = P.sb("rc_st", [128, 6], F32)
    mv = P.sb("rc_mv", [128, 4], F32)
    sc_ps = P.ps("rc_scps", [128, 128], F32)
    o_ps = [P.ps("rc_ops%d" % i, [128, 512], F32) for i in range(2)]
    s_ps = [P.ps("rc_sps%d" % i, [128, 512], F32) for i in range(2)]
    g_tp = P.ps("rc_gtp", [128, 512], BF16)
    dma(P, "sp", MT[:], T.retMT, [], ["rc_MT"], "rc_mld")
    dma(P, "sp", qdec[:], T.qdec, [], ["rc_qdec"], "rc_qdld")
    dma(P, "sp", gng[:], T.ret_gn_g[l].partition_broadcast(128), [], ["rc_gng"], "rc_gld")
    P.op("pool", lambda e: e.memset(Sf[:], 0.0), writes=[("Sf", hh, dd) for hh in range(8) for dd in range(2)])
    P.op("pool", lambda e: e.memset(Sb[:], 0.0), writes=[("Sb", hh) for hh in range(8)])
    hc = 0
    for t in range(NT):
        i = t % 2
        dma(P, "sp", qt[i][:], T.qT[:, t * 128:(t + 1) * 128].rearrange("(c p) t -> p c t", p=128), [("qT", t)], ["rc_qt%d" % i], "rc_l1%d" % i)
        dma(P, "sp", kt[i][:], T.kT[:, t * 128:(t + 1) * 128].rearrange("(c p) t -> p c t", p=128), [("kT", t)], ["rc_kt%d" % i], "rc_l2%d" % i)
        dma(P, "act", kdt[i][:], T.kd[t * 128:(t + 1) * 128, :], [("kd", t)], ["rc_kd%d" % i], "rc_l3%d" % i)
        dma(P, "act", vt[i][:], T.v[t * 128:(t + 1) * 128, :], [("v", t)], ["rc_v%d" % i], "rc_l4%d" % i)
        dma(P, "act", sgt[i][:], T.sg[t * 128:(t + 1) * 128, :], [("sg", t)], ["rc_sg%d" % i], "rc_l5%d" % i)
        for h in range(8):
            b = hc % 2
            hc += 1
            for dc in range(2):
                P.op("pe", lambda e, h=h, dc=dc, i=i: e.matmul(sc_ps[:], lhsT=kt[i][:, 2 * h + dc, :], rhs=qt[i][:, 2 * h + dc, :],
                                                               start=(dc == 0), stop=(dc == 1)),
                     reads=["rc_kt%d" % i, "rc_qt%d" % i], writes=["rc_scps"])
            P.op("dve", lambda e, h=h, b=b: e.tensor_tensor(out=PT[b][:], in0=sc_ps[:], in1=MT[:, h, :], op=ALU.mult),
                 reads=["rc_scps", "rc_MT"], writes=["rc_PT%d" % b])
            P.op("pool", lambda e, h=h, b=b, i=i: e.tensor_tensor(out=qd[b][:], in0=qt[i][:, 2 * h:2 * h + 2, :],
                                                                  in1=qdec[:, h, :].unsqueeze(1).to_broadcast([128, 2, 128]), op=ALU.mult),
                 reads=["rc_qt%d" % i, "rc_qdec"], writes=["rc_qd%d" % b])
            P.op("pe", lambda e, h=h, b=b, i=i: e.matmul(o_ps[b][:], lhsT=PT[b][:], rhs=vt[i][:, h * 512:(h + 1) * 512], start=True, stop=False),
                 reads=["rc_PT%d" % b, "rc_v%d" % i], writes=["rc_ops%d" % b])
            for dc in range(2):
                P.op("pe", lambda e, h=h, b=b, dc=dc: e.matmul(o_ps[b][:], lhsT=qd[b][:, dc, :], rhs=Sb[:, 2 * h + dc, :], start=False, stop=(dc == 1)),
                     reads=["rc_qd%d" % b, ("Sb", h)], writes=["rc_ops%d" % b])
            for dc in range(2):
                P.op("pe", lambda e, h=h, dc=dc, i=i: e.matmul(s_ps[dc][:], lhsT=kdt[i][:, h * 256 + dc * 128: h * 256 + (dc + 1) * 128],
                                                               rhs=vt[i][:, h * 512:(h + 1) * 512], start=True, stop=True),
                     reads=["rc_kd%d" % i, "rc_v%d" % i], writes=["rc_sps%d" % dc])
                P.op("dve", lambda e, h=h, dc=dc: e.scalar_tensor_tensor(out=Sf[:, 2 * h + dc, :], in0=Sf[:, 2 * h + dc, :], scalar=float(T.gamma128[h]),
                                                                         in1=s_ps[dc][:], op0=ALU.mult, op1=ALU.add),
                     reads=["rc_sps%d" % dc, ("Sf", h, dc)], writes=[("Sf", h, dc)])
                P.op("act", lambda e, h=h, dc=dc: e.copy(out=Sb[:, 2 * h + dc, :], in_=Sf[:, 2 * h + dc, :]),
                     reads=[("Sf", h, dc)], writes=[("Sb", h)])
            P.op("dve", lambda e, b=b: e.bn_stats(out=stt[:], in_=o_ps[b][:]), reads=["rc_ops%d" % b], writes=["rc_st"])
            P.op("dve", lambda e: e.bn_aggr(out=mv[:, 0:2], in_=stt[:]), reads=["rc_st"], writes=["rc_mv"])
            P.op("dve", lambda e: e.tensor_scalar(out=mv[:, 2:3], in0=mv[:, 1:2], scalar1=LN_EPS, scalar2=None, op0=ALU.add),
                 reads=["rc_mv"], writes=["rc_mv"])
            P.op("act", lambda e: e.activation(out=mv[:, 3:4], in_=mv[:, 2:3], func=AF.Sqrt), reads=["rc_mv"], writes=["rc_mv"])
            P.op("dve", lambda e: e.reciprocal(out=mv[:, 2:3], in_=mv[:, 3:4]), reads=["rc_mv"], writes=["rc_mv"])
            P.op("dve", lambda e, b=b: e.tensor_scalar(out=on[b][:], in0=o_ps[b][:], scalar1=mv[:, 0:1], scalar2=mv[:, 2:3],
                                                       op0=ALU.subtract, op1=ALU.mult),
                 reads=["rc_ops%d" % b, "rc_mv"], writes=["rc_on%d" % b])
            P.op("pool", lambda e, b=b, h=h: e.tensor_tensor(out=on[b][:], in0=on[b][:], in1=gng[:, h * 512:(h + 1) * 512], op=ALU.mult),
                 reads=["rc_on%d" % b, "rc_gng"], writes=["rc_on%d" % b])
            P.op("pool", lambda e, b=b, h=h, i=i: e.tensor_tensor(out=gb[b][:], in0=on[b][:], in1=sgt[i][:, h * 512:(h + 1) * 512], op=ALU.mult),
                 reads=["rc_on%d" % b, "rc_sg%d" % i], writes=["rc_gb%d" % b])
            for q in range(4):
                P.op("pe", lambda e, b=b, q=q: e.transpose(out=g_tp[:, q * 128:(q + 1) * 128], in_=gb[b][:, q * 128:(q + 1) * 128], identity=C.ident[:]),
                     reads=["rc_gb%d" % b, "ident"], writes=["rc_gtp"])
            P.op("act", lambda e, b=b: e.copy(out=gto[b][:], in_=g_tp[:].rearrange("p (q t) -> p q t", q=4)), reads=["rc_gtp"], writes=["rc_gto%d" % b])
            dma(P, "sp", T.gT[h * 512:(h + 1) * 512, t * 128:(t + 1) * 128].rearrange("(q p) t -> p q t", p=128), gto[b][:],
                ["rc_gto%d" % b], ["wo_actsrc"], "rc_gs%d" % b)
    P.release(m)
    m = P.mark()
    epi_o = make_store_epi_tok(P, T.mix, 0, 256, F32, "rdo_", "mix")
    linear_phase(P, C, T.gT, 4096, S, T.ret_w_out[l], 0, 2048, "tok", epi_o, "wo_", ncol=256)
    P.release(m)
    ln_pass(P, C, T.mix, xres_in_d, xres_out_d, T.xT, T.ln_g[l][0:1, :], T.ln_b[l][0:1, :], NT, "rln_")


def shared_kv_phase(P, C, T, NT):
    S = NT * 128
    m = P.mark()
    epi_k = make_store_epi_feat(P, T.KT, 0, "kvk_", "KT")
    linear_phase(P, C, T.xT, 2048, S, T.kv_w, 0, 2048, "feat", epi_k, "lkk_")
    P.release(m)
    m = P.mark()
    epi_v = make_store_epi_tok(P, T.Vs, 0, 512, BF16, "kvv_", "Vs")
    linear_phase(P, C, T.xT, 2048, S, T.kv_w, 2048, 2048, "tok", epi_v, "lkv_")
    P.release(m)


def sb_layer(P, C, T, l, xres_in_d, xres_out_d, NT):
    S = NT * 128
    lb = l - 2
    scale = 128.0 ** -0.5
    m = P.mark()
    epi_q = make_store_epi_feat(P, T.QT, 0, "sq_", "QT")
    linear_phase(P, C, T.xT, 2048, S, T.sb_wq[lb], 0, 2048, "feat", epi_q, "lsq_")
    P.release(m)
    m = P.mark()
    NQB = S // 512
    Kh = [P.sb("sa_K%d" % i, [128, S], BF16) for i in range(2)]
    Qh = [P.sb("sa_Q%d" % i, [128, S], BF16) for i in range(2)]
    Vh = [P.sb("sa_V%d" % i, [128, NT, 128], BF16) for i in range(2)]
    cm = P.sb("sa_cm", [128, 4, 512], F32)
    trif = P.sb("sa_trif", [128, 128], F32)
    onesf = P.sb("sa_onesf", [128, 128], F32)
    NS = 2
    ex = [P.sb("sa_ex%d" % i, [128, 512], F32) for i in range(NS)]
    sp = [P.sb("sa_sp%d" % i, [128, 512], F32) for i in range(NS)]
    spm = [P.sb("sa_spm%d" % i, [128, 512], F32) for i in range(NS)]
    spsum = [P.sb("sa_spsum%d" % i, [128, 512], F32) for i in range(NS)]
    u = [P.sb("sa_u%d" % i, [128, 512], F32) for i in range(NS)]
    lw = [P.sb("sa_lw%d" % i, [128, 512], F32) for i in range(NS)]
    ew = [P.sb("sa_ew%d" % i, [128, 512], F32) for i in range(NS)]
    w = [[P.sb("sa_w%d_%d" % (i, k), [128, 512], BF16) for k in range(2)] for i in range(NS)]
    oo = [P.sb("sa_oo%d" % i, [128, 512], BF16) for i in range(NS)]
    z_ps = [P.ps("sa_zps%d" % i, [128, 512], F32) for i in range(NS)]
    t_ps = [P.ps("sa_tps%d" % i, [128, 512], F32) for i in range(NS)]
    o_ps = [P.ps("sa_ops%d" % i, [128, 512], F32) for i in range(NS)]
    dma(P, "sp", cm[:], T.cmask, [], ["sa_cm"], "sa_cmld")
    P.op("pool", lambda e: e.memset(trif[:], 1.0), writes=["sa_trif"])
    P.op("pool", lambda e: e.affine_select(out=trif[:], in_=trif[:], pattern=[[-1, 128]], compare_op=ALU.is_gt, fill=0.0, base=0, channel_multiplier=1),
         reads=["sa_trif"], writes=["sa_trif"])
    P.op("pool", lambda e: e.memset(onesf[:], 1.0), writes=["sa_onesf"])
    wcnt = [0, 0]

    def step(si, hi, h, qb, ai, a, na):
        k = "%d" % si
        diag = a >= qb * 4
        P.op("pe", lambda e: e.matmul(z_ps[si][:], lhsT=Kh[hi][:, a * 128:(a + 1) * 128], rhs=Qh[hi][:, qb * 512:(qb + 1) * 512], start=True, stop=True),
             reads=["sa_K%d" % hi, "sa_Q%d" % hi], writes=["sa_zps" + k])
        P.op("act", lambda e: e.activation(out=ex[si][:], in_=z_ps[si][:], func=AF.Exp, scale=scale), reads=["sa_zps" + k], writes=["sa_ex" + k])
        P.op("act", lambda e: e.activation(out=sp[si][:], in_=ex[si][:], func=AF.Ln, bias=1.0, scale=1.0), reads=["sa_ex" + k], writes=["sa_sp" + k])
        if diag:
            P.op("pool", lambda e: e.tensor_tensor(out=spm[si][:], in0=sp[si][:], in1=cm[:, a - qb * 4, :], op=ALU.mult),
                 reads=["sa_sp" + k, "sa_cm"], writes=["sa_spm" + k])
            src, skey = spm[si], "sa_spm" + k
        else:
            src, skey = sp[si], "sa_sp" + k
        first = (ai == 0)
        P.op("pe", lambda e: e.matmul(t_ps[si][:], lhsT=trif[:], rhs=src[:], start=True, stop=first), reads=["sa_trif", skey], writes=["sa_tps" + k])
        if not first:
            P.op("pe", lambda e: e.matmul(t_ps[si][:], lhsT=onesf[:], rhs=spsum[si][:], start=False, stop=True), reads=["sa_onesf", "sa_spsum" + k], writes=["sa_tps" + k])
        P.op("dve", lambda e: e.tensor_tensor(out=u[si][:], in0=t_ps[si][:], in1=sp[si][:], op=ALU.add), reads=["sa_tps" + k, "sa_sp" + k], writes=["sa_u" + k])
        P.op("dve", lambda e: e.scalar_tensor_tensor(out=lw[si][:], in0=z_ps[si][:], scalar=scale, in1=u[si][:], op0=ALU.mult, op1=ALU.subtract),
             reads=["sa_zps" + k, "sa_u" + k], writes=["sa_lw" + k])
        if first:
            P.op("pool", lambda e: e.tensor_copy(out=spsum[si][:], in_=src[:]), reads=[skey], writes=["sa_spsum" + k])
        elif ai < na - 1:
            P.op("pool", lambda e: e.tensor_tensor(out=spsum[si][:], in0=spsum[si][:], in1=src[:], op=ALU.add), reads=[skey, "sa_spsum" + k], writes=["sa_spsum" + k])
        wi = wcnt[si] % 2
        wcnt[si] += 1
        wk = "sa_w%d_%d" % (si, wi)
        if diag:
            P.op("act", lambda e: e.activation(out=ew[si][:], in_=lw[si][:], func=AF.Exp), reads=["sa_lw" + k], writes=["sa_ew" + k])
            P.op("pool", lambda e: e.tensor_tensor(out=w[si][wi][:], in0=ew[si][:], in1=cm[:, a - qb * 4, :], op=ALU.mult),
                 reads=["sa_ew" + k, "sa_cm"], writes=[wk])
        else:
            P.op("act", lambda e: e.activation(out=w[si][wi][:], in_=lw[si][:], func=AF.Exp), reads=["sa_lw" + k], writes=[wk])
        P.op("pe", lambda e: e.matmul(o_ps[si][:], lhsT=Vh[hi][:, a, :], rhs=w[si][wi][:], start=(ai == 0), stop=(ai == na - 1)),
             reads=["sa_V%d" % hi, wk], writes=["sa_ops" + k])
        if ai == na - 1:
            P.op("act", lambda e: e.copy(out=oo[si][:], in_=o_ps[si][:]), reads=["sa_ops" + k], writes=["sa_oo" + k])
            dma(P, "sp", T.oT[h * 128:(h + 1) * 128, qb * 512:(qb + 1) * 512], oo[si][:], ["sa_oo" + k], ["wo_actsrc"], "sa_ost" + k)

    for h in range(16):
        hi = h % 2
        dma(P, "sp", Kh[hi][:], T.KT[h * 128:(h + 1) * 128, :], [], ["sa_K%d" % hi], "sa_kld%d" % hi)
        dma(P, "sp", Qh[hi][:], T.QT[h * 128:(h + 1) * 128, :], [], ["sa_Q%d" % hi], "sa_qld%d" % hi)
        dma(P, "act", Vh[hi][:], T.Vs[:, h * 128:(h + 1) * 128].rearrange("(a p) d -> p a d", p=128), [], ["sa_V%d" % hi], "sa_vld%d" % hi)
        qbs = list(range(NQB))
        pairs = []
        while qbs:
            a0 = qbs.pop(0)
            b0 = qbs.pop(-1) if qbs else None
            pairs.append((a0, b0))
        for (qa, qb_) in pairs:
            items = [(0, qa)] + ([(1, qb_)] if qb_ is not None else [])
            lists = []
            for si, q in items:
                na = (q + 1) * 4
                lists.append([(si, q, ai, a, na) for ai, a in enumerate(range(na - 1, -1, -1))])
            n = max(len(x) for x in lists)
            pos = [0] * len(lists)
            for stepi in range(n):
                for li, lst in enumerate(lists):
                    tgt = (stepi + 1) * len(lst) // n
                    while pos[li] < tgt:
                        si, q, ai, a, na = lst[pos[li]]
                        step(si, hi, h, q, ai, a, na)
                        pos[li] += 1
    P.release(m)
    m = P.mark()
    epi_o = make_store_epi_tok(P, T.mix, 0, 512, F32, "sdo_", "mix")
    linear_phase(P, C, T.oT, 2048, S, T.sb_w_out[lb], 0, 2048, "tok", epi_o, "wo_")
    P.release(m)
    ln_pass(P, C, T.mix, xres_in_d, xres_out_d, T.xT, T.ln_g[l][0:1, :], T.ln_b[l][0:1, :], NT, "sln_")


from concourse.bass_utils import run_bass_kernel_spmd

N_CORES = 4
SEQ = 4096
RET_HEADS = 8


def _consts(S):
    pos = np.arange(S, dtype=np.float32)
    inv_freq = (10000.0 ** (-np.arange(0, 256, 2, dtype=np.float32) / 256)).astype(np.float32)
    ang = (pos[None, :] * inv_freq[:, None]).astype(np.float32)
    cos = np.cos(ang.astype(np.float64)).astype(np.float32)
    sin = np.sin(ang.astype(np.float64)).astype(np.float32)
    rope = np.stack([cos, sin, cos / 16.0, sin / 16.0]).astype(np.float32)
    lg = np.log(1.0 - np.exp2(-5.0 - np.arange(8, dtype=np.float64)))
    n = np.arange(128)
    ch = n // 64
    MT = np.zeros((128, 8, 128), np.float32)
    for h in range(8):
        dist = np.abs(n[:, None] - n[None, :]).astype(np.float64)
        Mh = np.where(ch[:, None] == ch[None, :], np.exp(lg[h] * dist),
                      np.where(ch[None, :] < ch[:, None], np.exp(lg[h] * (n[:, None] - n[None, :])), 0.0))
        MT[:, h, :] = Mh.T
    qdec = np.zeros((128, 8, 128), np.float32)
    kdec = np.zeros((128, 8), np.float32)
    for h in range(8):
        qdec[:, h, :] = np.exp(lg[h] * (n + 1.0))[None, :]
        kdec[:, h] = np.exp(lg[h] * (127.0 - n))
    gamma128 = [float(np.exp(lg[h] * 128.0)) for h in range(8)]
    s = np.arange(128)[:, None]
    t = np.arange(512)[None, :]
    cmask = np.stack([((k * 128 + s) < t).astype(np.float32) for k in range(4)], axis=1)
    return dict(rope=rope, retMT=MT, qdec=qdec, kdec=kdec, cmask=np.ascontiguousarray(cmask)), gamma128


def build_nc(S=SEQ, depth=4, n_a=2):
    NT = S // 128
    nc = bass.Bass("TRN2", target_bir_lowering=False)
    T = Ctx()

    def din(name, shape):
        return nc.dram_tensor(name, list(shape), F32, kind="ExternalInput").ap()

    def dsc(name, shape, dt):
        return nc.dram_tensor(name, list(shape), dt, kind="Internal").ap()

    x = din("x", [S, 2048])
    T.ret_w_in = din("ret_w_in", [2, 2048, 12288])
    T.ret_gn_g = din("ret_gn_g", [2, 4096])
    T.ret_gn_g = [T.ret_gn_g[i:i + 1, :] for i in range(2)]
    T.ret_w_out = din("ret_w_out", [2, 4096, 2048])
    T.kv_w = din("kv_w", [2048, 4096])
    T.sb_wq = din("sb_wq", [2, 2048, 2048])
    T.sb_w_out = din("sb_w_out", [2, 2048, 2048])
    T.peer_wq = din("peer_wq", [4, 2048, 2048])
    T.keysT = din("keysT", [4, 128, 16, 128])
    T.UTp = din("UTp", [4, 2048, 128, 128])
    T.Vp = din("Vp", [4, 128, 128, 2048])
    T.ln_g = din("ln_g", [4, 2, 2048])
    T.ln_b = din("ln_b", [4, 2, 2048])
    T.rope = din("rope", [4, 128, S])
    T.retMT = din("retMT", [128, 8, 128])
    T.qdec = din("qdec", [128, 8, 128])
    T.kdec = din("kdec", [128, 8])
    T.cmask = din("cmask", [128, 4, 512])
    out = nc.dram_tensor("out", [S, 2048], F32, kind="ExternalOutput").ap()
    _, T.gamma128 = _consts(128)
    xres = dsc("xres", [S, 2048], F32)
    T.xT = dsc("xT", [2048, S], BF16)
    T.qT = dsc("qT", [2048, S], BF16)
    T.kT = dsc("kT", [2048, S], BF16)
    T.kd = dsc("kd", [S, 2048], BF16)
    T.v = dsc("v", [S, 4096], BF16)
    T.sg = dsc("sg", [S, 4096], F32)
    T.gT = dsc("gT", [4096, S], BF16)
    T.mix = dsc("mix", [S, 2048], F32)
    T.KT = dsc("KT", [2048, S], BF16)
    T.Vs = dsc("Vs", [S, 2048], BF16)
    T.QT = dsc("QT", [2048, S], BF16)
    T.oT = dsc("oT", [2048, S], BF16)
    T.pq = dsc("pq", [2048, S], BF16)
    T.UTb = dsc("UTb", [2048, 128, 128], BF16)
    T.Vb = dsc("Vb", [128, 128, 2048], BF16)
    T.GTd = dsc("GTd", [NT, 128, 128, 128], BF16)
    P = Prog(nc)
    C = Ctx()
    make_ident(P, C)
    phase_x_to_xT(P, C, x, T.xT, NT)
    cur = x
    for l in range(depth):
        if l < n_a:
            retention_layer(P, C, T, l, cur, xres, NT)
        else:
            sb_layer(P, C, T, l, cur, xres, NT)
        cur = xres
        m = P.mark()
        epi_pq = make_store_epi_feat(P, T.pq, 0, "pq_", "pq")
        linear_phase(P, C, T.xT, 2048, S, T.peer_wq[l], 0, 2048, "feat", epi_pq, "lpq_")
        P.release(m)
        peer_precast(P, C, T.UTp[l].rearrange("(p a) j i -> p (a j i)", p=128), T.UTb.rearrange("(p a) j i -> p (a j i)", p=128), 16 * 128 * 128, "pcu_")
        peer_precast(P, C, T.Vp[l].rearrange("j i d -> j (i d)"), T.Vb.rearrange("j i d -> j (i d)"), 128 * 2048, "pcv_")
        peer_gbuild(P, C, T.pq, T.keysT[l], T.GTd, NT)
        m = P.mark()
        C.ln = ln_alloc(P, C, "pln_")
        ln_load_params(P, C.ln, T.ln_g[l][1:2, :], T.ln_b[l][1:2, :])
        last = (l == depth - 1)
        dst = out if last else xres

        def out_cb(t, yap, ykey, dst=dst, last=last):
            ln_tile(P, C, C.ln, yap, [ykey], xres, dst, None if last else T.xT, t)
        peer_main(P, C, T.xT, T.UTb, T.Vb, T.GTd, NT, out_cb)
        P.release(m)
        if l == n_a - 1 and depth > n_a:
            shared_kv_phase(P, C, T, NT)
    P.flush()
    P.close()
    return nc


_NC_CACHE = {}


def kernel(x, ret_w_in, ret_gn_g, ret_w_out, kv_w, sb_wq, sb_w_out, peer_wq, peer_sub_keys, peer_u, peer_v, ln_g, ln_b):
    f = lambda a: np.ascontiguousarray(np.asarray(a, dtype=np.float32))
    x = f(x)
    B, S, _ = x.shape
    consts, _ = _consts(S)
    sk = f(peer_sub_keys)
    keysT = np.ascontiguousarray(sk.reshape(4, 16, 128, 128).transpose(0, 3, 1, 2))
    pu = f(peer_u).reshape(4, 128, 128, 2048)
    UTp = np.ascontiguousarray(pu.transpose(0, 3, 2, 1))
    del pu
    pv = f(peer_v).reshape(4, 128, 128, 2048)
    Vp = np.ascontiguousarray(pv.transpose(0, 2, 1, 3))
    del pv
    shared = dict(ret_w_in=f(ret_w_in), ret_gn_g=f(ret_gn_g), ret_w_out=f(ret_w_out), kv_w=f(kv_w), sb_wq=f(sb_wq),
                  sb_w_out=f(sb_w_out), peer_wq=f(peer_wq), keysT=keysT, UTp=UTp, Vp=Vp, ln_g=f(ln_g), ln_b=f(ln_b), **consts)
    if S not in _NC_CACHE:
        _NC_CACHE[S] = build_nc(S)
    nc = _NC_CACHE[S]
    in_maps = [dict(shared, x=np.ascontiguousarray(x[b])) for b in range(B)]
    res = run_bass_kernel_spmd(nc, in_maps, core_ids=list(range(B)))
    return np.stack([np.asarray(r["out"], dtype=np.float32) for r in res.results], axis=0)
```

```python
import numpy as np
import concourse.bass as bass
import concourse.mybir as mybir

F32 = mybir.dt.float32
BF16 = mybir.dt.bfloat16
ALU = mybir.AluOpType
AF = mybir.ActivationFunctionType
AX = mybir.AxisListType

ENGS = ("pe", "act", "dve", "pool", "sp")


class Op:
    __slots__ = ("eng", "fn", "reads", "writes", "dma", "deps", "idx", "eidx",
                 "need_inc", "inc_val", "slot")

    def __init__(self, eng, fn, reads, writes, dma):
        self.eng = eng
        self.fn = fn
        self.reads = reads
        self.writes = writes
        self.dma = dma
        self.deps = []
        self.need_inc = False
        self.inc_val = None
        self.slot = None


class Prog:
    def __init__(self, nc):
        self.nc = nc
        self.ops = []
        self.last_w = {}
        self.readers = {}
        self.eng_sems = None
        self.eng_cnt = {e: 0 for e in ENGS}
        self.dma_sems = {}
        self.waited = {e: {} for e in ENGS}
        self.eng_nops = {e: 0 for e in ENGS}
        self._ctx = []

    def sem(self, name):
        cm = self.nc.semaphore(name)
        h = cm.__enter__()
        self._ctx.append((cm, "sem"))
        return h

    def sb(self, name, shape, dt):
        self._uid = getattr(self, "_uid", 0) + 1
        cm = self.nc.sbuf_tensor("%s_%d" % (name, self._uid), list(shape), dt)
        h = cm.__enter__()
        self._ctx.append((cm, "sb"))
        return h

    def ps(self, name, shape, dt):
        self._uid = getattr(self, "_uid", 0) + 1
        cm = self.nc.psum_tensor("%s_%d" % (name, self._uid), list(shape), dt)
        h = cm.__enter__()
        self._ctx.append((cm, "ps"))
        return h

    def close(self):
        for cm, kind in reversed(self._ctx):
            cm.__exit__(None, None, None)
        self._ctx = []

    def mark(self):
        return len(self._ctx)

    def release(self, mark):
        self.flush()
        keep = []
        tail = self._ctx[mark:]
        self._ctx = self._ctx[:mark]
        for cm, kind in reversed(tail):
            if kind == "sem":
                keep.append((cm, kind))
            else:
                cm.__exit__(None, None, None)
        self._ctx.extend(reversed(keep))

    def init_sems(self):
        self.eng_sems = {e: self.sem("s_" + e) for e in ENGS}

    def op(self, eng, fn, reads=(), writes=(), dma=None):
        o = Op(eng, fn, tuple(reads), tuple(writes), dma)
        o.idx = len(self.ops)
        o.eidx = self.eng_nops[eng]
        self.eng_nops[eng] += 1
        deps = set()
        for k in o.reads:
            w = self.last_w.get(k)
            if w is not None:
                deps.add(w)
        for k in o.writes:
            w = self.last_w.get(k)
            if w is not None:
                deps.add(w)
            for r in self.readers.get(k, ()):
                deps.add(r)
        deps.discard(o.idx)
        for k in o.reads:
            self.readers.setdefault(k, []).append(o.idx)
        for k in o.writes:
            self.last_w[k] = o.idx
            self.readers[k] = []
        o.deps = sorted(deps)
        self.ops.append(o)
        return o

    def flush(self):
        nc = self.nc
        ops = self.ops
        if not ops:
            return
        if self.eng_sems is None:
            self.init_sems()
        edges = {}
        for o in ops:
            need = []
            for d in o.deps:
                p = ops[d]
                if p.dma is None and p.eng == o.eng and o.dma is None:
                    if o.eng == "pe":
                        continue
                    if o.eidx - p.eidx > 2:
                        continue
                need.append(d)
                if p.dma is None:
                    p.need_inc = True
            edges[o.idx] = need
        last_on = {}
        for o in ops:
            if o.dma is None:
                last_on[o.eng] = o
        for e, o in last_on.items():
            o.need_inc = True
        if not hasattr(self, "dma_pool"):
            self.dma_pool = []
        slotmap = {}
        for o in ops:
            if o.dma is not None:
                if o.dma not in slotmap:
                    k = len(slotmap)
                    if k >= len(self.dma_pool):
                        self.dma_pool.append([self.sem("dpool%d" % k), 0])
                    slotmap[o.dma] = k
                o.slot = slotmap[o.dma]
                ent = self.dma_pool[o.slot]
                ent[1] += 16
                o.inc_val = ent[1]
            elif o.need_inc:
                self.eng_cnt[o.eng] += 1
                o.inc_val = self.eng_cnt[o.eng]
        per_eng = {e: [] for e in ENGS}
        for o in ops:
            per_eng[o.eng].append(o)
        final_eng = dict(self.eng_cnt)
        final_dma = {k: v[1] for k, v in enumerate(self.dma_pool)}

        def emit_stream(ename, eobj):
            waited = self.waited[ename]
            for o in per_eng[ename]:
                for d in edges[o.idx]:
                    p = ops[d]
                    if p.dma is not None:
                        key = ("d", p.slot)
                        sem = self.dma_pool[p.slot][0]
                    else:
                        key = ("e", p.eng)
                        sem = self.eng_sems[p.eng]
                    if waited.get(key, 0) >= p.inc_val:
                        continue
                    waited[key] = p.inc_val
                    eobj.wait_ge(sem, p.inc_val)
                ins = o.fn(eobj)
                if o.dma is not None:
                    ins.then_inc(self.dma_pool[o.slot][0], 16)
                elif o.need_inc:
                    ins.then_inc(self.eng_sems[o.eng], 1)
            for e2 in ENGS:
                if final_eng[e2] > 0 and e2 != ename and waited.get(("e", e2), 0) < final_eng[e2]:
                    eobj.wait_ge(self.eng_sems[e2], final_eng[e2])
                    waited[("e", e2)] = final_eng[e2]
            for k, v in final_dma.items():
                if v > 0 and waited.get(("d", k), 0) < v:
                    eobj.wait_ge(self.dma_pool[k][0], v)
                    waited[("d", k)] = v

        with nc.Block() as block:
            @block.tensor
            def _(e):
                emit_stream("pe", e)

            @block.scalar
            def _(e):
                emit_stream("act", e)

            @block.vector
            def _(e):
                emit_stream("dve", e)

            @block.gpsimd
            def _(e):
                emit_stream("pool", e)

            @block.sync
            def _(e):
                emit_stream("sp", e)

        self.ops = []
        self.last_w = {}
        self.readers = {}
        self.eng_nops = {e: 0 for e in ENGS}


import math
import numpy as np

D = 2048
KC = 16
ALPHA = 8.0 ** 0.25
LN_EPS = 1e-5
U32 = mybir.dt.uint32


class Ctx:
    pass


def dma(P, eng, out, in_, reads, writes, slot):
    P.op(eng, lambda e: e.dma_start(out=out, in_=in_), reads=reads, writes=writes, dma=slot)


def make_ident(P, C):
    identf = P.sb("identf", [128, 128], F32)
    C.ident = P.sb("ident", [128, 128], BF16)
    P.op("pool", lambda e: e.memset(identf[:], 1.0), writes=["identf"])
    P.op("pool", lambda e: e.affine_select(out=identf[:], in_=identf[:], pattern=[[-1, 128]],
                                           compare_op=ALU.is_equal, fill=0.0, base=0, channel_multiplier=1),
         reads=["identf"], writes=["identf"])
    P.op("dve", lambda e: e.tensor_copy(out=C.ident[:], in_=identf[:]), reads=["identf"], writes=["ident"])
    C.iota = P.sb("iota_i", [128, 128], F32)
    P.op("pool", lambda e: e.iota(C.iota[:], pattern=[[1, 128]], base=0, channel_multiplier=0,
                                  allow_small_or_imprecise_dtypes=True), writes=["iota"])


def emit_tile_to_xT(P, C, src_sb, src_key, xT_d, t, pfx, bufs):
    xb, tp, xo = bufs
    P.op("act", lambda e: e.copy(out=xb[:], in_=src_sb), reads=[src_key], writes=[pfx + "xb"])
    for half in range(2):
        for c in range(8):
            cc = half * 8 + c
            P.op("pe", lambda e, cc=cc, c=c: e.transpose(out=tp[:, c * 128:(c + 1) * 128], in_=xb[:, cc * 128:(cc + 1) * 128],
                                                         identity=C.ident[:]),
                 reads=[pfx + "xb", "ident"], writes=[pfx + "tp"])
        P.op("dve", lambda e, half=half: e.tensor_copy(out=xo[:, half * 8:(half + 1) * 8, :],
                                                       in_=tp[:].rearrange("p (c t) -> p c t", c=8)),
             reads=[pfx + "tp"], writes=[pfx + "xo"])
    dma(P, "sp", xT_d[:, t * 128:(t + 1) * 128].rearrange("(c p) t -> p c t", p=128), xo[:],
        [pfx + "xo"], [("xT", t)], pfx + "xo_st")


def phase_x_to_xT(P, C, x_d, xT_d, NT):
    m = P.mark()
    xs = [P.sb("a_xs%d" % i, [128, D], F32) for i in range(2)]
    xb = P.sb("a_xb", [128, D], BF16)
    tp = P.ps("a_tp", [128, 1024], BF16)
    xo = P.sb("a_xo", [128, 16, 128], BF16)
    for t in range(NT):
        b = t % 2
        dma(P, "sp", xs[b][:], x_d[t * 128:(t + 1) * 128, :], [("xres", t)], ["a_xs%d" % b], "a_ld%d" % b)
        emit_tile_to_xT(P, C, xs[b][:], "a_xs%d" % b, xT_d, t, "a_", (xb, tp, xo))
    P.release(m)


def ln_alloc(P, C, pfx):
    L = Ctx()
    L.pfx = pfx
    L.g = P.sb(pfx + "g", [128, D], F32)
    L.b = P.sb(pfx + "b", [128, D], F32)
    L.xin = P.sb(pfx + "xin", [128, D], F32)
    L.y = P.sb(pfx + "y", [128, D], F32)
    L.st = P.sb(pfx + "st", [128, 4, 6], F32)
    L.mv = P.sb(pfx + "mv", [128, 4], F32)
    L.xb = P.sb(pfx + "xb", [128, D], BF16)
    L.tp = P.ps(pfx + "tp", [128, 1024], BF16)
    L.xo = P.sb(pfx + "xo", [128, 16, 128], BF16)
    return L


def ln_load_params(P, L, g_d, b_d):
    dma(P, "sp", L.g[:], g_d.partition_broadcast(128), [], [L.pfx + "g"], L.pfx + "gld")
    dma(P, "sp", L.b[:], b_d.partition_broadcast(128), [], [L.pfx + "b"], L.pfx + "bld")


def ln_tile(P, C, L, mix_ap, mix_keys, xres_in_d, xres_out_d, xT_d, t, mix_in_psum_parts=None):
    pfx = L.pfx
    dma(P, "sp", L.xin[:], xres_in_d[t * 128:(t + 1) * 128, :], [("xres", t)], [pfx + "xin"], pfx + "xin_ld")
    P.op("dve", lambda e: e.scalar_tensor_tensor(out=L.y[:], in0=L.xin[:], scalar=ALPHA, in1=mix_ap,
                                                 op0=ALU.mult, op1=ALU.add),
         reads=[pfx + "xin"] + list(mix_keys), writes=[pfx + "y"])
    for q in range(4):
        P.op("dve", lambda e, q=q: e.bn_stats(out=L.st[:, q, :], in_=L.y[:, q * 512:(q + 1) * 512]),
             reads=[pfx + "y"], writes=[pfx + "st"])
    P.op("dve", lambda e: e.bn_aggr(out=L.mv[:, 0:2], in_=L.st[:].rearrange("p a b -> p (a b)")), reads=[pfx + "st"], writes=[pfx + "mv"])
    P.op("dve", lambda e: e.tensor_scalar(out=L.mv[:, 2:3], in0=L.mv[:, 1:2], scalar1=LN_EPS, scalar2=None, op0=ALU.add),
         reads=[pfx + "mv"], writes=[pfx + "mv"])
    P.op("act", lambda e: e.activation(out=L.mv[:, 3:4], in_=L.mv[:, 2:3], func=AF.Sqrt), reads=[pfx + "mv"], writes=[pfx + "mv"])
    P.op("dve", lambda e: e.reciprocal(out=L.mv[:, 2:3], in_=L.mv[:, 3:4]), reads=[pfx + "mv"], writes=[pfx + "mv"])
    P.op("dve", lambda e: e.tensor_scalar(out=L.y[:], in0=L.y[:], scalar1=L.mv[:, 0:1], scalar2=L.mv[:, 2:3],
                                          op0=ALU.subtract, op1=ALU.mult),
         reads=[pfx + "y", pfx + "mv"], writes=[pfx + "y"])
    P.op("pool", lambda e: e.tensor_tensor(out=L.y[:], in0=L.y[:], in1=L.g[:], op=ALU.mult),
         reads=[pfx + "y", pfx + "g"], writes=[pfx + "y"])
    P.op("pool", lambda e: e.tensor_tensor(out=L.y[:], in0=L.y[:], in1=L.b[:], op=ALU.add),
         reads=[pfx + "y", pfx + "b"], writes=[pfx + "y"])
    dma(P, "sp", xres_out_d[t * 128:(t + 1) * 128, :], L.y[:], [pfx + "y"], [("xres", t)], pfx + "y_st")
    if xT_d is not None:
        emit_tile_to_xT(P, C, L.y[:], pfx + "y", xT_d, t, pfx, (L.xb, L.tp, L.xo))


def linear_phase(P, C, actT_d, K, S, W_d, n0, N, mode, epilogue, pfx, ncol=512, TB=512, post_block=None):
    kc = K // 128
    if kc * ncol > 8192:
        ncol = 8192 // kc
    NSUB = 2 if ((N // ncol) % 2 == 0 and kc <= 16) else 1
    m = P.mark()
    kh = kc // 2
    wf = [P.sb(pfx + "wf%d" % i, [128, kh, ncol], F32) for i in range(2)]
    wb = [P.sb(pfx + "wb%d" % i, [128, NSUB, kc, ncol], BF16) for i in range(2)]
    ab = [P.sb(pfx + "ab%d" % i, [128, kc, TB], BF16) for i in range(2)]
    pb = [P.ps(pfx + "pb%d" % i, [128, 512], F32) for i in range(4)]
    nsb = N // (ncol * NSUB)
    ntb = S // TB
    cnt = 0
    pcount = 0
    wcnt = 0
    for sbk in range(nsb):
        wi = sbk % 2
        for sub in range(NSUB):
            for half in range(2):
                fi = wcnt % 2
                wcnt += 1
                c0 = n0 + (sbk * NSUB + sub) * ncol
                dma(P, "sp", wf[fi][:], W_d[half * kh * 128:(half + 1) * kh * 128, c0:c0 + ncol].rearrange("(kc p) n -> p kc n", p=128),
                    [], [pfx + "wf%d" % fi], pfx + "wld%d" % fi)
                P.op("pool", lambda e, wi=wi, fi=fi, sub=sub, half=half: e.tensor_copy(out=wb[wi][:, sub, half * kh:(half + 1) * kh, :], in_=wf[fi][:]),
                     reads=[pfx + "wf%d" % fi], writes=[pfx + "wb%d" % wi])
        for tb in range(ntb):
            ai = cnt % 2
            cnt += 1
            dma(P, "act" if cnt % 2 else "sp", ab[ai][:], actT_d[:, tb * TB:(tb + 1) * TB].rearrange("(kc p) t -> p kc t", p=128),
                [pfx + "actsrc"], [pfx + "ab%d" % ai], pfx + "ald%d" % ai)
            for sub in range(NSUB):
                cb = sbk * NSUB + sub
                if mode == "tok":
                    for ti in range(TB // 128):
                        t = tb * (TB // 128) + ti
                        pi = pcount % 4
                        pcount += 1
                        for k in range(kc):
                            P.op("pe", lambda e, k=k, ai=ai, wi=wi, ti=ti, pi=pi, sub=sub: e.matmul(
                                pb[pi][:, 0:ncol], lhsT=ab[ai][:, k, ti * 128:(ti + 1) * 128], rhs=wb[wi][:, sub, k, :],
                                start=(k == 0), stop=(k == kc - 1)),
                                reads=[pfx + "ab%d" % ai, pfx + "wb%d" % wi], writes=[pfx + "pb%d" % pi])
                        epilogue(t, cb, pb[pi][:, 0:ncol], pfx + "pb%d" % pi)
                else:
                    for j in range(ncol // 128):
                        pi = pcount % 4
                        pcount += 1
                        for k in range(kc):
                            P.op("pe", lambda e, k=k, ai=ai, wi=wi, j=j, pi=pi, sub=sub: e.matmul(
                                pb[pi][:, 0:TB], lhsT=wb[wi][:, sub, k, j * 128:(j + 1) * 128], rhs=ab[ai][:, k, :],
                                start=(k == 0), stop=(k == kc - 1)),
                                reads=[pfx + "ab%d" % ai, pfx + "wb%d" % wi], writes=[pfx + "pb%d" % pi])
                        epilogue(tb, cb, j, pb[pi][:, 0:TB], pfx + "pb%d" % pi)
    P.release(m)


def peer_precast(P, C, src_d, dst_d, nelem_per_part, pfx):
    m = P.mark()
    CH = 8192
    f = [P.sb(pfx + "f%d" % i, [128, CH], F32) for i in range(2)]
    b = [P.sb(pfx + "b%d" % i, [128, CH], BF16) for i in range(2)]
    n = nelem_per_part // CH
    engs = ["dve", "pool", "act"]
    for i in range(n):
        bi = i % 2
        dma(P, "sp" if bi == 0 else "act", f[bi][:], src_d[:, i * CH:(i + 1) * CH], [], [pfx + "f%d" % bi], pfx + "ld%d" % bi)
        eg = ["dve", "act"][i % 2]
        if eg == "act":
            P.op("act", lambda e, bi=bi: e.copy(out=b[bi][:], in_=f[bi][:]), reads=[pfx + "f%d" % bi], writes=[pfx + "b%d" % bi])
        else:
            P.op(eg, lambda e, bi=bi: e.tensor_copy(out=b[bi][:], in_=f[bi][:]), reads=[pfx + "f%d" % bi], writes=[pfx + "b%d" % bi])
        dma(P, "pool" if bi == 0 else "sp", dst_d[:, i * CH:(i + 1) * CH], b[bi][:], [pfx + "b%d" % bi], [pfx + "dst%d" % i], pfx + "st%d" % bi)
    P.release(m)


def peer_gbuild(P, C, qT_d, keysT_d, GT_d, NT):
    m0 = P.mark()
    kb = P.sb("g_kb", [128, 16, 128], BF16)
    m1 = P.mark()
    kf = P.sb("g_kf", [128, 16, 128], F32)
    dma(P, "sp", kf[:], keysT_d, [], ["g_kf"], "g_kld")
    P.op("dve", lambda e: e.tensor_copy(out=kb[:], in_=kf[:]), reads=["g_kf"], writes=["g_kb"])
    P.release(m1)
    qt = [P.sb("g_qt%d" % i, [128, 16, 128], BF16) for i in range(2)]
    S_sb = P.sb("g_S", [128, 16, 128], F32)
    wk = P.sb("g_wk", [128, 256], F32)
    V16 = P.sb("g_V16", [128, 16, 16], F32)
    I1u = P.sb("g_I1u", [128, 8, 16], U32)
    I1f = P.sb("g_I1f", [128, 8, 16], F32)
    I1b = P.sb("g_I1b", [128, 128], BF16)
    I1T = [P.sb("g_I1T%d" % i, [128, 128], F32) for i in range(2)]
    cand = P.sb("g_cand", [128, 256], F32)
    T16 = P.sb("g_T16", [128, 8, 16], F32)
    neg = P.sb("g_neg", [128, 16], F32)
    negmx = P.sb("g_negmx", [128, 8], F32)
    e1 = P.sb("g_e1", [128, 8, 16], F32)
    e2 = P.sb("g_e2", [128, 8, 128], F32)
    junk = P.sb("g_junk", [128, 16], F32)
    Z = P.sb("g_Z", [128, 8], F32)
    rZ = P.sb("g_rZ", [128, 8], F32)
    cc = P.sb("g_cc", [128, 8, 16], F32)
    tmp = [P.sb("g_tmp%d" % i, [128, 8, 128], F32) for i in range(2)]
    tmp2 = [P.sb("g_tmp2%d" % i, [128, 8, 128], F32) for i in range(2)]
    R = [P.sb("g_R%d" % i, [128, 128, 128], BF16) for i in range(2)]
    RT2 = P.sb("g_RT2", [128, 128, 128], BF16)
    A = P.sb("g_A", [128, 64, 128], BF16)
    GT = P.sb("g_GT", [128, 128, 128], BF16)
    sps = P.ps("g_sps", [128, 512], F32)
    gps = [P.ps("g_gps%d" % i, [128, 512], F32) for i in range(2)]
    tps = [P.ps("g_tps%d" % i, [128, 1024], BF16) for i in range(2)]
    ips = P.ps("g_ips", [128, 128], BF16)

    def stage1(t):
        qi = t % 2
        ri = t % 2
        qk = "g_qt%d" % qi
        rk = "g_R%d" % ri
        dma(P, "sp", qt[qi][:], qT_d[:, t * 128:(t + 1) * 128].rearrange("(c p) t -> p c t", p=128), [], [qk], "g_qld%d" % qi)
        for g4 in range(4):
            for c4 in range(4):
                c = g4 * 4 + c4
                P.op("pe", lambda e, c=c, c4=c4: e.matmul(sps[:, c4 * 128:(c4 + 1) * 128], lhsT=qt[qi][:, c, :], rhs=kb[:, c, :],
                                                          start=True, stop=True), reads=[qk, "g_kb"], writes=["g_sps"])
            P.op("act", lambda e, g4=g4: e.copy(out=S_sb[:, g4 * 4:(g4 + 1) * 4, :], in_=sps[:].rearrange("p (c k) -> p c k", c=4)),
                 reads=["g_sps"], writes=["g_S"])
        for c in range(16):
            P.op("dve", lambda e, c=c: e.max(out=V16[:, c, 0:8], in_=S_sb[:, c, :]), reads=["g_S"], writes=["g_V16"])
            if c % 2 == 0:
                P.op("dve", lambda e, c=c: e.max_index(out=I1u[:, c // 2, 0:8], in_max=V16[:, c, 0:8], in_values=S_sb[:, c, :]),
                     reads=["g_S", "g_V16"], writes=["g_I1u"])
            P.op("dve", lambda e, c=c: e.match_replace(out=wk[:, 0:128], in_to_replace=V16[:, c, 0:8], in_values=S_sb[:, c, :], imm_value=-1e30),
                 reads=["g_S", "g_V16"], writes=["g_wk"])
            P.op("dve", lambda e, c=c: e.max(out=V16[:, c, 8:16], in_=wk[:, 0:128]), reads=["g_wk"], writes=["g_V16"])
            if c % 2 == 0:
                P.op("dve", lambda e, c=c: e.max_index(out=I1u[:, c // 2, 8:16], in_max=V16[:, c, 8:16], in_values=wk[:, 0:128]),
                     reads=["g_wk", "g_V16"], writes=["g_I1u"])
        P.op("dve", lambda e: e.tensor_copy(out=I1f[:], in_=I1u[:]), reads=["g_I1u"], writes=["g_I1f"])
        P.op("dve", lambda e: e.tensor_copy(out=I1b[:], in_=I1f[:].rearrange("p h a -> p (h a)")), reads=["g_I1f"], writes=["g_I1b"])
        P.op("pe", lambda e: e.transpose(out=ips[:], in_=I1b[:], identity=C.ident[:]), reads=["g_I1b", "ident"], writes=["g_ips"])
        P.op("act", lambda e: e.copy(out=I1T[t % 2][:], in_=ips[:]), reads=["g_ips"], writes=["g_I1T%d" % (t % 2)])
        for h in range(8):
            P.op("dve", lambda e, h=h: e.tensor_tensor(out=cand[:].rearrange("p (a b) -> p a b", a=16),
                                                       in0=V16[:, 2 * h, :].unsqueeze(2).to_broadcast([128, 16, 16]),
                                                       in1=V16[:, 2 * h + 1, :].unsqueeze(1).to_broadcast([128, 16, 16]), op=ALU.add),
                 reads=["g_V16"], writes=["g_cand"])
            P.op("dve", lambda e, h=h: e.max(out=T16[:, h, 0:8], in_=cand[:]), reads=["g_cand"], writes=["g_T16"])
            P.op("dve", lambda e, h=h: e.match_replace(out=wk[:], in_to_replace=T16[:, h, 0:8], in_values=cand[:], imm_value=-1e30),
                 reads=["g_cand", "g_T16"], writes=["g_wk"])
            P.op("dve", lambda e, h=h: e.max(out=T16[:, h, 8:16], in_=wk[:]), reads=["g_wk"], writes=["g_T16"])
        P.op("dve", lambda e: e.tensor_scalar(out=neg[:], in0=V16[:, :, 0], scalar1=-1.0, scalar2=None, op0=ALU.mult),
             reads=["g_V16"], writes=["g_neg"])
        P.op("dve", lambda e: e.tensor_scalar(out=negmx[:], in0=T16[:, :, 0], scalar1=-1.0, scalar2=None, op0=ALU.mult),
             reads=["g_T16"], writes=["g_negmx"])
        P.op("dve", lambda e: e.memset(Z[:], 0.0), writes=["g_Z"])
        for h in range(8):
            P.op("act", lambda e, h=h: e.activation(out=e1[:, h, :], in_=V16[:, 2 * h, :], func=AF.Exp, bias=neg[:, 2 * h:2 * h + 1], scale=1.0),
                 reads=["g_V16", "g_neg"], writes=["g_e1"])
            P.op("act", lambda e, h=h: e.activation(out=e2[:, h, :], in_=S_sb[:, 2 * h + 1, :], func=AF.Exp, bias=neg[:, 2 * h + 1:2 * h + 2], scale=1.0),
                 reads=["g_S", "g_neg"], writes=["g_e2"])
            P.op("act", lambda e, h=h: e.activation(out=junk[:], in_=T16[:, h, :], func=AF.Exp, bias=negmx[:, h:h + 1], scale=1.0,
                                                    accum_out=Z[:, h:h + 1]),
                 reads=["g_T16", "g_negmx", "g_Z"], writes=["g_junk", "g_Z"])
        P.op("dve", lambda e: e.reciprocal(out=rZ[:], in_=Z[:]), reads=["g_Z"], writes=["g_rZ"])
        P.op("dve", lambda e: e.tensor_tensor(out=cc[:], in0=e1[:], in1=rZ[:].unsqueeze(2).to_broadcast([128, 8, 16]), op=ALU.mult),
             reads=["g_e1", "g_rZ"], writes=["g_cc"])
        items = [(h, ah) for h in range(8) for ah in range(2)]

        def op1(k):
            h, ah = items[k]
            bi = k % 2
            a0 = ah * 8
            P.op("pool", lambda e: e.tensor_tensor(out=tmp[bi][:], in0=S_sb[:, 2 * h + 1, :].unsqueeze(1).to_broadcast([128, 8, 128]),
                                                   in1=V16[:, 2 * h, a0:a0 + 8].unsqueeze(2).to_broadcast([128, 8, 128]), op=ALU.add),
                 reads=["g_S", "g_V16"], writes=["g_tmp%d" % bi])

        op1(0)
        for k in range(16):
            h, ah = items[k]
            bi = k % 2
            a0 = ah * 8
            P.op("dve", lambda e, h=h, bi=bi: e.scalar_tensor_tensor(out=tmp2[bi][:], in0=tmp[bi][:], scalar=T16[:, h, 15:16],
                                                                     in1=e2[:, h, :].unsqueeze(1).to_broadcast([128, 8, 128]),
                                                                     op0=ALU.is_ge, op1=ALU.mult),
                 reads=["g_tmp%d" % bi, "g_T16", "g_e2"], writes=["g_tmp2%d" % bi])
            if k + 1 < 16:
                op1(k + 1)
            P.op("pool", lambda e, h=h, a0=a0, bi=bi: e.tensor_tensor(out=R[ri][:, h * 16 + a0:h * 16 + a0 + 8, :], in0=tmp2[bi][:],
                                                                      in1=cc[:, h, a0:a0 + 8].unsqueeze(2).to_broadcast([128, 8, 128]), op=ALU.mult),
                 reads=["g_tmp2%d" % bi, "g_cc"], writes=[rk])

    def stage2a(t):
        ri = t % 2
        rk = "g_R%d" % ri
        for jg in range(16):
            ti = jg % 2
            for jj in range(8):
                j = jg * 8 + jj
                P.op("pe", lambda e, j=j, jj=jj, ti=ti: e.transpose(out=tps[ti][:, jj * 128:(jj + 1) * 128], in_=R[ri][:, :, j], identity=C.ident[:]),
                     reads=[rk, "ident"], writes=["g_tps%d" % ti])
            P.op("act", lambda e, jg=jg, ti=ti: e.copy(out=RT2[:, :, jg * 8:(jg + 1) * 8], in_=tps[ti][:].rearrange("p (j t) -> p t j", j=8)),
                 reads=["g_tps%d" % ti], writes=["g_RT2"])

    def stage2b(t):
        ik = "g_I1T%d" % (t % 2)
        for th in range(2):
            P.op("dve", lambda e, th=th: e.tensor_tensor(out=A[:], in0=C.iota[:].unsqueeze(1).to_broadcast([128, 64, 128]),
                                                         in1=I1T[t % 2][:, th * 64:(th + 1) * 64].unsqueeze(2).to_broadcast([128, 64, 128]), op=ALU.is_equal),
                 reads=["iota", ik], writes=["g_A"])
            for tg in range(16):
                gi = tg % 2
                for t4 in range(4):
                    tl = tg * 4 + t4
                    tt = th * 64 + tl
                    P.op("pe", lambda e, tt=tt, tl=tl, t4=t4, gi=gi: e.matmul(gps[gi][:, t4 * 128:(t4 + 1) * 128], lhsT=A[:, tl, :], rhs=RT2[:, tt, :],
                                                                              start=True, stop=True),
                         reads=["g_A", "g_RT2"], writes=["g_gps%d" % gi])
                t0 = th * 64 + tg * 4
                P.op("act", lambda e, t0=t0, gi=gi: e.copy(out=GT[:, :, t0:t0 + 4], in_=gps[gi][:].rearrange("p (t j) -> p j t", t=4)),
                     reads=["g_gps%d" % gi], writes=["g_GT"])
        dma(P, "sp", GT_d[t], GT[:], ["g_GT"], [("GT", t)], "g_gst")

    stage1(0)
    for t in range(NT):
        stage2a(t)
        if t + 1 < NT:
            stage1(t + 1)
        stage2b(t)
    P.release(m0)


def peer_main(P, C, xT_d, UTb_d, Vb_d, GT_d, NT, out_cb, TBT=4, CG=4):
    m = P.mark()
    TB = TBT * 128
    NG = 128 // CG
    xt = P.sb("m_xt", [128, 16, TB], BF16)
    Y = [P.sb("m_Y%d" % i, [128, D], F32) for i in range(TBT)]
    ut = [P.sb("m_ut%d" % i, [128, 16, CG * 128], BF16) for i in range(2)]
    vt = [P.sb("m_vt%d" % i, [128, CG, D], BF16) for i in range(2)]
    gt = [P.sb("m_gt%d" % i, [128, CG, TB], BF16) for i in range(2)]
    hh = [P.sb("m_hh%d" % i, [128, TB], F32) for i in range(2)]
    gh = [P.sb("m_gh%d" % i, [128, CG, TB], BF16) for i in range(2)]
    hps = [P.ps("m_hps%d" % i, [128, 512], F32) for i in range(2)]
    yps = [P.ps("m_yps%d" % i, [128, 1024], F32) for i in range(2)]
    L = C.ln
    ntb = NT // TBT
    gcnt = 0
    ycnt = 0
    hcnt = 0
    for tb in range(ntb):
        dma(P, "act", xt[:], xT_d[:, tb * TB:(tb + 1) * TB].rearrange("(c p) t -> p c t", p=128),
            [("xT", tt) for tt in range(tb * TBT, (tb + 1) * TBT)], ["m_xt"], "m_xld")
        for g in range(NG):
            bi = gcnt % 2
            gcnt += 1
            j0 = g * CG
            dma(P, "sp", ut[bi][:].rearrange("p c (j i) -> p c j i", j=CG),
                UTb_d[:, j0:j0 + CG, :].rearrange("(c p) j i -> p c j i", p=128), ["UTb"], ["m_ut%d" % bi], "m_uld%d" % bi)
            dma(P, "pool", vt[bi][:], Vb_d[j0:j0 + CG, :, :].rearrange("j i d -> i j d"), ["Vb"], ["m_vt%d" % bi], "m_vld%d" % bi)
            for ti in range(TBT):
                dma(P, "sp", gt[bi][:, :, ti * 128:(ti + 1) * 128], GT_d[tb * TBT + ti][:, j0:j0 + CG, :],
                    [("GT", tb * TBT + ti)], ["m_gt%d" % bi], "m_gld%d" % bi)
            for cj in range(CG):
                hi = hcnt % 2
                hcnt += 1
                for k in range(16):
                    P.op("pe", lambda e, k=k, bi=bi, cj=cj, hi=hi: e.matmul(hps[hi][:, 0:TB], lhsT=ut[bi][:, k, cj * 128:(cj + 1) * 128], rhs=xt[:, k, :],
                                                                            start=(k == 0), stop=(k == 15)),
                         reads=["m_ut%d" % bi, "m_xt"], writes=["m_hps%d" % hi])
                P.op("act", lambda e, hi=hi: e.activation(out=hh[hi][:], in_=hps[hi][:, 0:TB], func=AF.Gelu_apprx_tanh),
                     reads=["m_hps%d" % hi], writes=["m_hh%d" % hi])
                P.op("dve", lambda e, hi=hi, bi=bi, cj=cj: e.tensor_tensor(out=gh[bi][:, cj, :], in0=hh[hi][:], in1=gt[bi][:, cj, :], op=ALU.mult),
                     reads=["m_hh%d" % hi, "m_gt%d" % bi], writes=["m_gh%d" % bi])
            for ti in range(TBT):
                for dh in range(2):
                    yi = ycnt % 2
                    ycnt += 1
                    for cj in range(CG):
                        for db in range(2):
                            P.op("pe", lambda e, cj=cj, db=db, dh=dh, ti=ti, bi=bi, yi=yi: e.matmul(
                                yps[yi][:, db * 512:(db + 1) * 512], lhsT=gh[bi][:, cj, ti * 128:(ti + 1) * 128],
                                rhs=vt[bi][:, cj, dh * 1024 + db * 512: dh * 1024 + (db + 1) * 512],
                                start=(cj == 0), stop=(cj == CG - 1)),
                                reads=["m_gh%d" % bi, "m_vt%d" % bi], writes=["m_yps%d" % yi])
                    if g == 0:
                        P.op("act", lambda e, ti=ti, dh=dh, yi=yi: e.copy(out=Y[ti][:, dh * 1024:(dh + 1) * 1024], in_=yps[yi][:]),
                             reads=["m_yps%d" % yi], writes=["m_Y%d" % ti])
                    else:
                        eng = "pool_no"
                        P.op("dve", lambda e, ti=ti, dh=dh, yi=yi: e.tensor_tensor(out=Y[ti][:, dh * 1024:(dh + 1) * 1024],
                                                                                  in0=Y[ti][:, dh * 1024:(dh + 1) * 1024], in1=yps[yi][:], op=ALU.add),
                             reads=["m_yps%d" % yi, "m_Y%d" % ti], writes=["m_Y%d" % ti])
        for ti in range(TBT):
            out_cb(tb * TBT + ti, Y[ti][:], "m_Y%d" % ti)
    P.release(m)


def ln_pass(P, C, mix_d, xres_in_d, xres_out_d, xT_d, g_d, b_d, NT, pfx):
    m = P.mark()
    L = ln_alloc(P, C, pfx)
    ln_load_params(P, L, g_d, b_d)
    mx = [P.sb(pfx + "mx%d" % i, [128, D], F32) for i in range(2)]
    for t in range(NT):
        i = t % 2
        dma(P, "act", mx[i][:], mix_d[t * 128:(t + 1) * 128, :], [("mix", t)], [pfx + "mx%d" % i], pfx + "mld%d" % i)
        ln_tile(P, C, L, mx[i][:], [pfx + "mx%d" % i], xres_in_d, xres_out_d, xT_d, t)
    P.release(m)


def make_store_epi_tok(P, dst_d, col0, ncol, dt, pfx, wkey, func=None):
    bufs = [P.sb(pfx + "eo%d" % i, [128, ncol], dt) for i in range(2)]
    cnt = [0]

    def epi(t, cb, ps, pkey):
        i = cnt[0] % 2
        cnt[0] += 1
        if func is None:
            P.op("act", lambda e: e.copy(out=bufs[i][:], in_=ps), reads=[pkey], writes=[pfx + "eo%d" % i])
        else:
            P.op("act", lambda e: e.activation(out=bufs[i][:], in_=ps, func=func), reads=[pkey], writes=[pfx + "eo%d" % i])
        dma(P, "sp", dst_d[t * 128:(t + 1) * 128, col0 + cb * ncol: col0 + (cb + 1) * ncol], bufs[i][:],
            [pfx + "eo%d" % i], [(wkey, t)], pfx + "est%d" % i)
    return epi


def make_store_epi_feat(P, dst_d, row0, pfx, wkey, TB=512):
    bufs = [P.sb(pfx + "fo%d" % i, [128, TB], BF16) for i in range(2)]
    cnt = [0]

    def epi(tb, cb, j, ps, pkey):
        i = cnt[0] % 2
        cnt[0] += 1
        P.op("act", lambda e: e.copy(out=bufs[i][:], in_=ps), reads=[pkey], writes=[pfx + "fo%d" % i])
        c = cb * 4 + j
        dma(P, "sp", dst_d[row0 + c * 128: row0 + (c + 1) * 128, tb * TB:(tb + 1) * TB], bufs[i][:],
            [pfx + "fo%d" % i], [(wkey, tt) for tt in range(tb * TB // 128, (tb + 1) * TB // 128)], pfx + "fst%d" % i)
    return epi


def retention_layer(P, C, T, l, xres_in_d, xres_out_d, NT):
    S = NT * 128
    w_in = T.ret_w_in[l]
    m = P.mark()
    tab = [P.sb("ra_tab%d" % i, [128, 4, 512], F32) for i in range(2)]
    t1 = P.sb("ra_t1", [128, 512], F32)
    t2 = P.sb("ra_t2", [128, 512], F32)
    t3 = P.sb("ra_t3", [128, 512], F32)
    t4 = P.sb("ra_t4", [128, 512], F32)
    o1 = [P.sb("ra_o1%d" % i, [128, 512], BF16) for i in range(2)]
    o2 = [P.sb("ra_o2%d" % i, [128, 512], BF16) for i in range(2)]
    kdec = P.sb("ra_kdec", [128, 8], F32)
    ktp = P.ps("ra_ktp", [128, 1024], BF16)
    kto = [P.sb("ra_kto%d" % i, [128, 4, 256], BF16) for i in range(2)]
    dma(P, "sp", kdec[:], T.kdec, [], ["ra_kdec"], "ra_kdld")
    st = {"prev": None, "cnt": 0, "tb": -1, "tabi": 0, "kcnt": 0}

    def qk_epi(tb, cb, j, ps, pkey):
        if j % 2 == 0:
            st["prev"] = (ps, pkey)
            return
        A, akey = st["prev"]
        B, bkey = ps, pkey
        isq = cb < 4
        ti = (tb % 2)
        if st["tb"] != (cb, tb):
            st["tb"] = (cb, tb)
            dma(P, "sp", tab[ti][:], T.rope[:, :, tb * 512:(tb + 1) * 512].rearrange("f p t -> p f t"), [], ["ra_tab%d" % ti], "ra_tld%d" % ti)
        cs = tab[ti][:, 2 if isq else 0, :]
        sn = tab[ti][:, 3 if isq else 1, :]
        i = st["cnt"] % 2
        st["cnt"] += 1
        P.op("dve", lambda e: e.tensor_tensor(out=t1[:], in0=A, in1=cs, op=ALU.mult), reads=[akey, "ra_tab%d" % ti], writes=["ra_t1"])
        P.op("dve", lambda e: e.tensor_tensor(out=t2[:], in0=B, in1=sn, op=ALU.mult), reads=[bkey, "ra_tab%d" % ti], writes=["ra_t2"])
        P.op("dve", lambda e: e.tensor_tensor(out=t3[:], in0=A, in1=sn, op=ALU.mult), reads=[akey, "ra_tab%d" % ti], writes=["ra_t3"])
        P.op("dve", lambda e: e.tensor_tensor(out=t4[:], in0=B, in1=cs, op=ALU.mult), reads=[bkey, "ra_tab%d" % ti], writes=["ra_t4"])
        P.op("pool", lambda e: e.tensor_tensor(out=o1[i][:], in0=t1[:], in1=t2[:], op=ALU.subtract), reads=["ra_t1", "ra_t2"], writes=["ra_o1%d" % i])
        P.op("pool", lambda e: e.tensor_tensor(out=o2[i][:], in0=t3[:], in1=t4[:], op=ALU.add), reads=["ra_t3", "ra_t4"], writes=["ra_o2%d" % i])
        c = (cb * 4 + j - 1) % 16
        dst = T.qT if isq else T.kT
        wk = "qT" if isq else "kT"
        tts = range(tb * 4, tb * 4 + 4)
        dma(P, "sp", dst[c * 128:(c + 1) * 128, tb * 512:(tb + 1) * 512], o1[i][:], ["ra_o1%d" % i], [(wk, tt) for tt in tts], "ra_s1%d" % i)
        dma(P, "sp", dst[(c + 1) * 128:(c + 2) * 128, tb * 512:(tb + 1) * 512], o2[i][:], ["ra_o2%d" % i], [(wk, tt) for tt in tts], "ra_s2%d" % i)
        if not isq:
            h = c // 2
            ki = st["kcnt"] % 2
            st["kcnt"] += 1
            for half, ob in enumerate((o1[i], o2[i])):
                for q in range(4):
                    P.op("pe", lambda e, ob=ob, q=q, half=half: e.transpose(out=ktp[:, (half * 4 + q) * 128:(half * 4 + q + 1) * 128],
                                                                            in_=ob[:, q * 128:(q + 1) * 128], identity=C.ident[:]),
                         reads=["ra_o1%d" % i, "ra_o2%d" % i, "ident"], writes=["ra_ktp"])
            P.op("dve", lambda e, ki=ki, h=h: e.tensor_scalar(out=kto[ki][:].rearrange("p q (a f) -> p a q f", a=2),
                                                              in0=ktp[:].rearrange("p (a q f) -> p a q f", a=2, q=4),
                                                              scalar1=kdec[:, h:h + 1], scalar2=None, op0=ALU.mult),
                 reads=["ra_ktp", "ra_kdec"], writes=["ra_kto%d" % ki])
            dma(P, "sp", T.kd[tb * 512:(tb + 1) * 512, h * 256:(h + 1) * 256].rearrange("(q p) f -> p q f", p=128), kto[ki][:],
                ["ra_kto%d" % ki], [("kd", tt) for tt in tts], "ra_ks%d" % ki)

    linear_phase(P, C, T.xT, 2048, S, w_in, 0, 4096, "feat", qk_epi, "lqk_")
    P.release(m)
    m = P.mark()
    epi_v = make_store_epi_tok(P, T.v, 0, 512, BF16, "rbv_", "v")
    linear_phase(P, C, T.xT, 2048, S, w_in, 4096, 4096, "tok", epi_v, "lv_")
    P.release(m)
    m = P.mark()
    epi_g = make_store_epi_tok(P, T.sg, 0, 512, F32, "rbg_", "sg", func=AF.Silu)
    linear_phase(P, C, T.xT, 2048, S, w_in, 8192, 4096, "tok", epi_g, "lg_")
    P.release(m)
    m = P.mark()
    qt = [P.sb("rc_qt%d" % i, [128, 16, 128], BF16) for i in range(2)]
    kt = [P.sb("rc_kt%d" % i, [128, 16, 128], BF16) for i in range(2)]
    kdt = [P.sb("rc_kd%d" % i, [128, 2048], BF16) for i in range(2)]
    vt = [P.sb("rc_v%d" % i, [128, 4096], BF16) for i in range(2)]
    sgt = [P.sb("rc_sg%d" % i, [128, 4096], F32) for i in range(2)]
    Sf = P.sb("rc_Sf", [128, 16, 512], F32)
    Sb = P.sb("rc_Sb", [128, 16, 512], BF16)
    MT = P.sb("rc_MT", [128, 8, 128], F32)
    qdec = P.sb("rc_qdec", [128, 8, 128], F32)
    gng = P.sb("rc_gng", [128, 4096], F32)
    PT = [P.sb("rc_PT%d" % i, [128, 128], BF16) for i in range(2)]
    qd = [P.sb("rc_qd%d" % i, [128, 2, 128], BF16) for i in range(2)]
    on = [P.sb("rc_on%d" % i, [128, 512], F32) for i in range(2)]
    gb = [P.sb("rc_gb%d" % i, [128, 512], BF16) for i in range(2)]
    gto = [P.sb("rc_gto%d" % i, [128, 4, 128], BF16) for i in range(2)]
    stt = P.sb("rc_st", [128, 6], F32)
    mv = P.sb("rc_mv", [128, 4], F32)
    sc_ps = P.ps("rc_scps", [128, 128], F32)
    o_ps = [P.ps("rc_ops%d" % i, [128, 512], F32) for i in range(2)]
    s_ps = [P.ps("rc_sps%d" % i, [128, 512], F32) for i in range(2)]
    g_tp = P.ps("rc_gtp", [128, 512], BF16)
    dma(P, "sp", MT[:], T.retMT, [], ["rc_MT"], "rc_mld")
    dma(P, "sp", qdec[:], T.qdec, [], ["rc_qdec"], "rc_qdld")
    dma(P, "sp", gng[:], T.ret_gn_g[l].partition_broadcast(128), [], ["rc_gng"], "rc_gld")
    P.op("pool", lambda e: e.memset(Sf[:], 0.0), writes=[("Sf", hh, dd) for hh in range(8) for dd in range(2)])
    P.op("pool", lambda e: e.memset(Sb[:], 0.0), writes=[("Sb", hh) for hh in range(8)])
    hc = 0
    for t in range(NT):
        i = t % 2
        dma(P, "sp", qt[i][:], T.qT[:, t * 128:(t + 1) * 128].rearrange("(c p) t -> p c t", p=128), [("qT", t)], ["rc_qt%d" % i], "rc_l1%d" % i)
        dma(P, "sp", kt[i][:], T.kT[:, t * 128:(t + 1) * 128].rearrange("(c p) t -> p c t", p=128), [("kT", t)], ["rc_kt%d" % i], "rc_l2%d" % i)
        dma(P, "act", kdt[i][:], T.kd[t * 128:(t + 1) * 128, :], [("kd", t)], ["rc_kd%d" % i], "rc_l3%d" % i)
        dma(P, "act", vt[i][:], T.v[t * 128:(t + 1) * 128, :], [("v", t)], ["rc_v%d" % i], "rc_l4%d" % i)
        dma(P, "act", sgt[i][:], T.sg[t * 128:(t + 1) * 128, :], [("sg", t)], ["rc_sg%d" % i], "rc_l5%d" % i)
        for h in range(8):
            b = hc % 2
            hc += 1
            for dc in range(2):
                P.op("pe", lambda e, h=h, dc=dc, i=i: e.matmul(sc_ps[:], lhsT=kt[i][:, 2 * h + dc, :], rhs=qt[i][:, 2 * h + dc, :],
                                                               start=(dc == 0), stop=(dc == 1)),
                     reads=["rc_kt%d" % i, "rc_qt%d" % i], writes=["rc_scps"])
            P.op("dve", lambda e, h=h, b=b: e.tensor_tensor(out=PT[b][:], in0=sc_ps[:], in1=MT[:, h, :], op=ALU.mult),
                 reads=["rc_scps", "rc_MT"], writes=["rc_PT%d" % b])
            P.op("pool", lambda e, h=h, b=b, i=i: e.tensor_tensor(out=qd[b][:], in0=qt[i][:, 2 * h:2 * h + 2, :],
                                                                  in1=qdec[:, h, :].unsqueeze(1).to_broadcast([128, 2, 128]), op=ALU.mult),
                 reads=["rc_qt%d" % i, "rc_qdec"], writes=["rc_qd%d" % b])
            P.op("pe", lambda e, h=h, b=b, i=i: e.matmul(o_ps[b][:], lhsT=PT[b][:], rhs=vt[i][:, h * 512:(h + 1) * 512], start=True, stop=False),
                 reads=["rc_PT%d" % b, "rc_v%d" % i], writes=["rc_ops%d" % b])
            for dc in range(2):
                P.op("pe", lambda e, h=h, b=b, dc=dc: e.matmul(o_ps[b][:], lhsT=qd[b][:, dc, :], rhs=Sb[:, 2 * h + dc, :], start=False, stop=(dc == 1)),
                     reads=["rc_qd%d" % b, ("Sb", h)], writes=["rc_ops%d" % b])
            for dc in range(2):
                P.op("pe", lambda e, h=h, dc=dc, i=i: e.matmul(s_ps[dc][:], lhsT=kdt[i][:, h * 256 + dc * 128: h * 256 + (dc + 1) * 128],
                                                               rhs=vt[i][:, h * 512:(h + 1) * 512], start=True, stop=True),
                     reads=["rc_kd%d" % i, "rc_v%d" % i], writes=["rc_sps%d" % dc])
                P.op("dve", lambda e, h=h, dc=dc: e.scalar_tensor_tensor(out=Sf[:, 2 * h + dc, :], in0=Sf[:, 2 * h + dc, :], scalar=float(T.gamma128[h]),
                                                                         in1=s_ps[dc][:], op0=ALU.mult, op1=ALU.add),
                     reads=["rc_sps%d" % dc, ("Sf", h, dc)], writes=[("Sf", h, dc)])
                P.op("act", lambda e, h=h, dc=dc: e.copy(out=Sb[:, 2 * h + dc, :], in_=Sf[:, 2 * h + dc, :]),
                     reads=[("Sf", h, dc)], writes=[("Sb", h)])
            P.op("dve", lambda e, b=b: e.bn_stats(out=stt[:], in_=o_ps[b][:]), reads=["rc_ops%d" % b], writes=["rc_st"])
            P.op("dve", lambda e: e.bn_aggr(out=mv[:, 0:2], in_=stt[:]), reads=["rc_st"], writes=["rc_mv"])
            P.op("dve", lambda e: e.tensor_scalar(out=mv[:, 2:3], in0=mv[:, 1:2], scalar1=LN_EPS, scalar2=None, op0=ALU.add),
                 reads=["rc_mv"], writes=["rc_mv"])
            P.op("act", lambda e: e.activation(out=mv[:, 3:4], in_=mv[:, 2:3], func=AF.Sqrt), reads=["rc_mv"], writes=["rc_mv"])
            P.op("dve", lambda e: e.reciprocal(out=mv[:, 2:3], in_=mv[:, 3:4]), reads=["rc_mv"], writes=["rc_mv"])
            P.op("dve", lambda e, b=b: e.tensor_scalar(out=on[b][:], in0=o_ps[b][:], scalar1=mv[:, 0:1], scalar2=mv[:, 2:3],
                                                       op0=ALU.subtract, op1=ALU.mult),
                 reads=["rc_ops%d" % b, "rc_mv"], writes=["rc_on%d" % b])
            P.op("pool", lambda e, b=b, h=h: e.tensor_tensor(out=on[b][:], in0=on[b][:], in1=gng[:, h * 512:(h + 1) * 512], op=ALU.mult),
                 reads=["rc_on%d" % b, "rc_gng"], writes=["rc_on%d" % b])
            P.op("pool", lambda e, b=b, h=h, i=i: e.tensor_tensor(out=gb[b][:], in0=on[b][:], in1=sgt[i][:, h * 512:(h + 1) * 512], op=ALU.mult),
                 reads=["rc_on%d" % b, "rc_sg%d" % i], writes=["rc_gb%d" % b])
            for q in range(4):
                P.op("pe", lambda e, b=b, q=q: e.transpose(out=g_tp[:, q * 128:(q + 1) * 128], in_=gb[b][:, q * 128:(q + 1) * 128], identity=C.ident[:]),
                     reads=["rc_gb%d" % b, "ident"], writes=["rc_gtp"])
            P.op("act", lambda e, b=b: e.copy(out=gto[b][:], in_=g_tp[:].rearrange("p (q t) -> p q t", q=4)), reads=["rc_gtp"], writes=["rc_gto%d" % b])
            dma(P, "sp", T.gT[h * 512:(h + 1) * 512, t * 128:(t + 1) * 128].rearrange("(q p) t -> p q t", p=128), gto[b][:],
                ["rc_gto%d" % b], ["wo_actsrc"], "rc_gs%d" % b)
    P.release(m)
    m = P.mark()
    epi_o = make_store_epi_tok(P, T.mix, 0, 256, F32, "rdo_", "mix")
    linear_phase(P, C, T.gT, 4096, S, T.ret_w_out[l], 0, 2048, "tok", epi_o, "wo_", ncol=256)
    P.release(m)
    ln_pass(P, C, T.mix, xres_in_d, xres_out_d, T.xT, T.ln_g[l][0:1, :], T.ln_b[l][0:1, :], NT, "rln_")


def shared_kv_phase(P, C, T, NT):
    S = NT * 128
    m = P.mark()
    epi_k = make_store_epi_feat(P, T.KT, 0, "kvk_", "KT")
    linear_phase(P, C, T.xT, 2048, S, T.kv_w, 0, 2048, "feat", epi_k, "lkk_")
    P.release(m)
    m = P.mark()
    epi_v = make_store_epi_tok(P, T.Vs, 0, 512, BF16, "kvv_", "Vs")
    linear_phase(P, C, T.xT, 2048, S, T.kv_w, 2048, 2048, "tok", epi_v, "lkv_")
    P.release(m)


def sb_layer(P, C, T, l, xres_in_d, xres_out_d, NT):
    S = NT * 128
    lb = l - 2
    scale = 128.0 ** -0.5
    m = P.mark()
    epi_q = make_store_epi_feat(P, T.QT, 0, "sq_", "QT")
    linear_phase(P, C, T.xT, 2048, S, T.sb_wq[lb], 0, 2048, "feat", epi_q, "lsq_")
    P.release(m)
    m = P.mark()
    NQB = S // 512
    Kh = [P.sb("sa_K%d" % i, [128, S], BF16) for i in range(2)]
    Qh = [P.sb("sa_Q%d" % i, [128, S], BF16) for i in range(2)]
    Vh = [P.sb("sa_V%d" % i, [128, NT, 128], BF16) for i in range(2)]
    cm = P.sb("sa_cm", [128, 4, 512], F32)
    trif = P.sb("sa_trif", [128, 128], F32)
    tri = P.sb("sa_tri", [128, 128], BF16)
    ones = P.sb("sa_ones", [128, 128], BF16)
    NS = 2
    mk = lambda nm, dt: [[P.sb("sa_%s%d_%d" % (nm, i, p), [128, 512], dt) for p in range(2)] for i in range(NS)]
    ex = mk("ex", F32)
    sp = mk("sp", F32)
    spm = mk("spm", F32)
    shi = mk("shi", BF16)
    slo = mk("slo", BF16)
    u = mk("u", F32)
    lw = mk("lw", F32)
    ew = mk("ew", F32)
    w = mk("w", BF16)
    spsum = [P.sb("sa_spsum%d" % i, [128, 512], F32) for i in range(NS)]
    sshi = [P.sb("sa_sshi%d" % i, [128, 512], BF16) for i in range(NS)]
    sslo = [P.sb("sa_sslo%d" % i, [128, 512], BF16) for i in range(NS)]
    oo = [P.sb("sa_oo%d" % i, [128, 512], BF16) for i in range(NS)]
    z_ps = [[P.ps("sa_zps%d_%d" % (i, p), [128, 512], F32) for p in range(2)] for i in range(NS)]
    t_ps = [P.ps("sa_tps%d" % i, [128, 512], F32) for i in range(NS)]
    o_ps = [P.ps("sa_ops%d" % i, [128, 512], F32) for i in range(NS)]
    dma(P, "sp", cm[:], T.cmask, [], ["sa_cm"], "sa_cmld")
    P.op("pool", lambda e: e.memset(trif[:], 1.0), writes=["sa_trif"])
    P.op("pool", lambda e: e.affine_select(out=trif[:], in_=trif[:], pattern=[[-1, 128]], compare_op=ALU.is_gt, fill=0.0, base=0, channel_multiplier=1),
         reads=["sa_trif"], writes=["sa_trif"])
    P.op("dve", lambda e: e.tensor_copy(out=tri[:], in_=trif[:]), reads=["sa_trif"], writes=["sa_tri"])
    P.op("pool", lambda e: e.memset(ones[:], 1.0), writes=["sa_ones"])

    def chain(si, p, hi, h, qb, ai, a, na):
        k = "%d_%d" % (si, p)
        ks = "%d" % si
        diag = a >= qb * 4
        first = (ai == 0)
        last = (ai == na - 1)
        P.op("pe", lambda e: e.matmul(z_ps[si][p][:], lhsT=Kh[hi][:, a * 128:(a + 1) * 128], rhs=Qh[hi][:, qb * 512:(qb + 1) * 512], start=True, stop=True),
             reads=["sa_K%d" % hi, "sa_Q%d" % hi], writes=["sa_zps" + k])
        yield
        P.op("act", lambda e: e.activation(out=ex[si][p][:], in_=z_ps[si][p][:], func=AF.Exp, scale=scale), reads=["sa_zps" + k], writes=["sa_ex" + k])
        yield
        P.op("act", lambda e: e.activation(out=sp[si][p][:], in_=ex[si][p][:], func=AF.Ln, bias=1.0, scale=1.0), reads=["sa_ex" + k], writes=["sa_sp" + k])
        yield
        if diag:
            P.op("pool", lambda e: e.tensor_tensor(out=spm[si][p][:], in0=sp[si][p][:], in1=cm[:, a - qb * 4, :], op=ALU.mult),
                 reads=["sa_sp" + k, "sa_cm"], writes=["sa_spm" + k])
            src, skey = spm[si][p], "sa_spm" + k
        else:
            src, skey = sp[si][p], "sa_sp" + k
        yield
        P.op("pool", lambda e: e.tensor_copy(out=shi[si][p][:], in_=src[:]), reads=[skey], writes=["sa_shi" + k])
        yield
        P.op("dve", lambda e: e.tensor_tensor(out=slo[si][p][:], in0=src[:], in1=shi[si][p][:], op=ALU.subtract), reads=[skey, "sa_shi" + k], writes=["sa_slo" + k])
        yield
        P.op("pe", lambda e: e.matmul(t_ps[si][:], lhsT=tri[:], rhs=shi[si][p][:], start=True, stop=False), reads=["sa_tri", "sa_shi" + k], writes=["sa_tps" + ks])
        P.op("pe", lambda e: e.matmul(t_ps[si][:], lhsT=tri[:], rhs=slo[si][p][:], start=False, stop=first), reads=["sa_tri", "sa_slo" + k], writes=["sa_tps" + ks])
        if not first:
            P.op("pe", lambda e: e.matmul(t_ps[si][:], lhsT=ones[:], rhs=sshi[si][:], start=False, stop=False), reads=["sa_ones", "sa_sshi" + ks], writes=["sa_tps" + ks])
            P.op("pe", lambda e: e.matmul(t_ps[si][:], lhsT=ones[:], rhs=sslo[si][:], start=False, stop=True), reads=["sa_ones", "sa_sslo" + ks], writes=["sa_tps" + ks])
        yield
        if not last:
            if first:
                P.op("pool", lambda e: e.tensor_copy(out=spsum[si][:], in_=src[:]), reads=[skey], writes=["sa_spsum" + ks])
            else:
                P.op("pool", lambda e: e.tensor_tensor(out=spsum[si][:], in0=spsum[si][:], in1=src[:], op=ALU.add), reads=[skey, "sa_spsum" + ks], writes=["sa_spsum" + ks])
        yield
        if not last:
            P.op("act", lambda e: e.copy(out=sshi[si][:], in_=spsum[si][:]), reads=["sa_spsum" + ks], writes=["sa_sshi" + ks])
        yield
        if not last:
            P.op("dve", lambda e: e.tensor_tensor(out=sslo[si][:], in0=spsum[si][:], in1=sshi[si][:], op=ALU.subtract), reads=["sa_spsum" + ks, "sa_sshi" + ks], writes=["sa_sslo" + ks])
        yield
        P.op("dve", lambda e: e.tensor_tensor(out=u[si][p][:], in0=t_ps[si][:], in1=sp[si][p][:], op=ALU.add), reads=["sa_tps" + ks, "sa_sp" + k], writes=["sa_u" + k])
        yield
        P.op("dve", lambda e: e.scalar_tensor_tensor(out=lw[si][p][:], in0=z_ps[si][p][:], scalar=scale, in1=u[si][p][:], op0=ALU.mult, op1=ALU.subtract),
             reads=["sa_zps" + k, "sa_u" + k], writes=["sa_lw" + k])
        yield
        if diag:
            P.op("act", lambda e: e.activation(out=ew[si][p][:], in_=lw[si][p][:], func=AF.Exp), reads=["sa_lw" + k], writes=["sa_ew" + k])
            yield
            P.op("pool", lambda e: e.tensor_tensor(out=w[si][p][:], in0=ew[si][p][:], in1=cm[:, a - qb * 4, :], op=ALU.mult),
                 reads=["sa_ew" + k, "sa_cm"], writes=["sa_w" + k])
        else:
            P.op("act", lambda e: e.activation(out=w[si][p][:], in_=lw[si][p][:], func=AF.Exp), reads=["sa_lw" + k], writes=["sa_w" + k])
            yield
        yield
        P.op("pe", lambda e: e.matmul(o_ps[si][:], lhsT=Vh[hi][:, a, :], rhs=w[si][p][:], start=first, stop=last),
             reads=["sa_V%d" % hi, "sa_w" + k], writes=["sa_ops" + ks])
        if last:
            P.op("act", lambda e: e.copy(out=oo[si][:], in_=o_ps[si][:]), reads=["sa_ops" + ks], writes=["sa_oo" + ks])
            dma(P, "sp", T.oT[h * 128:(h + 1) * 128, qb * 512:(qb + 1) * 512], oo[si][:], ["sa_oo" + ks], ["wo_actsrc"], "sa_ost" + ks)
        yield

    STAG = 7
    order = sorted(range(NQB), key=lambda q: -q)
    assign = [[], []]
    load = [0, 0]
    for q in order:
        si = 0 if load[0] <= load[1] else 1
        assign[si].append(q)
        load[si] += q + 1
    tcount = [0, 0]
    for h in range(16):
        hi = h % 2
        dma(P, "sp", Kh[hi][:], T.KT[h * 128:(h + 1) * 128, :], [], ["sa_K%d" % hi], "sa_kld%d" % hi)
        dma(P, "sp", Qh[hi][:], T.QT[h * 128:(h + 1) * 128, :], [], ["sa_Q%d" % hi], "sa_qld%d" % hi)
        dma(P, "act", Vh[hi][:], T.Vs[:, h * 128:(h + 1) * 128].rearrange("(a p) d -> p a d", p=128), [], ["sa_V%d" % hi], "sa_vld%d" % hi)
        todo = []
        for si in range(NS):
            lst = []
            for q in assign[si]:
                na = (q + 1) * 4
                for ai, a in enumerate(range(na - 1, -1, -1)):
                    lst.append((q, ai, a, na))
            todo.append(lst)
        active = []
        newest = [None, None]
        while any(todo) or active:
            for si in range(NS):
                if todo[si] and (newest[si] is None or newest[si][2] >= STAG or newest[si][3]):
                    q, ai, a, na = todo[si].pop(0)
                    p = tcount[si] % 2
                    tcount[si] += 1
                    ent = [chain(si, p, hi, h, q, ai, a, na), si, 0, False]
                    active.append(ent)
                    newest[si] = ent
            for ent in list(active):
                try:
                    next(ent[0])
                    ent[2] += 1
                except StopIteration:
                    ent[3] = True
                    active.remove(ent)
    P.release(m)
    m = P.mark()
    epi_o = make_store_epi_tok(P, T.mix, 0, 512, F32, "sdo_", "mix")
    linear_phase(P, C, T.oT, 2048, S, T.sb_w_out[lb], 0, 2048, "tok", epi_o, "wo_")
    P.release(m)
    ln_pass(P, C, T.mix, xres_in_d, xres_out_d, T.xT, T.ln_g[l][0:1, :], T.ln_b[l][0:1, :], NT, "sln_")


from concourse.bass_utils import run_bass_kernel_spmd

N_CORES = 4
SEQ = 4096
RET_HEADS = 8


def _consts(S):
    pos = np.arange(S, dtype=np.float32)
    inv_freq = (10000.0 ** (-np.arange(0, 256, 2, dtype=np.float32) / 256)).astype(np.float32)
    ang = (pos[None, :] * inv_freq[:, None]).astype(np.float32)
    cos = np.cos(ang.astype(np.float64)).astype(np.float32)
    sin = np.sin(ang.astype(np.float64)).astype(np.float32)
    rope = np.stack([cos, sin, cos / 16.0, sin / 16.0]).astype(np.float32)
    lg = np.log(1.0 - np.exp2(-5.0 - np.arange(8, dtype=np.float64)))
    n = np.arange(128)
    ch = n // 64
    MT = np.zeros((128, 8, 128), np.float32)
    for h in range(8):
        dist = np.abs(n[:, None] - n[None, :]).astype(np.float64)
        Mh = np.where(ch[:, None] == ch[None, :], np.exp(lg[h] * dist),
                      np.where(ch[None, :] < ch[:, None], np.exp(lg[h] * (n[:, None] - n[None, :])), 0.0))
        MT[:, h, :] = Mh.T
    qdec = np.zeros((128, 8, 128), np.float32)
    kdec = np.zeros((128, 8), np.float32)
    for h in range(8):
        qdec[:, h, :] = np.exp(lg[h] * (n + 1.0))[None, :]
        kdec[:, h] = np.exp(lg[h] * (127.0 - n))
    gamma128 = [float(np.exp(lg[h] * 128.0)) for h in range(8)]
    s = np.arange(128)[:, None]
    t = np.arange(512)[None, :]
    cmask = np.stack([((k * 128 + s) < t).astype(np.float32) for k in range(4)], axis=1)
    return dict(rope=rope, retMT=MT, qdec=qdec, kdec=kdec, cmask=np.ascontiguousarray(cmask)), gamma128


def build_nc(S=SEQ, depth=4, n_a=2):
    NT = S // 128
    nc = bass.Bass("TRN2", target_bir_lowering=False)
    T = Ctx()

    def din(name, shape):
        return nc.dram_tensor(name, list(shape), F32, kind="ExternalInput").ap()

    def dsc(name, shape, dt):
        return nc.dram_tensor(name, list(shape), dt, kind="Internal").ap()

    x = din("x", [S, 2048])
    T.ret_w_in = din("ret_w_in", [2, 2048, 12288])
    T.ret_gn_g = din("ret_gn_g", [2, 4096])
    T.ret_gn_g = [T.ret_gn_g[i:i + 1, :] for i in range(2)]
    T.ret_w_out = din("ret_w_out", [2, 4096, 2048])
    T.kv_w = din("kv_w", [2048, 4096])
    T.sb_wq = din("sb_wq", [2, 2048, 2048])
    T.sb_w_out = din("sb_w_out", [2, 2048, 2048])
    T.peer_wq = din("peer_wq", [4, 2048, 2048])
    T.keysT = din("keysT", [4, 128, 16, 128])
    T.UTp = din("UTp", [4, 2048, 128, 128])
    T.Vp = din("Vp", [4, 128, 128, 2048])
    T.ln_g = din("ln_g", [4, 2, 2048])
    T.ln_b = din("ln_b", [4, 2, 2048])
    T.rope = din("rope", [4, 128, S])
    T.retMT = din("retMT", [128, 8, 128])
    T.qdec = din("qdec", [128, 8, 128])
    T.kdec = din("kdec", [128, 8])
    T.cmask = din("cmask", [128, 4, 512])
    out = nc.dram_tensor("out", [S, 2048], F32, kind="ExternalOutput").ap()
    _, T.gamma128 = _consts(128)
    xres = dsc("xres", [S, 2048], F32)
    T.xT = dsc("xT", [2048, S], BF16)
    T.qT = dsc("qT", [2048, S], BF16)
    T.kT = dsc("kT", [2048, S], BF16)
    T.kd = dsc("kd", [S, 2048], BF16)
    T.v = dsc("v", [S, 4096], BF16)
    T.sg = dsc("sg", [S, 4096], F32)
    T.gT = dsc("gT", [4096, S], BF16)
    T.mix = dsc("mix", [S, 2048], F32)
    T.KT = dsc("KT", [2048, S], BF16)
    T.Vs = dsc("Vs", [S, 2048], BF16)
    T.QT = dsc("QT", [2048, S], BF16)
    T.oT = dsc("oT", [2048, S], BF16)
    T.pq = dsc("pq", [2048, S], BF16)
    T.UTb = dsc("UTb", [2048, 128, 128], BF16)
    T.Vb = dsc("Vb", [128, 128, 2048], BF16)
    T.GTd = dsc("GTd", [NT, 128, 128, 128], BF16)
    P = Prog(nc)
    C = Ctx()
    make_ident(P, C)
    phase_x_to_xT(P, C, x, T.xT, NT)
    cur = x
    for l in range(depth):
        if l < n_a:
            retention_layer(P, C, T, l, cur, xres, NT)
        else:
            sb_layer(P, C, T, l, cur, xres, NT)
        cur = xres
        m = P.mark()
        epi_pq = make_store_epi_feat(P, T.pq, 0, "pq_", "pq")
        linear_phase(P, C, T.xT, 2048, S, T.peer_wq[l], 0, 2048, "feat", epi_pq, "lpq_")
        P.release(m)
        peer_precast(P, C, T.UTp[l].rearrange("(p a) j i -> p (a j i)", p=128), T.UTb.rearrange("(p a) j i -> p (a j i)", p=128), 16 * 128 * 128, "pcu_")
        peer_precast(P, C, T.Vp[l].rearrange("j i d -> j (i d)"), T.Vb.rearrange("j i d -> j (i d)"), 128 * 2048, "pcv_")
        peer_gbuild(P, C, T.pq, T.keysT[l], T.GTd, NT)
        m = P.mark()
        C.ln = ln_alloc(P, C, "pln_")
        ln_load_params(P, C.ln, T.ln_g[l][1:2, :], T.ln_b[l][1:2, :])
        last = (l == depth - 1)
        dst = out if last else xres

        def out_cb(t, yap, ykey, dst=dst, last=last):
            ln_tile(P, C, C.ln, yap, [ykey], xres, dst, None if last else T.xT, t)
        peer_main(P, C, T.xT, T.UTb, T.Vb, T.GTd, NT, out_cb)
        P.release(m)
        if l == n_a - 1 and depth > n_a:
            shared_kv_phase(P, C, T, NT)
    P.flush()
    P.close()
    return nc


_NC_CACHE = {}


def kernel(x, ret_w_in, ret_gn_g, ret_w_out, kv_w, sb_wq, sb_w_out, peer_wq, peer_sub_keys, peer_u, peer_v, ln_g, ln_b):
    f = lambda a: np.ascontiguousarray(np.asarray(a, dtype=np.float32))
    x = f(x)
    B, S, _ = x.shape
    consts, _ = _consts(S)
    sk = f(peer_sub_keys)
    keysT = np.ascontiguousarray(sk.reshape(4, 16, 128, 128).transpose(0, 3, 1, 2))
    pu = f(peer_u).reshape(4, 128, 128, 2048)
    UTp = np.ascontiguousarray(pu.transpose(0, 3, 2, 1))
    del pu
    pv = f(peer_v).reshape(4, 128, 128, 2048)
    Vp = np.ascontiguousarray(pv.transpose(0, 2, 1, 3))
    del pv
    shared = dict(ret_w_in=f(ret_w_in), ret_gn_g=f(ret_gn_g), ret_w_out=f(ret_w_out), kv_w=f(kv_w), sb_wq=f(sb_wq),
                  sb_w_out=f(sb_w_out), peer_wq=f(peer_wq), keysT=keysT, UTp=UTp, Vp=Vp, ln_g=f(ln_g), ln_b=f(ln_b), **consts)
    if S not in _NC_CACHE:
        _NC_CACHE[S] = build_nc(S)
    nc = _NC_CACHE[S]
    in_maps = [dict(shared, x=np.ascontiguousarray(x[b])) for b in range(B)]
    res = run_bass_kernel_spmd(nc, in_maps, core_ids=list(range(B)))
    return np.stack([np.asarray(r["out"], dtype=np.float32) for r in res.results], axis=0)
```

```python
import numpy as np
import concourse.bass as bass
import concourse.mybir as mybir

F32 = mybir.dt.float32
BF16 = mybir.dt.bfloat16
ALU = mybir.AluOpType
AF = mybir.ActivationFunctionType
AX = mybir.AxisListType

ENGS = ("pe", "act", "dve", "pool", "sp")


class Op:
    __slots__ = ("eng", "fn", "reads", "writes", "dma", "deps", "idx", "eidx",
                 "need_inc", "inc_val", "slot")

    def __init__(self, eng, fn, reads, writes, dma):
        self.eng = eng
        self.fn = fn
        self.reads = reads
        self.writes = writes
        self.dma = dma
        self.deps = []
        self.need_inc = False
        self.inc_val = None
        self.slot = None


class Prog:
    def __init__(self, nc):
        self.nc = nc
        self.ops = []
        self.last_w = {}
        self.readers = {}
        self.eng_sems = None
        self.eng_cnt = {e: 0 for e in ENGS}
        self.dma_sems = {}
        self.waited = {e: {} for e in ENGS}
        self.eng_nops = {e: 0 for e in ENGS}
        self._ctx = []

    def sem(self, name):
        cm = self.nc.semaphore(name)
        h = cm.__enter__()
        self._ctx.append((cm, "sem"))
        return h

    def sb(self, name, shape, dt):
        self._uid = getattr(self, "_uid", 0) + 1
        cm = self.nc.sbuf_tensor("%s_%d" % (name, self._uid), list(shape), dt)
        h = cm.__enter__()
        self._ctx.append((cm, "sb"))
        return h

    def ps(self, name, shape, dt):
        self._uid = getattr(self, "_uid", 0) + 1
        cm = self.nc.psum_tensor("%s_%d" % (name, self._uid), list(shape), dt)
        h = cm.__enter__()
        self._ctx.append((cm, "ps"))
        return h

    def close(self):
        for cm, kind in reversed(self._ctx):
            cm.__exit__(None, None, None)
        self._ctx = []

    def mark(self):
        return len(self._ctx)

    def release(self, mark):
        self.flush()
        keep = []
        tail = self._ctx[mark:]
        self._ctx = self._ctx[:mark]
        for cm, kind in reversed(tail):
            if kind == "sem":
                keep.append((cm, kind))
            else:
                cm.__exit__(None, None, None)
        self._ctx.extend(reversed(keep))

    def init_sems(self):
        self.eng_sems = {e: self.sem("s_" + e) for e in ENGS}

    def op(self, eng, fn, reads=(), writes=(), dma=None):
        o = Op(eng, fn, tuple(reads), tuple(writes), dma)
        o.idx = len(self.ops)
        o.eidx = self.eng_nops[eng]
        self.eng_nops[eng] += 1
        deps = set()
        for k in o.reads:
            w = self.last_w.get(k)
            if w is not None:
                deps.add(w)
        for k in o.writes:
            w = self.last_w.get(k)
            if w is not None:
                deps.add(w)
            for r in self.readers.get(k, ()):
                deps.add(r)
        deps.discard(o.idx)
        for k in o.reads:
            self.readers.setdefault(k, []).append(o.idx)
        for k in o.writes:
            self.last_w[k] = o.idx
            self.readers[k] = []
        o.deps = sorted(deps)
        self.ops.append(o)
        return o

    def flush(self):
        nc = self.nc
        ops = self.ops
        if not ops:
            return
        if self.eng_sems is None:
            self.init_sems()
        edges = {}
        for o in ops:
            need = []
            for d in o.deps:
                p = ops[d]
                if p.dma is None and p.eng == o.eng and o.dma is None:
                    if o.eng == "pe":
                        continue
                    if o.eidx - p.eidx > 2:
                        continue
                need.append(d)
                if p.dma is None:
                    p.need_inc = True
            edges[o.idx] = need
        last_on = {}
        for o in ops:
            if o.dma is None:
                last_on[o.eng] = o
        for e, o in last_on.items():
            o.need_inc = True
        if not hasattr(self, "dma_pool"):
            self.dma_pool = []
        slotmap = {}
        for o in ops:
            if o.dma is not None:
                if o.dma not in slotmap:
                    k = len(slotmap)
                    if k >= len(self.dma_pool):
                        self.dma_pool.append([self.sem("dpool%d" % k), 0])
                    slotmap[o.dma] = k
                o.slot = slotmap[o.dma]
                ent = self.dma_pool[o.slot]
                ent[1] += 16
                o.inc_val = ent[1]
            elif o.need_inc:
                self.eng_cnt[o.eng] += 1
                o.inc_val = self.eng_cnt[o.eng]
        per_eng = {e: [] for e in ENGS}
        for o in ops:
            per_eng[o.eng].append(o)
        final_eng = dict(self.eng_cnt)
        final_dma = {k: v[1] for k, v in enumerate(self.dma_pool)}

        def emit_stream(ename, eobj):
            waited = self.waited[ename]
            for o in per_eng[ename]:
                for d in edges[o.idx]:
                    p = ops[d]
                    if p.dma is not None:
                        key = ("d", p.slot)
                        sem = self.dma_pool[p.slot][0]
                    else:
                        key = ("e", p.eng)
                        sem = self.eng_sems[p.eng]
                    if waited.get(key, 0) >= p.inc_val:
                        continue
                    waited[key] = p.inc_val
                    eobj.wait_ge(sem, p.inc_val)
                ins = o.fn(eobj)
                if o.dma is not None:
                    ins.then_inc(self.dma_pool[o.slot][0], 16)
                elif o.need_inc:
                    ins.then_inc(self.eng_sems[o.eng], 1)
            for e2 in ENGS:
                if final_eng[e2] > 0 and e2 != ename and waited.get(("e", e2), 0) < final_eng[e2]:
                    eobj.wait_ge(self.eng_sems[e2], final_eng[e2])
                    waited[("e", e2)] = final_eng[e2]
            for k, v in final_dma.items():
                if v > 0 and waited.get(("d", k), 0) < v:
                    eobj.wait_ge(self.dma_pool[k][0], v)
                    waited[("d", k)] = v

        with nc.Block() as block:
            @block.tensor
            def _(e):
                emit_stream("pe", e)

            @block.scalar
            def _(e):
                emit_stream("act", e)

            @block.vector
            def _(e):
                emit_stream("dve", e)

            @block.gpsimd
            def _(e):
                emit_stream("pool", e)

            @block.sync
            def _(e):
                emit_stream("sp", e)

        self.ops = []
        self.last_w = {}
        self.readers = {}
        self.eng_nops = {e: 0 for e in ENGS}


import math
import numpy as np

D = 2048
KC = 16
ALPHA = 8.0 ** 0.25
LN_EPS = 1e-5
U32 = mybir.dt.uint32


class Ctx:
    pass


def dma(P, eng, out, in_, reads, writes, slot):
    P.op(eng, lambda e: e.dma_start(out=out, in_=in_), reads=reads, writes=writes, dma=slot)


def make_ident(P, C):
    identf = P.sb("identf", [128, 128], F32)
    C.ident = P.sb("ident", [128, 128], BF16)
    P.op("pool", lambda e: e.memset(identf[:], 1.0), writes=["identf"])
    P.op("pool", lambda e: e.affine_select(out=identf[:], in_=identf[:], pattern=[[-1, 128]],
                                           compare_op=ALU.is_equal, fill=0.0, base=0, channel_multiplier=1),
         reads=["identf"], writes=["identf"])
    P.op("dve", lambda e: e.tensor_copy(out=C.ident[:], in_=identf[:]), reads=["identf"], writes=["ident"])
    C.iota = P.sb("iota_i", [128, 128], F32)
    P.op("pool", lambda e: e.iota(C.iota[:], pattern=[[1, 128]], base=0, channel_multiplier=0,
                                  allow_small_or_imprecise_dtypes=True), writes=["iota"])


def emit_tile_to_xT(P, C, src_sb, src_key, xT_d, t, pfx, bufs):
    xb, tp, xo = bufs
    P.op("act", lambda e: e.copy(out=xb[:], in_=src_sb), reads=[src_key], writes=[pfx + "xb"])
    for half in range(2):
        for c in range(8):
            cc = half * 8 + c
            P.op("pe", lambda e, cc=cc, c=c: e.transpose(out=tp[:, c * 128:(c + 1) * 128], in_=xb[:, cc * 128:(cc + 1) * 128],
                                                         identity=C.ident[:]),
                 reads=[pfx + "xb", "ident"], writes=[pfx + "tp"])
        P.op("dve", lambda e, half=half: e.tensor_copy(out=xo[:, half * 8:(half + 1) * 8, :],
                                                       in_=tp[:].rearrange("p (c t) -> p c t", c=8)),
             reads=[pfx + "tp"], writes=[pfx + "xo"])
    dma(P, "sp", xT_d[:, t * 128:(t + 1) * 128].rearrange("(c p) t -> p c t", p=128), xo[:],
        [pfx + "xo"], [("xT", t)], pfx + "xo_st")


def phase_x_to_xT(P, C, x_d, xT_d, NT):
    m = P.mark()
    xs = [P.sb("a_xs%d" % i, [128, D], F32) for i in range(2)]
    xb = P.sb("a_xb", [128, D], BF16)
    tp = P.ps("a_tp", [128, 1024], BF16)
    xo = P.sb("a_xo", [128, 16, 128], BF16)
    for t in range(NT):
        b = t % 2
        dma(P, "sp", xs[b][:], x_d[t * 128:(t + 1) * 128, :], [("xres", t)], ["a_xs%d" % b], "a_ld%d" % b)
        emit_tile_to_xT(P, C, xs[b][:], "a_xs%d" % b, xT_d, t, "a_", (xb, tp, xo))
    P.release(m)


def ln_alloc(P, C, pfx):
    L = Ctx()
    L.pfx = pfx
    L.g = P.sb(pfx + "g", [128, D], F32)
    L.b = P.sb(pfx + "b", [128, D], F32)
    L.xin = P.sb(pfx + "xin", [128, D], F32)
    L.y = P.sb(pfx + "y", [128, D], F32)
    L.st = P.sb(pfx + "st", [128, 4, 6], F32)
    L.mv = P.sb(pfx + "mv", [128, 4], F32)
    L.xb = P.sb(pfx + "xb", [128, D], BF16)
    L.tp = P.ps(pfx + "tp", [128, 1024], BF16)
    L.xo = P.sb(pfx + "xo", [128, 16, 128], BF16)
    return L


def ln_load_params(P, L, g_d, b_d):
    dma(P, "sp", L.g[:], g_d.partition_broadcast(128), [], [L.pfx + "g"], L.pfx + "gld")
    dma(P, "sp", L.b[:], b_d.partition_broadcast(128), [], [L.pfx + "b"], L.pfx + "bld")


def ln_tile(P, C, L, mix_ap, mix_keys, xres_in_d, xres_out_d, xT_d, t, mix_in_psum_parts=None):
    pfx = L.pfx
    dma(P, "sp", L.xin[:], xres_in_d[t * 128:(t + 1) * 128, :], [("xres", t)], [pfx + "xin"], pfx + "xin_ld")
    P.op("dve", lambda e: e.scalar_tensor_tensor(out=L.y[:], in0=L.xin[:], scalar=ALPHA, in1=mix_ap,
                                                 op0=ALU.mult, op1=ALU.add),
         reads=[pfx + "xin"] + list(mix_keys), writes=[pfx + "y"])
    for q in range(4):
        P.op("dve", lambda e, q=q: e.bn_stats(out=L.st[:, q, :], in_=L.y[:, q * 512:(q + 1) * 512]),
             reads=[pfx + "y"], writes=[pfx + "st"])
    P.op("dve", lambda e: e.bn_aggr(out=L.mv[:, 0:2], in_=L.st[:].rearrange("p a b -> p (a b)")), reads=[pfx + "st"], writes=[pfx + "mv"])
    P.op("dve", lambda e: e.tensor_scalar(out=L.mv[:, 2:3], in0=L.mv[:, 1:2], scalar1=LN_EPS, scalar2=None, op0=ALU.add),
         reads=[pfx + "mv"], writes=[pfx + "mv"])
    P.op("act", lambda e: e.activation(out=L.mv[:, 3:4], in_=L.mv[:, 2:3], func=AF.Sqrt), reads=[pfx + "mv"], writes=[pfx + "mv"])
    P.op("dve", lambda e: e.reciprocal(out=L.mv[:, 2:3], in_=L.mv[:, 3:4]), reads=[pfx + "mv"], writes=[pfx + "mv"])
    P.op("dve", lambda e: e.tensor_scalar(out=L.y[:], in0=L.y[:], scalar1=L.mv[:, 0:1], scalar2=L.mv[:, 2:3],
                                          op0=ALU.subtract, op1=ALU.mult),
         reads=[pfx + "y", pfx + "mv"], writes=[pfx + "y"])
    P.op("pool", lambda e: e.tensor_tensor(out=L.y[:], in0=L.y[:], in1=L.g[:], op=ALU.mult),
         reads=[pfx + "y", pfx + "g"], writes=[pfx + "y"])
    P.op("pool", lambda e: e.tensor_tensor(out=L.y[:], in0=L.y[:], in1=L.b[:], op=ALU.add),
         reads=[pfx + "y", pfx + "b"], writes=[pfx + "y"])
    dma(P, "sp", xres_out_d[t * 128:(t + 1) * 128, :], L.y[:], [pfx + "y"], [("xres", t)], pfx + "y_st")
    if xT_d is not None:
        emit_tile_to_xT(P, C, L.y[:], pfx + "y", xT_d, t, pfx, (L.xb, L.tp, L.xo))


def linear_phase(P, C, actT_d, K, S, W_d, n0, N, mode, epilogue, pfx, ncol=512, TB=512, post_block=None):
    kc = K // 128
    if kc * ncol > 8192:
        ncol = 8192 // kc
    NSUB = 2 if ((N // ncol) % 2 == 0) else 1
    m = P.mark()
    kh = kc // 2
    wf = [P.sb(pfx + "wf%d" % i, [128, kh, ncol], F32) for i in range(2)]
    wb = [P.sb(pfx + "wb%d" % i, [128, NSUB, kc, ncol], BF16) for i in range(2)]
    ab = [P.sb(pfx + "ab%d" % i, [128, kc, TB], BF16) for i in range(2)]
    pb = [P.ps(pfx + "pb%d" % i, [128, 512], F32) for i in range(4)]
    nsb = N // (ncol * NSUB)
    ntb = S // TB
    cnt = 0
    pcount = 0
    wcnt = 0
    for sbk in range(nsb):
        wi = sbk % 2
        for sub in range(NSUB):
            for half in range(2):
                fi = wcnt % 2
                wcnt += 1
                c0 = n0 + (sbk * NSUB + sub) * ncol
                dma(P, "sp", wf[fi][:], W_d[half * kh * 128:(half + 1) * kh * 128, c0:c0 + ncol].rearrange("(kc p) n -> p kc n", p=128),
                    [], [pfx + "wf%d" % fi], pfx + "wld%d" % fi)
                P.op("pool", lambda e, wi=wi, fi=fi, sub=sub, half=half: e.tensor_copy(out=wb[wi][:, sub, half * kh:(half + 1) * kh, :], in_=wf[fi][:]),
                     reads=[pfx + "wf%d" % fi], writes=[pfx + "wb%d" % wi])
        for tb in range(ntb):
            ai = cnt % 2
            cnt += 1
            for hf in range(2):
                dma(P, "act" if hf else "sp", ab[ai][:, hf * kh:(hf + 1) * kh, :],
                    actT_d[hf * kh * 128:(hf + 1) * kh * 128, tb * TB:(tb + 1) * TB].rearrange("(kc p) t -> p kc t", p=128),
                    [pfx + "actsrc"], [pfx + "ab%d_%d" % (ai, hf)], pfx + "ald%d_%d" % (ai, hf))
            for sub in range(NSUB):
                cb = sbk * NSUB + sub
                if mode == "tok":
                    for ti in range(TB // 128):
                        t = tb * (TB // 128) + ti
                        pi = pcount % 4
                        pcount += 1
                        for k in range(kc):
                            P.op("pe", lambda e, k=k, ai=ai, wi=wi, ti=ti, pi=pi, sub=sub: e.matmul(
                                pb[pi][:, 0:ncol], lhsT=ab[ai][:, k, ti * 128:(ti + 1) * 128], rhs=wb[wi][:, sub, k, :],
                                start=(k == 0), stop=(k == kc - 1)),
                                reads=[pfx + "ab%d_%d" % (ai, 0 if k < kh else 1), pfx + "wb%d" % wi], writes=[pfx + "pb%d" % pi])
                        epilogue(t, cb, pb[pi][:, 0:ncol], pfx + "pb%d" % pi)
                else:
                    for j in range(ncol // 128):
                        pi = pcount % 4
                        pcount += 1
                        for k in range(kc):
                            P.op("pe", lambda e, k=k, ai=ai, wi=wi, j=j, pi=pi, sub=sub: e.matmul(
                                pb[pi][:, 0:TB], lhsT=wb[wi][:, sub, k, j * 128:(j + 1) * 128], rhs=ab[ai][:, k, :],
                                start=(k == 0), stop=(k == kc - 1)),
                                reads=[pfx + "ab%d_%d" % (ai, 0 if k < kh else 1), pfx + "wb%d" % wi], writes=[pfx + "pb%d" % pi])
                        epilogue(tb, cb, j, pb[pi][:, 0:TB], pfx + "pb%d" % pi)
    P.release(m)


def peer_precast(P, C, src_d, dst_d, nelem_per_part, pfx):
    m = P.mark()
    CH = 8192
    f = [P.sb(pfx + "f%d" % i, [128, CH], F32) for i in range(2)]
    b = [P.sb(pfx + "b%d" % i, [128, CH], BF16) for i in range(2)]
    n = nelem_per_part // CH
    engs = ["dve", "pool", "act"]
    for i in range(n):
        bi = i % 2
        dma(P, "sp" if bi == 0 else "act", f[bi][:], src_d[:, i * CH:(i + 1) * CH], [], [pfx + "f%d" % bi], pfx + "ld%d" % bi)
        eg = ["dve", "act"][i % 2]
        if eg == "act":
            P.op("act", lambda e, bi=bi: e.copy(out=b[bi][:], in_=f[bi][:]), reads=[pfx + "f%d" % bi], writes=[pfx + "b%d" % bi])
        else:
            P.op(eg, lambda e, bi=bi: e.tensor_copy(out=b[bi][:], in_=f[bi][:]), reads=[pfx + "f%d" % bi], writes=[pfx + "b%d" % bi])
        dma(P, "pool" if bi == 0 else "sp", dst_d[:, i * CH:(i + 1) * CH], b[bi][:], [pfx + "b%d" % bi], [pfx + "dst%d" % i], pfx + "st%d" % bi)
    P.release(m)


def precast_gen(P, jobs, pfx, CH=2048):
    f = [P.sb(pfx + "f%d" % i, [128, CH], F32) for i in range(2)]
    b = [P.sb(pfx + "b%d" % i, [128, CH], BF16) for i in range(2)]
    chunks = []
    for (src_d, dst_d, nper) in jobs:
        for i in range(nper // CH):
            chunks.append((src_d[:, i * CH:(i + 1) * CH], dst_d[:, i * CH:(i + 1) * CH]))

    def load(i):
        bi = i % 2
        dma(P, "sp", f[bi][:], chunks[i][0], [], [pfx + "f%d" % bi], pfx + "ld%d" % bi)

    load(0)
    for i in range(len(chunks)):
        bi = i % 2
        if i + 1 < len(chunks):
            load(i + 1)
        P.op("act", lambda e, bi=bi: e.copy(out=b[bi][:], in_=f[bi][:]), reads=[pfx + "f%d" % bi], writes=[pfx + "b%d" % bi])
        dma(P, "sp", chunks[i][1], b[bi][:], [pfx + "b%d" % bi], [pfx + "dst%d" % i], pfx + "st%d" % bi)
        yield


def peer_gbuild(P, C, qT_d, keysT_d, GT_d, NT):
    m0 = P.mark()
    kb = P.sb("g_kb", [128, 16, 128], BF16)
    m1 = P.mark()
    kf = P.sb("g_kf", [128, 16, 128], F32)
    dma(P, "sp", kf[:], keysT_d, [], ["g_kf"], "g_kld")
    P.op("dve", lambda e: e.tensor_copy(out=kb[:], in_=kf[:]), reads=["g_kf"], writes=["g_kb"])
    P.release(m1)
    qt = [P.sb("g_qt%d" % i, [128, 16, 128], BF16) for i in range(2)]
    S_sb = P.sb("g_S", [128, 16, 128], F32)
    wk = P.sb("g_wk", [128, 256], F32)
    V16 = P.sb("g_V16", [128, 16, 16], F32)
    I1u = P.sb("g_I1u", [128, 8, 16], U32)
    I1f = P.sb("g_I1f", [128, 8, 16], F32)
    I1b = P.sb("g_I1b", [128, 128], BF16)
    I1T = [P.sb("g_I1T%d" % i, [128, 128], F32) for i in range(2)]
    cand = P.sb("g_cand", [128, 256], F32)
    T16 = P.sb("g_T16", [128, 8, 16], F32)
    neg = P.sb("g_neg", [128, 16], F32)
    negmx = P.sb("g_negmx", [128, 8], F32)
    e1 = P.sb("g_e1", [128, 8, 16], F32)
    e2 = P.sb("g_e2", [128, 8, 128], F32)
    junk = P.sb("g_junk", [128, 16], F32)
    Z = P.sb("g_Z", [128, 8], F32)
    rZ = P.sb("g_rZ", [128, 8], F32)
    cc = P.sb("g_cc", [128, 8, 16], F32)
    tmp = [P.sb("g_tmp%d" % i, [128, 8, 128], F32) for i in range(2)]
    tmp2 = [P.sb("g_tmp2%d" % i, [128, 8, 128], F32) for i in range(2)]
    R = [P.sb("g_R%d" % i, [128, 128, 128], BF16) for i in range(2)]
    RT2 = P.sb("g_RT2", [128, 128, 128], BF16)
    A = P.sb("g_A", [128, 64, 128], BF16)
    GT = P.sb("g_GT", [128, 128, 128], BF16)
    sps = P.ps("g_sps", [128, 512], F32)
    gps = [P.ps("g_gps%d" % i, [128, 512], F32) for i in range(2)]
    tps = [P.ps("g_tps%d" % i, [128, 1024], BF16) for i in range(2)]
    ips = P.ps("g_ips", [128, 128], BF16)

    def stage1(t):
        qi = t % 2
        ri = t % 2
        qk = "g_qt%d" % qi
        rk = "g_R%d" % ri
        dma(P, "sp", qt[qi][:], qT_d[:, t * 128:(t + 1) * 128].rearrange("(c p) t -> p c t", p=128), [], [qk], "g_qld%d" % qi)
        for g4 in range(4):
            for c4 in range(4):
                c = g4 * 4 + c4
                P.op("pe", lambda e, c=c, c4=c4: e.matmul(sps[:, c4 * 128:(c4 + 1) * 128], lhsT=qt[qi][:, c, :], rhs=kb[:, c, :],
                                                          start=True, stop=True), reads=[qk, "g_kb"], writes=["g_sps"])
            P.op("act", lambda e, g4=g4: e.copy(out=S_sb[:, g4 * 4:(g4 + 1) * 4, :], in_=sps[:].rearrange("p (c k) -> p c k", c=4)),
                 reads=["g_sps"], writes=["g_S"])
        for c in range(16):
            P.op("dve", lambda e, c=c: e.max(out=V16[:, c, 0:8], in_=S_sb[:, c, :]), reads=["g_S"], writes=["g_V16"])
            if c % 2 == 0:
                P.op("dve", lambda e, c=c: e.max_index(out=I1u[:, c // 2, 0:8], in_max=V16[:, c, 0:8], in_values=S_sb[:, c, :]),
                     reads=["g_S", "g_V16"], writes=["g_I1u"])
            P.op("dve", lambda e, c=c: e.match_replace(out=wk[:, 0:128], in_to_replace=V16[:, c, 0:8], in_values=S_sb[:, c, :], imm_value=-1e30),
                 reads=["g_S", "g_V16"], writes=["g_wk"])
            P.op("dve", lambda e, c=c: e.max(out=V16[:, c, 8:16], in_=wk[:, 0:128]), reads=["g_wk"], writes=["g_V16"])
            if c % 2 == 0:
                P.op("dve", lambda e, c=c: e.max_index(out=I1u[:, c // 2, 8:16], in_max=V16[:, c, 8:16], in_values=wk[:, 0:128]),
                     reads=["g_wk", "g_V16"], writes=["g_I1u"])
        P.op("dve", lambda e: e.tensor_copy(out=I1f[:], in_=I1u[:]), reads=["g_I1u"], writes=["g_I1f"])
        P.op("dve", lambda e: e.tensor_copy(out=I1b[:], in_=I1f[:].rearrange("p h a -> p (h a)")), reads=["g_I1f"], writes=["g_I1b"])
        P.op("pe", lambda e: e.transpose(out=ips[:], in_=I1b[:], identity=C.ident[:]), reads=["g_I1b", "ident"], writes=["g_ips"])
        P.op("act", lambda e: e.copy(out=I1T[t % 2][:], in_=ips[:]), reads=["g_ips"], writes=["g_I1T%d" % (t % 2)])
        for h in range(8):
            P.op("dve", lambda e, h=h: e.tensor_tensor(out=cand[:].rearrange("p (a b) -> p a b", a=16),
                                                       in0=V16[:, 2 * h, :].unsqueeze(2).to_broadcast([128, 16, 16]),
                                                       in1=V16[:, 2 * h + 1, :].unsqueeze(1).to_broadcast([128, 16, 16]), op=ALU.add),
                 reads=["g_V16"], writes=["g_cand"])
            P.op("dve", lambda e, h=h: e.max(out=T16[:, h, 0:8], in_=cand[:]), reads=["g_cand"], writes=["g_T16"])
            P.op("dve", lambda e, h=h: e.match_replace(out=wk[:], in_to_replace=T16[:, h, 0:8], in_values=cand[:], imm_value=-1e30),
                 reads=["g_cand", "g_T16"], writes=["g_wk"])
            P.op("dve", lambda e, h=h: e.max(out=T16[:, h, 8:16], in_=wk[:]), reads=["g_wk"], writes=["g_T16"])
        P.op("dve", lambda e: e.tensor_scalar(out=neg[:], in0=V16[:, :, 0], scalar1=-1.0, scalar2=None, op0=ALU.mult),
             reads=["g_V16"], writes=["g_neg"])
        P.op("dve", lambda e: e.tensor_scalar(out=negmx[:], in0=T16[:, :, 0], scalar1=-1.0, scalar2=None, op0=ALU.mult),
             reads=["g_T16"], writes=["g_negmx"])
        P.op("dve", lambda e: e.memset(Z[:], 0.0), writes=["g_Z"])
        for h in range(8):
            P.op("act", lambda e, h=h: e.activation(out=e1[:, h, :], in_=V16[:, 2 * h, :], func=AF.Exp, bias=neg[:, 2 * h:2 * h + 1], scale=1.0),
                 reads=["g_V16", "g_neg"], writes=["g_e1"])
            P.op("act", lambda e, h=h: e.activation(out=e2[:, h, :], in_=S_sb[:, 2 * h + 1, :], func=AF.Exp, bias=neg[:, 2 * h + 1:2 * h + 2], scale=1.0),
                 reads=["g_S", "g_neg"], writes=["g_e2"])
            P.op("act", lambda e, h=h: e.activation(out=junk[:], in_=T16[:, h, :], func=AF.Exp, bias=negmx[:, h:h + 1], scale=1.0,
                                                    accum_out=Z[:, h:h + 1]),
                 reads=["g_T16", "g_negmx", "g_Z"], writes=["g_junk", "g_Z"])
        P.op("dve", lambda e: e.reciprocal(out=rZ[:], in_=Z[:]), reads=["g_Z"], writes=["g_rZ"])
        P.op("dve", lambda e: e.tensor_tensor(out=cc[:], in0=e1[:], in1=rZ[:].unsqueeze(2).to_broadcast([128, 8, 16]), op=ALU.mult),
             reads=["g_e1", "g_rZ"], writes=["g_cc"])
        items = [(h, ah) for h in range(8) for ah in range(2)]

        def op1(k):
            h, ah = items[k]
            bi = k % 2
            a0 = ah * 8
            P.op("pool" if k % 2 else "dve", lambda e: e.tensor_tensor(out=tmp[bi][:], in0=S_sb[:, 2 * h + 1, :].unsqueeze(1).to_broadcast([128, 8, 128]),
                                                   in1=V16[:, 2 * h, a0:a0 + 8].unsqueeze(2).to_broadcast([128, 8, 128]), op=ALU.add),
                 reads=["g_S", "g_V16"], writes=["g_tmp%d" % bi])

        op1(0)
        for k in range(16):
            h, ah = items[k]
            bi = k % 2
            a0 = ah * 8
            P.op("dve", lambda e, h=h, bi=bi: e.scalar_tensor_tensor(out=tmp2[bi][:], in0=tmp[bi][:], scalar=T16[:, h, 15:16],
                                                                     in1=e2[:, h, :].unsqueeze(1).to_broadcast([128, 8, 128]),
                                                                     op0=ALU.is_ge, op1=ALU.mult),
                 reads=["g_tmp%d" % bi, "g_T16", "g_e2"], writes=["g_tmp2%d" % bi])
            if k + 1 < 16:
                op1(k + 1)
            P.op("pool", lambda e, h=h, a0=a0, bi=bi: e.tensor_tensor(out=R[ri][:, h * 16 + a0:h * 16 + a0 + 8, :], in0=tmp2[bi][:],
                                                                      in1=cc[:, h, a0:a0 + 8].unsqueeze(2).to_broadcast([128, 8, 128]), op=ALU.mult),
                 reads=["g_tmp2%d" % bi, "g_cc"], writes=[rk])

    def stage2a(t):
        ri = t % 2
        rk = "g_R%d" % ri
        for jg in range(16):
            ti = jg % 2
            for jj in range(8):
                j = jg * 8 + jj
                P.op("pe", lambda e, j=j, jj=jj, ti=ti: e.transpose(out=tps[ti][:, jj * 128:(jj + 1) * 128], in_=R[ri][:, :, j], identity=C.ident[:]),
                     reads=[rk, "ident"], writes=["g_tps%d" % ti])
            P.op("act", lambda e, jg=jg, ti=ti: e.copy(out=RT2[:, :, jg * 8:(jg + 1) * 8], in_=tps[ti][:].rearrange("p (j t) -> p t j", j=8)),
                 reads=["g_tps%d" % ti], writes=["g_RT2"])

    def stage2b(t):
        ik = "g_I1T%d" % (t % 2)
        for th in range(2):
            P.op("dve", lambda e, th=th: e.tensor_tensor(out=A[:], in0=C.iota[:].unsqueeze(1).to_broadcast([128, 64, 128]),
                                                         in1=I1T[t % 2][:, th * 64:(th + 1) * 64].unsqueeze(2).to_broadcast([128, 64, 128]), op=ALU.is_equal),
                 reads=["iota", ik], writes=["g_A"])
            for tg in range(16):
                gi = tg % 2
                for t4 in range(4):
                    tl = tg * 4 + t4
                    tt = th * 64 + tl
                    P.op("pe", lambda e, tt=tt, tl=tl, t4=t4, gi=gi: e.matmul(gps[gi][:, t4 * 128:(t4 + 1) * 128], lhsT=A[:, tl, :], rhs=RT2[:, tt, :],
                                                                              start=True, stop=True),
                         reads=["g_A", "g_RT2"], writes=["g_gps%d" % gi])
                t0 = th * 64 + tg * 4
                P.op("act", lambda e, t0=t0, gi=gi: e.copy(out=GT[:, :, t0:t0 + 4], in_=gps[gi][:].rearrange("p (t j) -> p j t", t=4)),
                     reads=["g_gps%d" % gi], writes=["g_GT"])
        dma(P, "sp", GT_d[t], GT[:], ["g_GT"], [("GT", t)], "g_gst")

    stage1(0)
    for t in range(NT):
        stage2a(t)
        if t + 1 < NT:
            stage1(t + 1)
        stage2b(t)
    P.release(m0)


def peer_main(P, C, xT_d, UTb_d, Vb_d, GT_d, NT, out_cb, TBT=4, CG=4):
    m = P.mark()
    TB = TBT * 128
    NG = 128 // CG
    xt = P.sb("m_xt", [128, 16, TB], BF16)
    Y = [P.sb("m_Y%d" % i, [128, D], F32) for i in range(TBT)]
    ut = [P.sb("m_ut%d" % i, [128, 16, CG * 128], BF16) for i in range(2)]
    vt = [P.sb("m_vt%d" % i, [128, CG, D], BF16) for i in range(2)]
    gt = [P.sb("m_gt%d" % i, [128, CG, TB], BF16) for i in range(2)]
    hh = [P.sb("m_hh%d" % i, [128, TB], F32) for i in range(2)]
    gh = [P.sb("m_gh%d" % i, [128, CG, TB], BF16) for i in range(2)]
    hps = [P.ps("m_hps%d" % i, [128, 512], F32) for i in range(2)]
    yps = [P.ps("m_yps%d" % i, [128, 1024], F32) for i in range(2)]
    L = C.ln
    ntb = NT // TBT
    gcnt = 0
    ycnt = 0
    hcnt = 0
    for tb in range(ntb):
        dma(P, "act", xt[:], xT_d[:, tb * TB:(tb + 1) * TB].rearrange("(c p) t -> p c t", p=128),
            [("xT", tt) for tt in range(tb * TBT, (tb + 1) * TBT)], ["m_xt"], "m_xld")
        for g in range(NG):
            bi = gcnt % 2
            gcnt += 1
            j0 = g * CG
            dma(P, "sp", ut[bi][:].rearrange("p c (j i) -> p c j i", j=CG),
                UTb_d[:, j0:j0 + CG, :].rearrange("(c p) j i -> p c j i", p=128), ["UTb"], ["m_ut%d" % bi], "m_uld%d" % bi)
            dma(P, "pool", vt[bi][:], Vb_d[j0:j0 + CG, :, :].rearrange("j i d -> i j d"), ["Vb"], ["m_vt%d" % bi], "m_vld%d" % bi)
            for ti in range(TBT):
                dma(P, "sp", gt[bi][:, :, ti * 128:(ti + 1) * 128], GT_d[tb * TBT + ti][:, j0:j0 + CG, :],
                    [("GT", tb * TBT + ti)], ["m_gt%d" % bi], "m_gld%d" % bi)
            for cj in range(CG):
                hi = hcnt % 2
                hcnt += 1
                for k in range(16):
                    P.op("pe", lambda e, k=k, bi=bi, cj=cj, hi=hi: e.matmul(hps[hi][:, 0:TB], lhsT=ut[bi][:, k, cj * 128:(cj + 1) * 128], rhs=xt[:, k, :],
                                                                            start=(k == 0), stop=(k == 15)),
                         reads=["m_ut%d" % bi, "m_xt"], writes=["m_hps%d" % hi])
                P.op("act", lambda e, hi=hi: e.activation(out=hh[hi][:], in_=hps[hi][:, 0:TB], func=AF.Gelu_apprx_tanh),
                     reads=["m_hps%d" % hi], writes=["m_hh%d" % hi])
                P.op("dve", lambda e, hi=hi, bi=bi, cj=cj: e.tensor_tensor(out=gh[bi][:, cj, :], in0=hh[hi][:], in1=gt[bi][:, cj, :], op=ALU.mult),
                     reads=["m_hh%d" % hi, "m_gt%d" % bi], writes=["m_gh%d" % bi])
            for ti in range(TBT):
                for dh in range(2):
                    yi = ycnt % 2
                    ycnt += 1
                    for cj in range(CG):
                        for db in range(2):
                            P.op("pe", lambda e, cj=cj, db=db, dh=dh, ti=ti, bi=bi, yi=yi: e.matmul(
                                yps[yi][:, db * 512:(db + 1) * 512], lhsT=gh[bi][:, cj, ti * 128:(ti + 1) * 128],
                                rhs=vt[bi][:, cj, dh * 1024 + db * 512: dh * 1024 + (db + 1) * 512],
                                start=(cj == 0), stop=(cj == CG - 1)),
                                reads=["m_gh%d" % bi, "m_vt%d" % bi], writes=["m_yps%d" % yi])
                    if g == 0:
                        P.op("act", lambda e, ti=ti, dh=dh, yi=yi: e.copy(out=Y[ti][:, dh * 1024:(dh + 1) * 1024], in_=yps[yi][:]),
                             reads=["m_yps%d" % yi], writes=["m_Y%d" % ti])
                    else:
                        eng = "pool_no"
                        P.op("dve", lambda e, ti=ti, dh=dh, yi=yi: e.tensor_tensor(out=Y[ti][:, dh * 1024:(dh + 1) * 1024],
                                                                                  in0=Y[ti][:, dh * 1024:(dh + 1) * 1024], in1=yps[yi][:], op=ALU.add),
                             reads=["m_yps%d" % yi, "m_Y%d" % ti], writes=["m_Y%d" % ti])
        for ti in range(TBT):
            out_cb(tb * TBT + ti, Y[ti][:], "m_Y%d" % ti)
    P.release(m)


def ln_pass(P, C, mix_d, xres_in_d, xres_out_d, xT_d, g_d, b_d, NT, pfx):
    m = P.mark()
    L = ln_alloc(P, C, pfx)
    ln_load_params(P, L, g_d, b_d)
    mx = [P.sb(pfx + "mx%d" % i, [128, D], F32) for i in range(2)]
    for t in range(NT):
        i = t % 2
        dma(P, "act", mx[i][:], mix_d[t * 128:(t + 1) * 128, :], [("mix", t)], [pfx + "mx%d" % i], pfx + "mld%d" % i)
        ln_tile(P, C, L, mx[i][:], [pfx + "mx%d" % i], xres_in_d, xres_out_d, xT_d, t)
    P.release(m)


def make_store_epi_tok(P, dst_d, col0, ncol, dt, pfx, wkey, func=None):
    bufs = [P.sb(pfx + "eo%d" % i, [128, ncol], dt) for i in range(2)]
    cnt = [0]

    def epi(t, cb, ps, pkey):
        i = cnt[0] % 2
        cnt[0] += 1
        if func is None:
            P.op("act", lambda e: e.copy(out=bufs[i][:], in_=ps), reads=[pkey], writes=[pfx + "eo%d" % i])
        else:
            P.op("act", lambda e: e.activation(out=bufs[i][:], in_=ps, func=func), reads=[pkey], writes=[pfx + "eo%d" % i])
        dma(P, "sp", dst_d[t * 128:(t + 1) * 128, col0 + cb * ncol: col0 + (cb + 1) * ncol], bufs[i][:],
            [pfx + "eo%d" % i], [(wkey, t)], pfx + "est%d" % i)
    return epi


def make_store_epi_feat(P, dst_d, row0, pfx, wkey, TB=512):
    bufs = [P.sb(pfx + "fo%d" % i, [128, TB], BF16) for i in range(2)]
    cnt = [0]

    def epi(tb, cb, j, ps, pkey):
        i = cnt[0] % 2
        cnt[0] += 1
        P.op("act", lambda e: e.copy(out=bufs[i][:], in_=ps), reads=[pkey], writes=[pfx + "fo%d" % i])
        c = cb * 4 + j
        dma(P, "sp", dst_d[row0 + c * 128: row0 + (c + 1) * 128, tb * TB:(tb + 1) * TB], bufs[i][:],
            [pfx + "fo%d" % i], [(wkey, tt) for tt in range(tb * TB // 128, (tb + 1) * TB // 128)], pfx + "fst%d" % i)
    return epi


def retention_layer(P, C, T, l, xres_in_d, xres_out_d, NT, bg_jobs=None):
    S = NT * 128
    w_in = T.ret_w_in[l]
    m = P.mark()
    tab = [P.sb("ra_tab%d" % i, [128, 4, 512], F32) for i in range(2)]
    t1 = P.sb("ra_t1", [128, 512], F32)
    t2 = P.sb("ra_t2", [128, 512], F32)
    t3 = P.sb("ra_t3", [128, 512], F32)
    t4 = P.sb("ra_t4", [128, 512], F32)
    o1 = [P.sb("ra_o1%d" % i, [128, 512], BF16) for i in range(2)]
    o2 = [P.sb("ra_o2%d" % i, [128, 512], BF16) for i in range(2)]
    kdec = P.sb("ra_kdec", [128, 8], F32)
    ktp = P.ps("ra_ktp", [128, 1024], BF16)
    kto = [P.sb("ra_kto%d" % i, [128, 4, 256], BF16) for i in range(2)]
    dma(P, "sp", kdec[:], T.kdec, [], ["ra_kdec"], "ra_kdld")
    st = {"prev": None, "cnt": 0, "tb": -1, "tabi": 0, "kcnt": 0}

    def qk_epi(tb, cb, j, ps, pkey):
        if j % 2 == 0:
            st["prev"] = (ps, pkey)
            return
        A, akey = st["prev"]
        B, bkey = ps, pkey
        isq = cb < 4
        ti = (tb % 2)
        if st["tb"] != (cb, tb):
            st["tb"] = (cb, tb)
            dma(P, "sp", tab[ti][:], T.rope[:, :, tb * 512:(tb + 1) * 512].rearrange("f p t -> p f t"), [], ["ra_tab%d" % ti], "ra_tld%d" % ti)
        cs = tab[ti][:, 2 if isq else 0, :]
        sn = tab[ti][:, 3 if isq else 1, :]
        i = st["cnt"] % 2
        st["cnt"] += 1
        P.op("dve", lambda e: e.tensor_tensor(out=t1[:], in0=A, in1=cs, op=ALU.mult), reads=[akey, "ra_tab%d" % ti], writes=["ra_t1"])
        P.op("dve", lambda e: e.tensor_tensor(out=t2[:], in0=B, in1=sn, op=ALU.mult), reads=[bkey, "ra_tab%d" % ti], writes=["ra_t2"])
        P.op("dve", lambda e: e.tensor_tensor(out=t3[:], in0=A, in1=sn, op=ALU.mult), reads=[akey, "ra_tab%d" % ti], writes=["ra_t3"])
        P.op("dve", lambda e: e.tensor_tensor(out=t4[:], in0=B, in1=cs, op=ALU.mult), reads=[bkey, "ra_tab%d" % ti], writes=["ra_t4"])
        P.op("pool", lambda e: e.tensor_tensor(out=o1[i][:], in0=t1[:], in1=t2[:], op=ALU.subtract), reads=["ra_t1", "ra_t2"], writes=["ra_o1%d" % i])
        P.op("pool", lambda e: e.tensor_tensor(out=o2[i][:], in0=t3[:], in1=t4[:], op=ALU.add), reads=["ra_t3", "ra_t4"], writes=["ra_o2%d" % i])
        c = (cb * 4 + j - 1) % 16
        dst = T.qT if isq else T.kT
        wk = "qT" if isq else "kT"
        tts = range(tb * 4, tb * 4 + 4)
        dma(P, "sp", dst[c * 128:(c + 1) * 128, tb * 512:(tb + 1) * 512], o1[i][:], ["ra_o1%d" % i], [(wk, tt) for tt in tts], "ra_s1%d" % i)
        dma(P, "sp", dst[(c + 1) * 128:(c + 2) * 128, tb * 512:(tb + 1) * 512], o2[i][:], ["ra_o2%d" % i], [(wk, tt) for tt in tts], "ra_s2%d" % i)
        if not isq:
            h = c // 2
            ki = st["kcnt"] % 2
            st["kcnt"] += 1
            for half, ob in enumerate((o1[i], o2[i])):
                for q in range(4):
                    P.op("pe", lambda e, ob=ob, q=q, half=half: e.transpose(out=ktp[:, (half * 4 + q) * 128:(half * 4 + q + 1) * 128],
                                                                            in_=ob[:, q * 128:(q + 1) * 128], identity=C.ident[:]),
                         reads=["ra_o1%d" % i, "ra_o2%d" % i, "ident"], writes=["ra_ktp"])
            P.op("dve", lambda e, ki=ki, h=h: e.tensor_scalar(out=kto[ki][:].rearrange("p q (a f) -> p a q f", a=2),
                                                              in0=ktp[:].rearrange("p (a q f) -> p a q f", a=2, q=4),
                                                              scalar1=kdec[:, h:h + 1], scalar2=None, op0=ALU.mult),
                 reads=["ra_ktp", "ra_kdec"], writes=["ra_kto%d" % ki])
            dma(P, "sp", T.kd[tb * 512:(tb + 1) * 512, h * 256:(h + 1) * 256].rearrange("(q p) f -> p q f", p=128), kto[ki][:],
                ["ra_kto%d" % ki], [("kd", tt) for tt in tts], "ra_ks%d" % ki)

    linear_phase(P, C, T.xT, 2048, S, w_in, 0, 4096, "feat", qk_epi, "lqk_")
    P.release(m)
    m = P.mark()
    epi_v = make_store_epi_tok(P, T.v, 0, 512, BF16, "rbv_", "v")
    linear_phase(P, C, T.xT, 2048, S, w_in, 4096, 4096, "tok", epi_v, "lv_")
    P.release(m)
    m = P.mark()
    epi_g = make_store_epi_tok(P, T.sg, 0, 512, F32, "rbg_", "sg", func=AF.Silu)
    linear_phase(P, C, T.xT, 2048, S, w_in, 8192, 4096, "tok", epi_g, "lg_")
    P.release(m)
    m = P.mark()
    qt = [P.sb("rc_qt%d" % i, [128, 16, 128], BF16) for i in range(2)]
    kt = [P.sb("rc_kt%d" % i, [128, 16, 128], BF16) for i in range(2)]
    kdt = [P.sb("rc_kd%d" % i, [128, 2048], BF16) for i in range(2)]
    vt = [P.sb("rc_v%d" % i, [128, 4096], BF16) for i in range(2)]
    sgt = [P.sb("rc_sg%d" % i, [128, 4096], F32) for i in range(2)]
    Sf = P.sb("rc_Sf", [128, 16, 512], F32)
    Sb = P.sb("rc_Sb", [128, 16, 512], BF16)
    MT = P.sb("rc_MT", [128, 8, 128], F32)
    qdec = P.sb("rc_qdec", [128, 8, 128], F32)
    gng = P.sb("rc_gng", [128, 4096], F32)
    PT = [P.sb("rc_PT%d" % i, [128, 128], BF16) for i in range(2)]
    qd = [P.sb("rc_qd%d" % i, [128, 2, 128], BF16) for i in range(2)]
    on = [P.sb("rc_on%d" % i, [128, 512], F32) for i in range(2)]
    gb = [P.sb("rc_gb%d" % i, [128, 512], BF16) for i in range(2)]
    gto = [P.sb("rc_gto%d" % i, [128, 4, 128], BF16) for i in range(2)]
    stt = P.sb("rc_st", [128, 6], F32)
    mv = P.sb("rc_mv", [128, 4], F32)
    sc_ps = P.ps("rc_scps", [128, 128], F32)
    o_ps = [P.ps("rc_ops%d" % i, [128, 512], F32) for i in range(2)]
    s_ps = [P.ps("rc_sps%d" % i, [128, 512], F32) for i in range(2)]
    g_tp = P.ps("rc_gtp", [128, 512], BF16)
    dma(P, "sp", MT[:], T.retMT, [], ["rc_MT"], "rc_mld")
    dma(P, "sp", qdec[:], T.qdec, [], ["rc_qdec"], "rc_qdld")
    dma(P, "sp", gng[:], T.ret_gn_g[l].partition_broadcast(128), [], ["rc_gng"], "rc_gld")
    P.op("pool", lambda e: e.memset(Sf[:], 0.0), writes=[("Sf", hh, dd) for hh in range(8) for dd in range(2)])
    P.op("pool", lambda e: e.memset(Sb[:], 0.0), writes=[("Sb", hh) for hh in range(8)])
    bg = precast_gen(P, bg_jobs, "rcbg_") if bg_jobs else iter(())
    sc2 = [sc_ps, P.ps("rc_scps1", [128, 128], F32)]
    gtp2 = [g_tp, P.ps("rc_gtp1", [128, 512], BF16)]
    stt2 = [stt, P.sb("rc_st1", [128, 6], F32)]
    mv2 = [mv, P.sb("rc_mv1", [128, 4], F32)]

    def head_chain(t, i, h):
        b = h % 2
        bs = "%d" % b
        scp, gtp, st_, mv_ = sc2[b], gtp2[b], stt2[b], mv2[b]
        for dc in range(2):
            P.op("pe", lambda e, dc=dc: e.matmul(scp[:], lhsT=kt[i][:, 2 * h + dc, :], rhs=qt[i][:, 2 * h + dc, :], start=(dc == 0), stop=(dc == 1)),
                 reads=["rc_kt%d" % i, "rc_qt%d" % i], writes=["rc_scps" + bs])
        P.op("pool", lambda e: e.tensor_tensor(out=qd[b][:], in0=qt[i][:, 2 * h:2 * h + 2, :],
                                               in1=qdec[:, h, :].unsqueeze(1).to_broadcast([128, 2, 128]), op=ALU.mult),
             reads=["rc_qt%d" % i, "rc_qdec"], writes=["rc_qd" + bs])
        yield
        P.op("dve", lambda e: e.tensor_tensor(out=PT[b][:], in0=scp[:], in1=MT[:, h, :], op=ALU.mult),
             reads=["rc_scps" + bs, "rc_MT"], writes=["rc_PT" + bs])
        yield
        P.op("pe", lambda e: e.matmul(o_ps[b][:], lhsT=PT[b][:], rhs=vt[i][:, h * 512:(h + 1) * 512], start=True, stop=False),
             reads=["rc_PT" + bs, "rc_v%d" % i], writes=["rc_ops" + bs])
        for dc in range(2):
            P.op("pe", lambda e, dc=dc: e.matmul(o_ps[b][:], lhsT=qd[b][:, dc, :], rhs=Sb[:, 2 * h + dc, :], start=False, stop=(dc == 1)),
                 reads=["rc_qd" + bs, ("Sb", h)], writes=["rc_ops" + bs])
        yield
        for dc in range(2):
            P.op("pe", lambda e, dc=dc: e.matmul(s_ps[b][:], lhsT=kdt[i][:, h * 256 + dc * 128: h * 256 + (dc + 1) * 128],
                                                 rhs=vt[i][:, h * 512:(h + 1) * 512], start=True, stop=True),
                 reads=["rc_kd%d" % i, "rc_v%d" % i], writes=["rc_sps" + bs])
            if dc == 0:
                P.op("dve", lambda e: e.bn_stats(out=st_[:], in_=o_ps[b][:]), reads=["rc_ops" + bs], writes=["rc_st" + bs])
            yield
            P.op("dve", lambda e, dc=dc: e.scalar_tensor_tensor(out=Sf[:, 2 * h + dc, :], in0=Sf[:, 2 * h + dc, :], scalar=float(T.gamma128[h]),
                                                                in1=s_ps[b][:], op0=ALU.mult, op1=ALU.add),
                 reads=["rc_sps" + bs, ("Sf", h, dc)], writes=[("Sf", h, dc)])
            yield
            P.op("act", lambda e, dc=dc: e.copy(out=Sb[:, 2 * h + dc, :], in_=Sf[:, 2 * h + dc, :]),
                 reads=[("Sf", h, dc)], writes=[("Sb", h)])
            yield
        P.op("dve", lambda e: e.bn_aggr(out=mv_[:, 0:2], in_=st_[:]), reads=["rc_st" + bs], writes=["rc_mv" + bs])
        yield
        P.op("dve", lambda e: e.tensor_scalar(out=mv_[:, 2:3], in0=mv_[:, 1:2], scalar1=LN_EPS, scalar2=None, op0=ALU.add),
             reads=["rc_mv" + bs], writes=["rc_mv" + bs])
        yield
        P.op("act", lambda e: e.activation(out=mv_[:, 3:4], in_=mv_[:, 2:3], func=AF.Sqrt), reads=["rc_mv" + bs], writes=["rc_mv" + bs])
        yield
        P.op("dve", lambda e: e.reciprocal(out=mv_[:, 2:3], in_=mv_[:, 3:4]), reads=["rc_mv" + bs], writes=["rc_mv" + bs])
        yield
        P.op("dve", lambda e: e.tensor_scalar(out=on[b][:], in0=o_ps[b][:], scalar1=mv_[:, 0:1], scalar2=mv_[:, 2:3],
                                              op0=ALU.subtract, op1=ALU.mult),
             reads=["rc_ops" + bs, "rc_mv" + bs], writes=["rc_on" + bs])
        yield
        P.op("pool", lambda e: e.tensor_tensor(out=on[b][:], in0=on[b][:], in1=gng[:, h * 512:(h + 1) * 512], op=ALU.mult),
             reads=["rc_on" + bs, "rc_gng"], writes=["rc_on" + bs])
        yield
        P.op("pool", lambda e: e.tensor_tensor(out=gb[b][:], in0=on[b][:], in1=sgt[i][:, h * 512:(h + 1) * 512], op=ALU.mult),
             reads=["rc_on" + bs, "rc_sg%d" % i], writes=["rc_gb" + bs])
        yield
        for q in range(4):
            P.op("pe", lambda e, q=q: e.transpose(out=gtp[:, q * 128:(q + 1) * 128], in_=gb[b][:, q * 128:(q + 1) * 128], identity=C.ident[:]),
                 reads=["rc_gb" + bs, "ident"], writes=["rc_gtp" + bs])
        yield
        P.op("act", lambda e: e.copy(out=gto[b][:], in_=gtp[:].rearrange("p (q t) -> p q t", q=4)), reads=["rc_gtp" + bs], writes=["rc_gto" + bs])
        dma(P, "sp", T.gT[h * 512:(h + 1) * 512, t * 128:(t + 1) * 128].rearrange("(q p) t -> p q t", p=128), gto[b][:],
            ["rc_gto" + bs], ["wo_actsrc"], "rc_gs" + bs)
        yield

    for t in range(NT):
        i = t % 2
        dma(P, "sp", qt[i][:], T.qT[:, t * 128:(t + 1) * 128].rearrange("(c p) t -> p c t", p=128), [], ["rc_qt%d" % i], "rc_l1%d" % i)
        dma(P, "sp", kt[i][:], T.kT[:, t * 128:(t + 1) * 128].rearrange("(c p) t -> p c t", p=128), [], ["rc_kt%d" % i], "rc_l2%d" % i)
        dma(P, "act", kdt[i][:], T.kd[t * 128:(t + 1) * 128, :], [], ["rc_kd%d" % i], "rc_l3%d" % i)
        dma(P, "act", vt[i][:], T.v[t * 128:(t + 1) * 128, :], [], ["rc_v%d" % i], "rc_l4%d" % i)
        dma(P, "act", sgt[i][:], T.sg[t * 128:(t + 1) * 128, :], [], ["rc_sg%d" % i], "rc_l5%d" % i)
        for hp in range(4):
            gens = [head_chain(t, i, 2 * hp), head_chain(t, i, 2 * hp + 1)]
            next(bg, None)
            next(bg, None)
            alive = list(gens)
            while alive:
                for g in list(alive):
                    try:
                        next(g)
                    except StopIteration:
                        alive.remove(g)
    for _ in bg:
        pass
    P.release(m)
    m = P.mark()
    epi_o = make_store_epi_tok(P, T.mix, 0, 256, F32, "rdo_", "mix")
    linear_phase(P, C, T.gT, 4096, S, T.ret_w_out[l], 0, 2048, "tok", epi_o, "wo_", ncol=256)
    P.release(m)
    ln_pass(P, C, T.mix, xres_in_d, xres_out_d, T.xT, T.ln_g[l][0:1, :], T.ln_b[l][0:1, :], NT, "rln_")


def shared_kv_phase(P, C, T, NT):
    S = NT * 128
    m = P.mark()
    epi_k = make_store_epi_feat(P, T.KT, 0, "kvk_", "KT")
    linear_phase(P, C, T.xT, 2048, S, T.kv_w, 0, 2048, "feat", epi_k, "lkk_")
    P.release(m)
    m = P.mark()
    epi_v = make_store_epi_tok(P, T.Vs, 0, 512, BF16, "kvv_", "Vs")
    linear_phase(P, C, T.xT, 2048, S, T.kv_w, 2048, 2048, "tok", epi_v, "lkv_")
    P.release(m)


def sb_layer(P, C, T, l, xres_in_d, xres_out_d, NT, bg_jobs=None):
    S = NT * 128
    lb = l - 2
    scale = 128.0 ** -0.5
    m = P.mark()
    epi_q = make_store_epi_feat(P, T.QT, 0, "sq_", "QT")
    linear_phase(P, C, T.xT, 2048, S, T.sb_wq[lb], 0, 2048, "feat", epi_q, "lsq_")
    P.release(m)
    m = P.mark()
    NQB = S // 512
    Kh = [P.sb("sa_K%d" % i, [128, S], BF16) for i in range(2)]
    Qh = [P.sb("sa_Q%d" % i, [128, S], BF16) for i in range(2)]
    Vh = [P.sb("sa_V%d" % i, [128, NT, 128], BF16) for i in range(2)]
    cm = P.sb("sa_cm", [128, 4, 512], F32)
    trif = P.sb("sa_trif", [128, 128], F32)
    tri = P.sb("sa_tri", [128, 128], BF16)
    ones = P.sb("sa_ones", [128, 128], BF16)
    NS = 2
    mk = lambda nm, dt: [[P.sb("sa_%s%d_%d" % (nm, i, p), [128, 512], dt) for p in range(2)] for i in range(NS)]
    ex = mk("ex", F32)
    sp = mk("sp", F32)
    spm = mk("spm", F32)
    shi = mk("shi", BF16)
    slo = mk("slo", BF16)
    u = mk("u", F32)
    lw = mk("lw", F32)
    ew = mk("ew", F32)
    w = mk("w", BF16)
    spsum = [P.sb("sa_spsum%d" % i, [128, 512], F32) for i in range(NS)]
    sshi = [P.sb("sa_sshi%d" % i, [128, 512], BF16) for i in range(NS)]
    sslo = [P.sb("sa_sslo%d" % i, [128, 512], BF16) for i in range(NS)]
    oo = [P.sb("sa_oo%d" % i, [128, 512], BF16) for i in range(NS)]
    z_ps = [[P.ps("sa_zps%d_%d" % (i, p), [128, 512], F32) for p in range(2)] for i in range(NS)]
    t_ps = [P.ps("sa_tps%d" % i, [128, 512], F32) for i in range(NS)]
    o_ps = [P.ps("sa_ops%d" % i, [128, 512], F32) for i in range(NS)]
    dma(P, "sp", cm[:], T.cmask, [], ["sa_cm"], "sa_cmld")
    P.op("pool", lambda e: e.memset(trif[:], 1.0), writes=["sa_trif"])
    P.op("pool", lambda e: e.affine_select(out=trif[:], in_=trif[:], pattern=[[-1, 128]], compare_op=ALU.is_gt, fill=0.0, base=0, channel_multiplier=1),
         reads=["sa_trif"], writes=["sa_trif"])
    P.op("dve", lambda e: e.tensor_copy(out=tri[:], in_=trif[:]), reads=["sa_trif"], writes=["sa_tri"])
    P.op("pool", lambda e: e.memset(ones[:], 1.0), writes=["sa_ones"])

    def chain(si, p, hi, h, qb, ai, a, na):
        k = "%d_%d" % (si, p)
        ks = "%d" % si
        diag = a >= qb * 4
        first = (ai == 0)
        last = (ai == na - 1)
        P.op("pe", lambda e: e.matmul(z_ps[si][p][:], lhsT=Kh[hi][:, a * 128:(a + 1) * 128], rhs=Qh[hi][:, qb * 512:(qb + 1) * 512], start=True, stop=True),
             reads=["sa_K%d" % hi, "sa_Q%d" % hi], writes=["sa_zps" + k])
        yield
        P.op("act", lambda e: e.activation(out=ex[si][p][:], in_=z_ps[si][p][:], func=AF.Exp, scale=scale), reads=["sa_zps" + k], writes=["sa_ex" + k])
        yield
        P.op("act", lambda e: e.activation(out=sp[si][p][:], in_=ex[si][p][:], func=AF.Ln, bias=1.0, scale=1.0), reads=["sa_ex" + k], writes=["sa_sp" + k])
        yield
        if diag:
            P.op("pool", lambda e: e.tensor_tensor(out=spm[si][p][:], in0=sp[si][p][:], in1=cm[:, a - qb * 4, :], op=ALU.mult),
                 reads=["sa_sp" + k, "sa_cm"], writes=["sa_spm" + k])
            src, skey = spm[si][p], "sa_spm" + k
        else:
            src, skey = sp[si][p], "sa_sp" + k
        yield
        P.op("act", lambda e: e.copy(out=shi[si][p][:], in_=src[:]), reads=[skey], writes=["sa_shi" + k])
        yield
        P.op("pool", lambda e: e.tensor_tensor(out=slo[si][p][:], in0=src[:], in1=shi[si][p][:], op=ALU.subtract), reads=[skey, "sa_shi" + k], writes=["sa_slo" + k])
        P.op("dve", lambda e: e.scalar_tensor_tensor(out=u[si][p][:], in0=z_ps[si][p][:], scalar=scale, in1=sp[si][p][:], op0=ALU.mult, op1=ALU.subtract),
             reads=["sa_zps" + k, "sa_sp" + k], writes=["sa_u" + k])
        yield
        P.op("pe", lambda e: e.matmul(t_ps[si][:], lhsT=tri[:], rhs=shi[si][p][:], start=True, stop=False), reads=["sa_tri", "sa_shi" + k], writes=["sa_tps" + ks])
        P.op("pe", lambda e: e.matmul(t_ps[si][:], lhsT=tri[:], rhs=slo[si][p][:], start=False, stop=first), reads=["sa_tri", "sa_slo" + k], writes=["sa_tps" + ks])
        if not first:
            P.op("pe", lambda e: e.matmul(t_ps[si][:], lhsT=ones[:], rhs=sshi[si][:], start=False, stop=False), reads=["sa_ones", "sa_sshi" + ks], writes=["sa_tps" + ks])
            P.op("pe", lambda e: e.matmul(t_ps[si][:], lhsT=ones[:], rhs=sslo[si][:], start=False, stop=True), reads=["sa_ones", "sa_sslo" + ks], writes=["sa_tps" + ks])
        yield
        if not last:
            if first:
                P.op("pool", lambda e: e.tensor_copy(out=spsum[si][:], in_=src[:]), reads=[skey], writes=["sa_spsum" + ks])
            else:
                P.op("pool", lambda e: e.tensor_tensor(out=spsum[si][:], in0=spsum[si][:], in1=src[:], op=ALU.add), reads=[skey, "sa_spsum" + ks], writes=["sa_spsum" + ks])
        yield
        if not last:
            P.op("act", lambda e: e.copy(out=sshi[si][:], in_=spsum[si][:]), reads=["sa_spsum" + ks], writes=["sa_sshi" + ks])
        yield
        if not last:
            P.op("dve", lambda e: e.tensor_tensor(out=sslo[si][:], in0=spsum[si][:], in1=sshi[si][:], op=ALU.subtract), reads=["sa_spsum" + ks, "sa_sshi" + ks], writes=["sa_sslo" + ks])
        yield
        yield
        P.op("dve", lambda e: e.tensor_tensor(out=lw[si][p][:], in0=u[si][p][:], in1=t_ps[si][:], op=ALU.subtract),
             reads=["sa_tps" + ks, "sa_u" + k], writes=["sa_lw" + k])
        yield
        if diag:
            P.op("act", lambda e: e.activation(out=ew[si][p][:], in_=lw[si][p][:], func=AF.Exp), reads=["sa_lw" + k], writes=["sa_ew" + k])
            yield
            P.op("pool", lambda e: e.tensor_tensor(out=w[si][p][:], in0=ew[si][p][:], in1=cm[:, a - qb * 4, :], op=ALU.mult),
                 reads=["sa_ew" + k, "sa_cm"], writes=["sa_w" + k])
        else:
            P.op("act", lambda e: e.activation(out=w[si][p][:], in_=lw[si][p][:], func=AF.Exp), reads=["sa_lw" + k], writes=["sa_w" + k])
            yield
        yield
        P.op("pe", lambda e: e.matmul(o_ps[si][:], lhsT=Vh[hi][:, a, :], rhs=w[si][p][:], start=first, stop=last),
             reads=["sa_V%d" % hi, "sa_w" + k], writes=["sa_ops" + ks])
        if last:
            P.op("act", lambda e: e.copy(out=oo[si][:], in_=o_ps[si][:]), reads=["sa_ops" + ks], writes=["sa_oo" + ks])
            dma(P, "sp", T.oT[h * 128:(h + 1) * 128, qb * 512:(qb + 1) * 512], oo[si][:], ["sa_oo" + ks], ["wo_actsrc"], "sa_ost" + ks)
        yield

    STAG = 7
    order = sorted(range(NQB), key=lambda q: -q)
    assign = [[], []]
    load = [0, 0]
    for q in order:
        si = 0 if load[0] <= load[1] else 1
        assign[si].append(q)
        load[si] += q + 1
    tcount = [0, 0]
    bg = precast_gen(P, bg_jobs, "sabg_") if bg_jobs else iter(())
    nstart = 0
    for h in range(16):
        hi = h % 2
        dma(P, "sp", Kh[hi][:], T.KT[h * 128:(h + 1) * 128, :], [], ["sa_K%d" % hi], "sa_kld%d" % hi)
        dma(P, "sp", Qh[hi][:], T.QT[h * 128:(h + 1) * 128, :], [], ["sa_Q%d" % hi], "sa_qld%d" % hi)
        dma(P, "act", Vh[hi][:], T.Vs[:, h * 128:(h + 1) * 128].rearrange("(a p) d -> p a d", p=128), [], ["sa_V%d" % hi], "sa_vld%d" % hi)
        todo = []
        for si in range(NS):
            lst = []
            for q in assign[si]:
                na = (q + 1) * 4
                for ai, a in enumerate(range(na - 1, -1, -1)):
                    lst.append((q, ai, a, na))
            todo.append(lst)
        active = []
        newest = [None, None]
        while any(todo) or active:
            for si in range(NS):
                if todo[si] and (newest[si] is None or newest[si][2] >= STAG or newest[si][3]):
                    q, ai, a, na = todo[si].pop(0)
                    p = tcount[si] % 2
                    tcount[si] += 1
                    nstart += 1
                    if nstart % 8 == 0:
                        next(bg, None)
                    ent = [chain(si, p, hi, h, q, ai, a, na), si, 0, False]
                    active.append(ent)
                    newest[si] = ent
            for ent in list(active):
                try:
                    next(ent[0])
                    ent[2] += 1
                except StopIteration:
                    ent[3] = True
                    active.remove(ent)
    for _ in bg:
        pass
    P.release(m)
    m = P.mark()
    epi_o = make_store_epi_tok(P, T.mix, 0, 512, F32, "sdo_", "mix")
    linear_phase(P, C, T.oT, 2048, S, T.sb_w_out[lb], 0, 2048, "tok", epi_o, "wo_")
    P.release(m)
    ln_pass(P, C, T.mix, xres_in_d, xres_out_d, T.xT, T.ln_g[l][0:1, :], T.ln_b[l][0:1, :], NT, "sln_")


from concourse.bass_utils import run_bass_kernel_spmd

N_CORES = 4
SEQ = 4096
RET_HEADS = 8


def _consts(S):
    pos = np.arange(S, dtype=np.float32)
    inv_freq = (10000.0 ** (-np.arange(0, 256, 2, dtype=np.float32) / 256)).astype(np.float32)
    ang = (pos[None, :] * inv_freq[:, None]).astype(np.float32)
    cos = np.cos(ang.astype(np.float64)).astype(np.float32)
    sin = np.sin(ang.astype(np.float64)).astype(np.float32)
    rope = np.stack([cos, sin, cos / 16.0, sin / 16.0]).astype(np.float32)
    lg = np.log(1.0 - np.exp2(-5.0 - np.arange(8, dtype=np.float64)))
    n = np.arange(128)
    ch = n // 64
    MT = np.zeros((128, 8, 128), np.float32)
    for h in range(8):
        dist = np.abs(n[:, None] - n[None, :]).astype(np.float64)
        Mh = np.where(ch[:, None] == ch[None, :], np.exp(lg[h] * dist),
                      np.where(ch[None, :] < ch[:, None], np.exp(lg[h] * (n[:, None] - n[None, :])), 0.0))
        MT[:, h, :] = Mh.T
    qdec = np.zeros((128, 8, 128), np.float32)
    kdec = np.zeros((128, 8), np.float32)
    for h in range(8):
        qdec[:, h, :] = np.exp(lg[h] * (n + 1.0))[None, :]
        kdec[:, h] = np.exp(lg[h] * (127.0 - n))
    gamma128 = [float(np.exp(lg[h] * 128.0)) for h in range(8)]
    s = np.arange(128)[:, None]
    t = np.arange(512)[None, :]
    cmask = np.stack([((k * 128 + s) < t).astype(np.float32) for k in range(4)], axis=1)
    return dict(rope=rope, retMT=MT, qdec=qdec, kdec=kdec, cmask=np.ascontiguousarray(cmask)), gamma128


def build_nc(S=SEQ, depth=4, n_a=2):
    NT = S // 128
    nc = bass.Bass("TRN2", target_bir_lowering=False)
    T = Ctx()

    def din(name, shape):
        return nc.dram_tensor(name, list(shape), F32, kind="ExternalInput").ap()

    def dsc(name, shape, dt):
        return nc.dram_tensor(name, list(shape), dt, kind="Internal").ap()

    x = din("x", [S, 2048])
    T.ret_w_in = din("ret_w_in", [2, 2048, 12288])
    T.ret_gn_g = din("ret_gn_g", [2, 4096])
    T.ret_gn_g = [T.ret_gn_g[i:i + 1, :] for i in range(2)]
    T.ret_w_out = din("ret_w_out", [2, 4096, 2048])
    T.kv_w = din("kv_w", [2048, 4096])
    T.sb_wq = din("sb_wq", [2, 2048, 2048])
    T.sb_w_out = din("sb_w_out", [2, 2048, 2048])
    T.peer_wq = din("peer_wq", [4, 2048, 2048])
    T.keysT = din("keysT", [4, 128, 16, 128])
    T.UTp = din("UTp", [4, 2048, 128, 128])
    T.Vp = din("Vp", [4, 128, 128, 2048])
    T.ln_g = din("ln_g", [4, 2, 2048])
    T.ln_b = din("ln_b", [4, 2, 2048])
    T.rope = din("rope", [4, 128, S])
    T.retMT = din("retMT", [128, 8, 128])
    T.qdec = din("qdec", [128, 8, 128])
    T.kdec = din("kdec", [128, 8])
    T.cmask = din("cmask", [128, 4, 512])
    out = nc.dram_tensor("out", [S, 2048], F32, kind="ExternalOutput").ap()
    _, T.gamma128 = _consts(128)
    xres = dsc("xres", [S, 2048], F32)
    T.xT = dsc("xT", [2048, S], BF16)
    T.qT = dsc("qT", [2048, S], BF16)
    T.kT = dsc("kT", [2048, S], BF16)
    T.kd = dsc("kd", [S, 2048], BF16)
    T.v = dsc("v", [S, 4096], BF16)
    T.sg = dsc("sg", [S, 4096], F32)
    T.gT = dsc("gT", [4096, S], BF16)
    T.mix = dsc("mix", [S, 2048], F32)
    T.KT = dsc("KT", [2048, S], BF16)
    T.Vs = dsc("Vs", [S, 2048], BF16)
    T.QT = dsc("QT", [2048, S], BF16)
    T.oT = dsc("oT", [2048, S], BF16)
    T.pq = dsc("pq", [2048, S], BF16)
    T.UTb = dsc("UTb", [2048, 128, 128], BF16)
    T.Vb = dsc("Vb", [128, 128, 2048], BF16)
    T.GTd = dsc("GTd", [NT, 128, 128, 128], BF16)
    P = Prog(nc)
    C = Ctx()
    make_ident(P, C)
    phase_x_to_xT(P, C, x, T.xT, NT)
    cur = x
    for l in range(depth):
        bg_jobs = [(T.UTp[l].rearrange("(p a) j i -> p (a j i)", p=128), T.UTb.rearrange("(p a) j i -> p (a j i)", p=128), 16 * 128 * 128),
                   (T.Vp[l].rearrange("j i d -> j (i d)"), T.Vb.rearrange("j i d -> j (i d)"), 128 * 2048)]
        if l < n_a:
            retention_layer(P, C, T, l, cur, xres, NT, bg_jobs)
        else:
            sb_layer(P, C, T, l, cur, xres, NT, bg_jobs)
        cur = xres
        m = P.mark()
        epi_pq = make_store_epi_feat(P, T.pq, 0, "pq_", "pq")
        linear_phase(P, C, T.xT, 2048, S, T.peer_wq[l], 0, 2048, "feat", epi_pq, "lpq_")
        P.release(m)
        peer_gbuild(P, C, T.pq, T.keysT[l], T.GTd, NT)
        m = P.mark()
        C.ln = ln_alloc(P, C, "pln_")
        ln_load_params(P, C.ln, T.ln_g[l][1:2, :], T.ln_b[l][1:2, :])
        last = (l == depth - 1)
        dst = out if last else xres

        def out_cb(t, yap, ykey, dst=dst, last=last):
            ln_tile(P, C, C.ln, yap, [ykey], xres, dst, None if last else T.xT, t)
        peer_main(P, C, T.xT, T.UTb, T.Vb, T.GTd, NT, out_cb)
        P.release(m)
        if l == n_a - 1 and depth > n_a:
            shared_kv_phase(P, C, T, NT)
    P.flush()
    P.close()
    return nc


_NC_CACHE = {}


def kernel(x, ret_w_in, ret_gn_g, ret_w_out, kv_w, sb_wq, sb_w_out, peer_wq, peer_sub_keys, peer_u, peer_v, ln_g, ln_b):
    f = lambda a: np.ascontiguousarray(np.asarray(a, dtype=np.float32))
    x = f(x)
    B, S, _ = x.shape
    consts, _ = _consts(S)
    sk = f(peer_sub_keys)
    keysT = np.ascontiguousarray(sk.reshape(4, 16, 128, 128).transpose(0, 3, 1, 2))
    pu = f(peer_u).reshape(4, 128, 128, 2048)
    UTp = np.ascontiguousarray(pu.transpose(0, 3, 2, 1))
    del pu
    pv = f(peer_v).reshape(4, 128, 128, 2048)
    Vp = np.ascontiguousarray(pv.transpose(0, 2, 1, 3))
    del pv
    shared = dict(ret_w_in=f(ret_w_in), ret_gn_g=f(ret_gn_g), ret_w_out=f(ret_w_out), kv_w=f(kv_w), sb_wq=f(sb_wq),
                  sb_w_out=f(sb_w_out), peer_wq=f(peer_wq), keysT=keysT, UTp=UTp, Vp=Vp, ln_g=f(ln_g), ln_b=f(ln_b), **consts)
    if S not in _NC_CACHE:
        _NC_CACHE[S] = build_nc(S)
    nc = _NC_CACHE[S]
    in_maps = [dict(shared, x=np.ascontiguousarray(x[b])) for b in range(B)]
    res = run_bass_kernel_spmd(nc, in_maps, core_ids=list(range(B)))
    return np.stack([np.asarray(r["out"], dtype=np.float32) for r in res.results], axis=0)
```
